# Optimizing a Trainium2 kernel written in Bass

```python
import math
import jax, jax.numpy as jnp
from jax import lax
import numpy as np

D_MODEL = 4096
BATCH = 1
SEQ = 16384
DEPTH = 1

MIX_WIDTH = D_MODEL
NSA_WIDTH = D_MODEL // 2
GM_WIDTH = D_MODEL // 4
MEM_WIDTH = D_MODEL // 4
DH = 128
N_HEADS_NSA = NSA_WIDTH // DH
HEADS_PER_GROUP = 4
N_KV = N_HEADS_NSA // HEADS_PER_GROUP
KV_WIDTH = N_KV * DH
N_BRANCH = 3
CMP_LEN = 32
CMP_STRIDE = 16
SLC_BLOCK = 64
SLC_RATIO = SLC_BLOCK // CMP_STRIDE
TOP_N = 16
WINDOW = 512
Q_BLOCK = 128
CHUNK = 128
GM_GROUPS = 4
GM_DG = GM_WIDTH // GM_GROUPS
MEM_LEN = 256
MEM_HEADS = 4
MEM_DH = MEM_WIDTH // MEM_HEADS
LN_EPS = 1e-5
SPLIT_SIZES = (NSA_WIDTH,
               KV_WIDTH, KV_WIDTH,
               KV_WIDTH, KV_WIDTH,
               KV_WIDTH, KV_WIDTH,
               N_HEADS_NSA * N_BRANCH,
               NSA_WIDTH,
               GM_WIDTH, GM_WIDTH, GM_WIDTH,
               MEM_WIDTH, MEM_WIDTH)
IN_WIDTH = 2 * NSA_WIDTH + 6 * KV_WIDTH + N_HEADS_NSA * N_BRANCH + 3 * GM_WIDTH + 2 * MEM_WIDTH

kernel_name = "hybrid_nsa_gmlp_memxattn_deepnorm"


def _layernorm(x, g, b):
    xf = x.astype(jnp.float32)
    mu = jnp.mean(xf, axis=-1, keepdims=True)
    var = jnp.mean(jnp.square(xf - mu), axis=-1, keepdims=True)
    return ((xf - mu) * lax.rsqrt(var + LN_EPS) * g.astype(jnp.float32) + b.astype(jnp.float32)).astype(x.dtype)


def _masked_softmax(s, mask):
    s = jnp.where(mask, s.astype(jnp.float32), -jnp.inf)
    m = jnp.max(s, axis=-1, keepdims=True)
    m = jnp.where(jnp.isfinite(m), m, 0.0)
    p = jnp.where(mask, jnp.exp(s - m), 0.0)
    return p / jnp.maximum(jnp.sum(p, axis=-1, keepdims=True), 1e-30)


def _alibi_slopes(n):
    return jnp.exp2(-8.0 * jnp.arange(1, n + 1, dtype=jnp.float32) / n)


def _compress(a, pe, w1, w2):
    B, G, T, _ = a.shape
    ch = a.reshape(B, G, T // CMP_STRIDE, CMP_STRIDE, DH)
    blk = jnp.concatenate([ch[:, :, :-1], ch[:, :, 1:]], axis=3)
    nc = blk.shape[2]
    blk = (blk + pe).reshape(B, G, nc, CMP_LEN * DH)
    return jax.nn.gelu(blk @ w1) @ w2


def setup_inputs(seed: int = 0) -> dict:
    key = jax.random.key(seed)
    ks = jax.random.split(key, 20)
    f32 = jnp.float32

    def nrm(k, shape, scale):
        return jax.random.normal(k, shape, f32) * scale

    beta = (8.0 * DEPTH) ** -0.25
    return {
        "x": nrm(ks[0], (BATCH, SEQ, D_MODEL), 1.0),
        "mem": nrm(ks[1], (BATCH, MEM_LEN, D_MODEL), 1.0),
        "w_in": nrm(ks[2], (D_MODEL, IN_WIDTH), D_MODEL ** -0.5),
        "w_cmp_k1": nrm(ks[3], (CMP_LEN * DH, DH), (CMP_LEN * DH) ** -0.5),
        "w_cmp_k2": nrm(ks[4], (DH, DH), DH ** -0.5),
        "w_cmp_v1": nrm(ks[5], (CMP_LEN * DH, DH), (CMP_LEN * DH) ** -0.5),
        "w_cmp_v2": nrm(ks[6], (DH, DH), DH ** -0.5),
        "pe_cmp_k": nrm(ks[7], (CMP_LEN, DH), 0.02),
        "pe_cmp_v": nrm(ks[8], (CMP_LEN, DH), 0.02),
        "gm_ln_g": 1.0 + nrm(ks[9], (GM_WIDTH,), 0.01),
        "gm_ln_b": nrm(ks[10], (GM_WIDTH,), 0.01),
        "w_spatial": nrm(ks[11], (GM_GROUPS, CHUNK, CHUNK), 0.1 * CHUNK ** -0.5),
        "b_spatial": 1.0 + nrm(ks[12], (GM_GROUPS, CHUNK), 0.01),
        "w_mem_kv": nrm(ks[13], (D_MODEL, 2 * MEM_WIDTH), D_MODEL ** -0.5),
        "w_out": nrm(ks[14], (MIX_WIDTH, D_MODEL), beta * MIX_WIDTH ** -0.5),
        "ln_g": 1.0 + nrm(ks[15], (D_MODEL,), 0.01),
        "ln_b": nrm(ks[16], (D_MODEL,), 0.01),
    }


def _nsa(q, k_c, v_c, k_s, v_s, k_w, v_w, gates, w_cmp_k1, w_cmp_k2, w_cmp_v1, w_cmp_v2,
         pe_cmp_k, pe_cmp_v):
    B, T, _ = q.shape
    G, HG = N_KV, HEADS_PER_GROUP
    scale = DH ** -0.5
    qh = q.reshape(B, T, G, HG, DH).transpose(0, 2, 3, 1, 4)

    def kvh(a):
        return a.reshape(B, T, G, DH).transpose(0, 2, 1, 3)

    kc = _compress(kvh(k_c), pe_cmp_k, w_cmp_k1, w_cmp_k2)
    vc = _compress(kvh(v_c), pe_cmp_v, w_cmp_v1, w_cmp_v2)
    n_cmp = kc.shape[2]
    cmp_end = jnp.arange(n_cmp) * CMP_STRIDE + (CMP_LEN - 1)

    n_slc = T // SLC_BLOCK
    n_top = min(TOP_N, n_slc)
    ks_blocks = kvh(k_s).reshape(B, G, n_slc, SLC_BLOCK, DH)
    vs_blocks = kvh(v_s).reshape(B, G, n_slc, SLC_BLOCK, DH)

    kw_pad = jnp.pad(kvh(k_w), ((0, 0), (0, 0), (WINDOW, 0), (0, 0)))
    vw_pad = jnp.pad(kvh(v_w), ((0, 0), (0, 0), (WINDOW, 0), (0, 0)))

    g_all = jax.nn.sigmoid(gates.reshape(B, T, G, HG, N_BRANCH).transpose(0, 2, 3, 1, 4))
    slope = _alibi_slopes(N_HEADS_NSA).reshape(G, HG)[None, :, :, None, None]
    bi = jnp.arange(B)[:, None, None, None]
    gi = jnp.arange(G)[None, :, None, None]
    jblk = jnp.arange(n_slc)

    def block(i):
        q0 = i * Q_BLOCK
        qb = lax.dynamic_slice_in_dim(qh, q0, Q_BLOCK, axis=3)
        t = q0 + jnp.arange(Q_BLOCK)

        dist_c = t[:, None] - cmp_end[None, :]
        s_c = jnp.einsum('bghqd,bgkd->bghqk', qb, kc) * scale - slope * dist_c
        p_c = _masked_softmax(s_c, dist_c >= 0)
        o_c = jnp.einsum('bghqk,bgkd->bghqd', p_c, vc.astype(jnp.float32))

        imp = jnp.sum(p_c, axis=2)
        imp_p = jnp.pad(imp, ((0, 0), (0, 0), (0, 0), (1, 1)))
        imp_s = (imp_p[..., :SLC_RATIO * n_slc].reshape(B, G, Q_BLOCK, n_slc, SLC_RATIO).sum(-1)
                 + imp_p[..., SLC_RATIO::SLC_RATIO])
        cur = (t // SLC_BLOCK)[:, None]
        forced = (jblk[None] == 0) | (jblk[None] == cur) | (jblk[None] == cur - 1)
        sel_score = jnp.where(jblk[None] > cur, -jnp.inf, jnp.where(forced, jnp.inf, imp_s))
        _, idx = lax.top_k(sel_score, n_top)

        ks_g = ks_blocks[bi, gi, idx].reshape(B, G, Q_BLOCK, n_top * SLC_BLOCK, DH)
        vs_g = vs_blocks[bi, gi, idx].reshape(B, G, Q_BLOCK, n_top * SLC_BLOCK, DH)
        pos = (idx[..., None] * SLC_BLOCK + jnp.arange(SLC_BLOCK)).reshape(B, G, Q_BLOCK, n_top * SLC_BLOCK)
        dist_s = (t[None, None, :, None] - pos)[:, :, None]
        s_s = jnp.einsum('bghqd,bgqkd->bghqk', qb, ks_g) * scale - slope * dist_s
        p_s = _masked_softmax(s_s, dist_s >= 0)
        o_s = jnp.einsum('bghqk,bgqkd->bghqd', p_s, vs_g.astype(jnp.float32))

        kw_b = lax.dynamic_slice_in_dim(kw_pad, q0, Q_BLOCK + WINDOW, axis=2)
        vw_b = lax.dynamic_slice_in_dim(vw_pad, q0, Q_BLOCK + WINDOW, axis=2)
        spos = q0 - WINDOW + jnp.arange(Q_BLOCK + WINDOW)
        dist_w = t[:, None] - spos[None, :]
        mask_w = (dist_w >= 0) & (dist_w < WINDOW) & (spos[None, :] >= 0)
        s_w = jnp.einsum('bghqd,bgkd->bghqk', qb, kw_b) * scale - slope * dist_w
        p_w = _masked_softmax(s_w, mask_w)
        o_w = jnp.einsum('bghqk,bgkd->bghqd', p_w, vw_b.astype(jnp.float32))

        gb = lax.dynamic_slice_in_dim(g_all, q0, Q_BLOCK, axis=3).astype(jnp.float32)
        return gb[..., 0:1] * o_c + gb[..., 1:2] * o_s + gb[..., 2:3] * o_w

    out = lax.map(block, jnp.arange(T // Q_BLOCK))
    out = out.transpose(1, 0, 4, 2, 3, 5).reshape(B, T, NSA_WIDTH)
    return out.astype(q.dtype)


def _gmlp(u, v, gm_ln_g, gm_ln_b, w_spatial, b_spatial):
    B, T, _ = u.shape
    v = _layernorm(v, gm_ln_g, gm_ln_b)
    vc = v.reshape(B, T // CHUNK, CHUNK, GM_GROUPS, GM_DG)
    causal = jnp.tril(jnp.ones((CHUNK, CHUNK), dtype=bool))
    ws = jnp.where(causal[None], w_spatial, 0.0)
    sv = jnp.einsum('gts,bcsgd->bctgd', ws, vc) + b_spatial.T[None, None, :, :, None]
    return (u.reshape(B, T // CHUNK, CHUNK, GM_GROUPS, GM_DG) * sv).reshape(B, T, GM_WIDTH).astype(u.dtype)


def _mem_xattn(q_m, mem, w_mem_kv):
    B, T, _ = q_m.shape
    kv = jnp.einsum('bmd,dn->bmn', mem, w_mem_kv)
    k_m, v_m = jnp.split(kv, 2, axis=-1)
    k_m = k_m.reshape(B, MEM_LEN, MEM_HEADS, MEM_DH)
    v_m = v_m.reshape(B, MEM_LEN, MEM_HEADS, MEM_DH)
    qh = q_m.reshape(B, T, MEM_HEADS, MEM_DH)
    s = jnp.einsum('bthd,bmhd->bhtm', qh, k_m) * (MEM_DH ** -0.5)
    p = jax.nn.softmax(s.astype(jnp.float32), axis=-1)
    o = jnp.einsum('bhtm,bmhd->bthd', p, v_m.astype(jnp.float32))
    return o.reshape(B, T, MEM_WIDTH).astype(q_m.dtype)


def reference(x, mem, w_in, w_cmp_k1, w_cmp_k2, w_cmp_v1, w_cmp_v2, pe_cmp_k, pe_cmp_v,
              gm_ln_g, gm_ln_b, w_spatial, b_spatial, w_mem_kv, w_out, ln_g, ln_b):
    alpha = (2.0 * DEPTH) ** 0.25
    offs = np.cumsum(np.array(SPLIT_SIZES))[:-1].tolist()
    for _layer in range(DEPTH):
        proj = jnp.einsum('btd,dn->btn', x, w_in)
        (q, k_c, v_c, k_s, v_s, k_w, v_w, gates, z_nsa,
         u, v, z_gm, q_m, z_m) = jnp.split(proj, offs, axis=-1)
        y_nsa = _nsa(q, k_c, v_c, k_s, v_s, k_w, v_w, gates, w_cmp_k1, w_cmp_k2,
                     w_cmp_v1, w_cmp_v2, pe_cmp_k, pe_cmp_v) * jax.nn.silu(z_nsa)
        y_gm = _gmlp(u, v, gm_ln_g, gm_ln_b, w_spatial, b_spatial) * jax.nn.silu(z_gm)
        y_mem = _mem_xattn(q_m, mem, w_mem_kv) * jax.nn.silu(z_m)
        h = jnp.einsum('btn,nd->btd', jnp.concatenate([y_nsa, y_gm, y_mem], axis=-1), w_out)
        x = _layernorm(alpha * x + h, ln_g, ln_b)
    return x
```

```python
import os
import numpy as np
import ml_dtypes
from contextlib import ExitStack
import concourse.bass as bass
import concourse.mybir as mybir
from concourse.bass_utils import run_bass_kernel_spmd

F32 = mybir.dt.float32
BF16 = mybir.dt.bfloat16
AF = mybir.ActivationFunctionType
ALU = mybir.AluOpType
AX = mybir.AxisListType

NCORES = 8
WSHARD = True
D = 4096
SEQ = 16384
TOK = SEQ // NCORES
HALO = 512
NT = TOK + HALO
INW = 12336
DH = 128
SCALE = DH ** -0.5
ISC = DH ** 0.5
BIG = 30000.0
LN_EPS = 1e-5
ALPHA = 2.0 ** 0.25
COLS = dict(q=(0, 2048), kc=(2048, 512), vc=(2560, 512), ks=(3072, 512), vs=(3584, 512),
            kw=(4096, 512), vw=(4608, 512), gate=(5120, 48), zn=(5168, 2048), u=(7216, 1024),
            vg=(8240, 1024), zg=(9264, 1024), qm=(10288, 1024), zm=(11312, 1024))
DMAX = [3, 10, 40, 200]
SLOPES = np.exp2(-8.0 * np.arange(1, 17, dtype=np.float64) / 16).astype(np.float32)


class Buf:
    __slots__ = ("name", "w", "rs")

    def __init__(self, name=""):
        self.name = name
        self.w = None
        self.rs = []


class Tok:
    __slots__ = ("sem", "val", "eng", "dma")

    def __init__(self, sem, val, eng, dma):
        self.sem, self.val, self.eng, self.dma = sem, val, eng, dma


class FW:
    def __init__(self, nc, n_dma_sems=64):
        self.nc = nc
        self.engs = {"pe": nc.tensor, "act": nc.scalar, "dve": nc.vector,
                     "pool": nc.gpsimd, "sp": nc.sync}
        self.esem = {e: nc.alloc_semaphore(name="s_" + e) for e in self.engs}
        self.ecnt = {e: 0 for e in self.engs}
        self.dsems = [nc.alloc_semaphore(name="d%d" % i) for i in range(n_dma_sems)]
        self.dcnt = [0] * n_dma_sems
        self.dlast = [None] * n_dma_sems
        self.dnext = 0
        self.known = {e: {} for e in self.engs}
        self.nwaits = 0
        self.nins = 0

    def _wait(self, e, tok):
        if tok is None:
            return
        k = self.known[e]
        sid = id(tok.sem)
        if k.get(sid, 0) >= tok.val:
            return
        self.engs[e].wait_ge(tok.sem, tok.val)
        self.nwaits += 1
        k[sid] = tok.val

    def _deps(self, e, reads, writes, is_dma):
        for b in reads:
            t = b.w
            if t is not None:
                self._wait(e, t)
        for b in writes:
            t = b.w
            if t is not None and (t.dma or is_dma or t.eng != e):
                self._wait(e, t)
            for r in b.rs:
                if r.dma or is_dma or r.eng != e:
                    self._wait(e, r)

    def _commit(self, tok, reads, writes):
        for b in reads:
            b.rs.append(tok)
        for b in writes:
            b.w = tok
            b.rs = []

    def op(self, e, fn, reads=(), writes=()):
        self._deps(e, reads, writes, False)
        ins = fn()
        self.nins += 1
        self.ecnt[e] += 1
        ins.then_inc(self.esem[e], 1)
        tok = Tok(self.esem[e], self.ecnt[e], e, False)
        self._commit(tok, reads, writes)
        return tok

    def ops(self, e, fns, reads=(), writes=()):
        self._deps(e, reads, writes, False)
        ins = None
        for fn in fns:
            ins = fn()
            self.nins += 1
        self.ecnt[e] += 1
        ins.then_inc(self.esem[e], 1)
        tok = Tok(self.esem[e], self.ecnt[e], e, False)
        self._commit(tok, reads, writes)
        return tok

    def dma(self, e, out, in_, reads=(), writes=(), **kw):
        i = self.dnext
        self.dnext = (self.dnext + 1) % len(self.dsems)
        self._wait(e, self.dlast[i])
        self._deps(e, reads, writes, True)
        ins = self.engs[e].dma_start(out=out, in_=in_, **kw)
        self.nins += 1
        self.dcnt[i] += 16
        ins.then_inc(self.dsems[i], 16)
        tok = Tok(self.dsems[i], self.dcnt[i], e, True)
        self.dlast[i] = tok
        self._commit(tok, reads, writes)
        return tok

    def allgather(self, in_ap, out_ap, reads=(), writes=()):
        e = "pool"
        i = self.dnext
        self.dnext = (self.dnext + 1) % len(self.dsems)
        self._wait(e, self.dlast[i])
        self._deps(e, reads, writes, True)
        ins = self.nc.gpsimd.collective_compute(
            "AllGather", ALU.bypass, replica_groups=[list(range(NCORES))],
            ins=[in_ap], outs=[out_ap])
        self.dcnt[i] += 1
        ins.then_inc(self.dsems[i], 1)
        tok = Tok(self.dsems[i], self.dcnt[i], e, True)
        self.dlast[i] = tok
        self._commit(tok, reads, writes)
        return tok

    def _all_toks(self):
        toks = []
        for e in self.engs:
            if self.ecnt[e] > 0:
                toks.append(Tok(self.esem[e], self.ecnt[e], e, False))
        for t in self.dlast:
            if t is not None:
                toks.append(t)
        return toks

    def barrier(self):
        toks = self._all_toks()
        for e in self.engs:
            for t in toks:
                if t.eng == e and not t.dma:
                    continue
                self._wait(e, t)

    def finish(self, e="sp"):
        for t in self._all_toks():
            self._wait(e, t)


def _bf(a):
    return np.asarray(a, np.float32).astype(ml_dtypes.bfloat16)


def _split3(a):
    a = np.asarray(a, np.float64)
    h = _bf(a).astype(np.float64)
    m = _bf(a - h).astype(np.float64)
    l = _bf(a - h - m)
    return _bf(h), _bf(m), l


def _shared_consts():
    c = {}
    c["ident"] = np.eye(128, dtype=np.float32)
    lt = np.zeros((71, 128, 128), np.float32)
    k = np.arange(128)
    for ch in range(128):
        jq = 2 * (ch % 32) + k // 64
        lt[jq, ch, k] = 1.0
        lt[64, ch, :] = 128.0 * ch
        lt[65, ch, :] = 128.0 * ch
        lt[66, ch, :] = k
        lt[67, ch, :] = k
        lt[68:71, ch, :] = 1.0
    c["ltab"] = _bf(lt.reshape(71, 128 * 128))
    dq = np.arange(128)[None, None, :]
    rk = np.arange(128)[:, None, None]
    bs = np.zeros((4, 128, 4, 128), np.float32)
    bw = np.zeros((4, 5, 128, 4, 128), np.float32)
    for g in range(4):
        sl = SLOPES[4 * g:4 * g + 4].astype(np.float64)[None, :, None]
        dist = (dq - rk) + np.zeros((128, 4, 128))
        b = np.where(dist >= 0, -sl * dist, -BIG) * ISC
        bs[g] = b
        for m in range(5):
            dist = 128.0 * m + (dq - rk) + np.zeros((128, 4, 128))
            b = np.where((dist >= 0) & (dist < 512), -sl * dist, -BIG) * ISC
            bw[g, m] = b
    c["bs"] = _bf(bs.reshape(4, 128, 512))
    c["bw"] = _bf(bw.reshape(4, 5, 128, 512))
    bc = np.zeros((16, 4, 128, 4, 128), np.float32)
    mp = np.arange(128)[:, None, None]
    for i in range(16):
        for g in range(4):
            sl = SLOPES[4 * g:4 * g + 4].astype(np.float64)[None, :, None]
            dist = 128.0 * i + dq - 16.0 * mp - 15.0 + np.zeros((128, 4, 128))
            bc[i, g] = np.where(dist >= 0, -sl * dist, -BIG) * ISC
    c["bc"] = _bf(bc.reshape(16, 4, 128, 512))
    mc = np.zeros((8, 128, 256), np.float32)
    for cc in range(8):
        for r in range(128):
            idx = 128 * cc + r
            for j in (idx // 4, idx // 4 - 1):
                if 0 <= j < 256 and 4 * j <= idx <= 4 * j + 4:
                    mc[cc, r, j] = 1.0
    c["mc"] = _bf(mc)
    tri = (np.arange(128)[:, None] <= np.arange(128)[None, :]).astype(np.float32)
    c["tri"] = tri
    return c


def _core_consts(r, shared_mc):
    c = {}
    t = 2048 * r + np.arange(2048)
    ra = np.zeros((16, 4, 7, 4, 4, 128), np.float32)
    rc = np.zeros((16, 4, 8, 4, 128), np.float32)
    rab = np.zeros((16, 4, 7, 4, 4, 128), ml_dtypes.bfloat16)
    rcb = np.zeros((16, 4, 8, 4, 128), ml_dtypes.bfloat16)
    for g in range(4):
        sl = SLOPES[4 * g:4 * g + 4].astype(np.float64) * ISC
        shi = _bf(sl).astype(np.float64)
        slo = _bf(sl - shi)
        for i in range(16):
            tq = t[128 * i:128 * (i + 1)].astype(np.float64)
            st = -(sl[:, None] * tq[None, :])
            a, b, cpart = _split3(st)
            stc = -(sl[:, None] * (tq[None, :] - 15.0))
            a2, b2, c2 = _split3(stc)
            for cp in range(4):
                rab[i, g, 0, cp] = _bf(shi)[:, None]
                rab[i, g, 1, cp] = slo[:, None]
                rab[i, g, 2, cp] = _bf(shi)[:, None]
                rab[i, g, 3, cp] = slo[:, None]
                rab[i, g, 4, cp] = a
                rab[i, g, 5, cp] = b
                rab[i, g, 6, cp] = cpart
            rcb[i, g, 0] = _bf(shi)[:, None]
            rcb[i, g, 1] = slo[:, None]
            rcb[i, g, 2] = _bf(shi)[:, None]
            rcb[i, g, 3] = slo[:, None]
            rcb[i, g, 4] = a2
            rcb[i, g, 5] = b2
            rcb[i, g, 6] = c2
            rcb[i, g, 7] = _bf(np.full((4, 128), -BIG * ISC))
    c["raug"] = rab.reshape(16, 4, 7, 4 * 512)
    c["rcmp"] = rcb.reshape(16, 4, 8, 512)
    lc = np.zeros((8, 8, 128), np.float32)
    for cc in range(8):
        lc[0, cc] = 2048.0 * cc
        lc[1, cc] = 2048.0 * cc
        lc[2, cc] = 16.0 * np.arange(128)
        lc[3, cc] = 16.0 * np.arange(128)
        lc[4:7, cc] = 1.0
        lc[7, cc] = 0.0 if cc < r else 1.0
    lc[7, 0, 0] = 1.0
    c["lcmp"] = _bf(lc.reshape(8, 8 * 128))
    ml = np.zeros((128, 256), np.float32)
    for m in range(128):
        idx = 128 * r + m
        for j in (idx // 4, idx // 4 - 1):
            if 0 <= j < 256 and 4 * j <= idx <= 4 * j + 4:
                ml[m, j] = 1.0
    c["mloc"] = _bf(ml)
    pen = np.zeros((128, 2), np.float32)
    if r == 0:
        pen[0, 0] = -BIG
        pen[:, 1] = -BIG
    c["pen"] = pen
    sb = np.zeros((2048, 256), np.float32)
    val = np.zeros((2048, 256), np.float32)
    j = np.arange(256)[None, :]
    cur = (t // 64)[:, None]
    sb[:] = 0.0
    forced = (j == 0) | (j == cur) | (j == cur - 1)
    sb = np.where(forced, 1.0e4 + j, sb)
    sb = np.where(j > cur, -1.0e9, sb).astype(np.float32)
    I = (t // 128)[:, None]
    val = (j < 2 * I).astype(np.float32)
    c["sbm"] = sb
    c["val"] = val
    return c


def build(stop_after=99, dbg=()):
    nc = bass.Bass("TRN2", target_bir_lowering=False)
    fw = FW(nc)

    def din(name, shape, dt=F32):
        return nc.dram_tensor(name, list(shape), dt, kind="ExternalInput")

    def dscr(name, shape, dt=BF16):
        if name in dbg:
            return nc.dram_tensor(name, list(shape), dt, kind="ExternalOutput")
        return nc.dram_tensor(name, list(shape), dt)

    xh = din("xh", [NT, D])
    mem = din("mem", [256, D])
    if WSHARD:
        w_in_s = din("w_in", [D // NCORES, INW]); w_mem_kv_s = din("w_mem_kv", [D // NCORES, 2048])
        w_out_s = din("w_out", [D // NCORES, D])
        w_in = nc.dram_tensor("w_in_full", [D, INW], F32)
        w_mem_kv = nc.dram_tensor("w_mem_kv_full", [D, 2048], F32)
        w_out = nc.dram_tensor("w_out_full", [D, D], F32)
        for (src, full, ncol) in [(w_in_s, w_in, INW), (w_mem_kv_s, w_mem_kv, 2048), (w_out_s, w_out, D)]:
            loc = nc.dram_tensor(full.name + "_loc", [D // NCORES, ncol], F32)
            bl = Buf()
            for r4 in range(4):
                fw.dma("sp", loc[r4 * 128:(r4 + 1) * 128, :], src[r4 * 128:(r4 + 1) * 128, :], writes=[Buf()])
            fw.barrier()
            fw.allgather(loc.ap(), full.ap())
        fw.barrier()
    else:
        w_in = din("w_in", [D, INW])
        w_mem_kv = din("w_mem_kv", [D, 2048])
        w_out = din("w_out", [D, D])
    w_k1 = din("w_cmp_k1", [4096, 128]); w_k2 = din("w_cmp_k2", [128, 128])
    w_v1 = din("w_cmp_v1", [4096, 128]); w_v2 = din("w_cmp_v2", [128, 128])
    pe_k = din("pe_cmp_k", [32, 128]); pe_v = din("pe_cmp_v", [32, 128])
    gm_g = din("gm_ln_g", [1, 1024]); gm_b = din("gm_ln_b", [1, 1024])
    w_sp = din("w_spatial", [4, 128, 128]); b_sp = din("b_spatial", [4, 128])
    ln_g = din("ln_g", [1, D]); ln_b = din("ln_b", [1, D])
    c_ident = din("ident", [128, 128])
    c_ltab = din("ltab", [71, 128 * 128], BF16)
    c_bs = din("bs", [4, 128, 512], BF16)
    c_bw = din("bw", [4, 5, 128, 512], BF16)
    c_bc = din("bc", [16, 4, 128, 512], BF16)
    c_mc = din("mc", [8, 128, 256], BF16)
    c_tri = din("tri", [128, 128])
    c_raug = din("raug", [16, 4, 7, 2048], BF16)
    c_rcmp = din("rcmp", [16, 4, 8, 512], BF16)
    c_lcmp = din("lcmp", [8, 1024], BF16)
    c_mloc = din("mloc", [128, 256], BF16)
    c_pen = din("pen", [128, 2])
    c_sbm = din("sbm", [2048, 256])
    c_val = din("val", [2048, 256])
    out = nc.dram_tensor("out", [TOK, D], F32, kind="ExternalOutput")

    QT = dscr("QT", [2048, TOK]); KCT = dscr("KCT", [512, NT]); VCT = dscr("VCT", [512, NT])
    KST = dscr("KST", [512, TOK]); KWT = dscr("KWT", [512, NT]); QMT = dscr("QMT", [1024, TOK])
    VS = dscr("VS", [TOK, 512]); VW = dscr("VW", [NT, 512]); GATE = dscr("GATE", [TOK, 48])
    ZN = dscr("ZN", [TOK, 2048]); U = dscr("U", [TOK, 1024]); VG = dscr("VG", [TOK, 1024])
    ZG = dscr("ZG", [TOK, 1024]); ZM = dscr("ZM", [TOK, 1024])
    KMT = dscr("KMT", [1024, 256]); VM = dscr("VM", [256, 1024])
    KCC = dscr("KCC", [512, 128]); VCC = dscr("VCC", [128, 512])
    KST_all = dscr("KST_all", [8 * 512, TOK]); VS_all = dscr("VS_all", [SEQ, 512])
    KCC_all = dscr("KCC_all", [8 * 512, 128]); VCC_all = dscr("VCC_all", [1024, 512])
    YT = dscr("YT", [D, TOK])

    PA = nc.alloc_psum_tensor("PA", [128, 1024], F32)
    PB = nc.alloc_psum_tensor("PB", [128, 1024], F32)
    P4 = nc.alloc_psum_tensor("P4", [128, 512], F32)
    P5 = nc.alloc_psum_tensor("P5", [128, 512], F32)
    P6 = nc.alloc_psum_tensor("P6", [128, 512], F32)
    P7 = nc.alloc_psum_tensor("P7", [128, 1024], BF16)
    bPA, bPB, bP4, bP5, bP6, bP7 = (Buf(n) for n in ["PA", "PB", "P4", "P5", "P6", "P7"])
    bPA0, bPA1, bPB0, bPB1 = Buf("PA0"), Buf("PA1"), Buf("PB0"), Buf("PB1")

    ident = nc.alloc_sbuf_tensor("identf", [128, 128], F32)
    identb = nc.alloc_sbuf_tensor("identb", [128, 128], BF16)
    b_ident = Buf("ident")
    fw.dma("sp", ident[:, :], c_ident[:, :], writes=[b_ident])
    fw.op("dve", lambda: nc.vector.tensor_copy(identb[:, :], ident[:, :]), reads=[b_ident], writes=[b_ident])

    cnt = {"ev": 0}

    def evac_eng():
        cnt["ev"] += 1
        return "act" if cnt["ev"] % 2 else "dve"

    def copy_on(e, o, i):
        if e == "act":
            return lambda: nc.scalar.copy(o, i)
        if e == "dve":
            return lambda: nc.vector.tensor_copy(o, i)
        return lambda: nc.gpsimd.tensor_copy(o, i)

    def proj_phase(xsrc, wsrc, tiles, tag):
        with ExitStack() as es:
            xT = es.enter_context(nc.sbuf_tensor("xT" + tag, [128, 32, 1024], BF16))
            xs = [es.enter_context(nc.sbuf_tensor("xs%d%s" % (i, tag), [128, D], F32)) for i in range(2)]
            wb = [es.enter_context(nc.sbuf_tensor("wb%d%s" % (i, tag), [128, 32, 128], BF16)) for i in range(3)]
            ost = [es.enter_context(nc.sbuf_tensor("ost%d%s" % (i, tag), [128, 1024], BF16)) for i in range(3)]
            b_xT = Buf("xT"); b_xs = [Buf(), Buf()]; b_wb = [Buf(), Buf(), Buf()]; b_ost = [Buf(), Buf(), Buf()]
            wv = wsrc.ap().rearrange("(kc p) n -> p kc n", p=128)
            nblk = 0
            nx = 0
            for (row0, ntok, blocks) in tiles:
                nch = ntok // 128
                for ch in range(nch):
                    s = nx % 2
                    nx += 1
                    fw.dma("sp", xs[s][:, :], xsrc[row0 + ch * 128: row0 + (ch + 1) * 128, :], writes=[b_xs[s]])
                    for kg in range(8):
                        P, bP = (P4, bP4) if kg % 2 == 0 else (P5, bP5)
                        fw.ops("pe", [(lambda j=j, kg=kg, P=P, s=s: nc.tensor.transpose(
                            P[:, j * 128:(j + 1) * 128], xs[s][:, (kg * 4 + j) * 128:(kg * 4 + j + 1) * 128], ident[:, :]))
                            for j in range(4)], reads=[b_xs[s], b_ident], writes=[bP])
                        e = evac_eng()
                        fw.op(e, copy_on(e, xT[:, kg * 4:(kg + 1) * 4, ch * 128:(ch + 1) * 128],
                                         P[:, :].rearrange("p (a b) -> p a b", a=4)),
                              reads=[bP], writes=[b_xT])
                for (mode, c0, w, dst) in blocks:
                    s = nblk % 3
                    pp, bpp = (PA, bPA) if nblk % 2 == 0 else (PB, bPB)
                    nblk += 1
                    for q4 in range(4):
                        fw.dma("pool", wb[s][:, q4 * 8:(q4 + 1) * 8, 0:w], wv[:, q4 * 8:(q4 + 1) * 8, c0:c0 + w],
                               writes=[b_wb[s]])
                    fns = []
                    if mode == "FM":
                        nh = (ntok + 511) // 512
                        for kc in range(32):
                            for h in range(nh):
                                n0 = h * 512
                                n1 = min(ntok, n0 + 512)
                                fns.append(lambda kc=kc, n0=n0, n1=n1, s=s, pp=pp: nc.tensor.matmul(
                                    pp[0:w, n0:n1], wb[s][:, kc, 0:w], xT[:, kc, n0:n1], start=(kc == 0), stop=(kc == 31)))
                        fw.ops("pe", fns, reads=[b_wb[s], b_xT], writes=[bpp])
                        e = evac_eng()
                        fw.op(e, copy_on(e, ost[s][0:w, 0:ntok], pp[0:w, 0:ntok]), reads=[bpp], writes=[b_ost[s]])
                        fw.dma("sp", dst, ost[s][0:w, 0:ntok], reads=[b_ost[s]])
                    else:
                        for sb in range(nch):
                            for kc in range(32):
                                fns.append(lambda kc=kc, sb=sb, s=s, pp=pp: nc.tensor.matmul(
                                    pp[:, sb * 128: sb * 128 + w], xT[:, kc, sb * 128:(sb + 1) * 128], wb[s][:, kc, 0:w],
                                    start=(kc == 0), stop=(kc == 31)))
                        fw.ops("pe", fns, reads=[b_wb[s], b_xT], writes=[bpp])
                        e = evac_eng()
                        fw.op(e, copy_on(e, ost[s][:, 0:nch * w].rearrange("p (a b) -> p a b", a=nch),
                                         pp[:, 0:nch * 128].rearrange("p (a b) -> p a b", a=nch)[:, :, 0:w]),
                              reads=[bpp], writes=[b_ost[s]])
                        fw.dma("sp", dst, ost[s][:, 0:nch * w].rearrange("p (a b) -> p a b", a=nch), reads=[b_ost[s]])
            fw.barrier()

    FMT = dict(q=(QT, False), kc=(KCT, True), vc=(VCT, True), ks=(KST, False), kw=(KWT, True), qm=(QMT, False))
    TMT = dict(vs=(VS, False), vw=(VW, True), gate=(GATE, False), zn=(ZN, False), u=(U, False), vg=(VG, False),
               zg=(ZG, False), zm=(ZM, False))

    def in_blocks(names, row0, ntok):
        bl = []
        for nm in names:
            c0, wtot = COLS[nm]
            for b0 in range(0, wtot, 128):
                w = min(128, wtot - b0)
                if nm in FMT:
                    T, halo = FMT[nm]
                    tcol = row0 if halo else row0 - HALO
                    bl.append(("FM", c0 + b0, w, T[b0:b0 + w, tcol:tcol + ntok]))
                else:
                    T, halo = TMT[nm]
                    trow = row0 if halo else row0 - HALO
                    bl.append(("TM", c0 + b0, w,
                               T[trow:trow + ntok, b0:b0 + w].rearrange("(a p) c -> p a c", p=128)))
        return bl

    order = ["ks", "vs", "kc", "vc", "kw", "vw", "q", "gate", "zn", "u", "vg", "zg", "qm", "zm"]
    tiles = [(0, 512, in_blocks(["kc", "vc", "kw", "vw"], 0, 512)),
             (512, 1024, in_blocks(order, 512, 1024)),
             (1536, 1024, in_blocks(order, 1536, 1024))]
    proj_phase(xh, w_in, tiles, "a")
    if stop_after <= 1:
        fw.finish("sp")
        return nc

    mblocks = []
    for b0 in range(0, 1024, 128):
        mblocks.append(("FM", b0, 128, KMT[b0:b0 + 128, 0:256]))
    for b0 in range(0, 1024, 128):
        mblocks.append(("TM", 1024 + b0, 128, VM[0:256, b0:b0 + 128].rearrange("(a p) c -> p a c", p=128)))
    proj_phase(mem, w_mem_kv, [(0, 256, mblocks)], "m")
    if stop_after <= 2:
        fw.finish("sp")
        return nc

    with ExitStack() as es:
        sb = lambda n, shp, dt: es.enter_context(nc.sbuf_tensor(n, shp, dt))
        w1 = [sb("cw1%d" % i, [128, 32, 128], BF16) for i in range(2)]
        w2 = [sb("cw2%d" % i, [128, 128], BF16) for i in range(2)]
        peT = [sb("cpe%d" % i, [128, 32], BF16) for i in range(2)]
        AT = [sb("cAT%d" % i, [128, 4, 2064], BF16) for i in range(2)]
        cv = sb("ccv", [128, 2], F32)
        xhid = sb("cxh", [128, 512], F32)
        t1 = sb("ct1", [128, 512], F32)
        t2 = sb("ct2", [128, 512], F32)
        hT = sb("chT", [128, 512], BF16)
        cout = sb("cout", [128, 512], BF16)
        bw = Buf(); bA = Buf(); bcv = Buf(); bx = Buf(); bt1 = Buf(); bt2 = Buf(); bh = Buf(); bco = Buf()
        for kv, (wa, wb2, pe, SRC) in enumerate([(w_k1, w_k2, pe_k, KCT), (w_v1, w_v2, pe_v, VCT)]):
            fw.dma("pool", w1[kv][:, :, :], wa.ap().rearrange("(j p) o -> p j o", p=128), writes=[bw])
            fw.dma("pool", w2[kv][:, :], wb2[:, :], writes=[bw])
            fw.dma("pool", peT[kv][:, :], pe.ap().rearrange("j d -> d j"), writes=[bw], allow_slow_non_contiguous=True)
            fw.dma("sp", AT[kv][:, :, :], SRC.ap().rearrange("(g p) t -> p g t", p=128)[:, :, 496:2560], writes=[bA])
        for kv in range(2):
            fns = []
            for g in range(4):
                A4 = AT[kv][:, g, :].rearrange("p (m s) -> p m s", s=16)
                for j in range(32):
                    a, b = j // 16, j % 16
                    fns.append(lambda g=g, j=j, a=a, b=b, A4=A4: nc.tensor.matmul(
                        P4[:, g * 128:(g + 1) * 128], w1[kv][:, j, :], A4[:, a:a + 128, b],
                        start=(j == 0), stop=(j == 31)))
            fw.ops("pe", fns, reads=[bw, bA], writes=[bP4])
            fw.ops("pe", [(lambda j=j: nc.tensor.matmul(P6[:, 0:1], w1[kv][:, j, :], peT[kv][:, j:j + 1],
                                                        start=(j == 0), stop=(j == 31))) for j in range(32)],
                   reads=[bw], writes=[bP6])
            fw.op("dve", lambda: nc.vector.tensor_copy(cv[:, kv:kv + 1], P6[:, 0:1]), reads=[bP6], writes=[bcv])
            fw.op("act", lambda: nc.scalar.activation(xhid[:, :], P4[:, :], AF.Identity, bias=cv[:, kv:kv + 1], scale=1.0),
                  reads=[bP4, bcv], writes=[bx])
            fw.op("dve", lambda: nc.vector.tensor_tensor(t1[:, :], xhid[:, :], xhid[:, :], ALU.mult), reads=[bx], writes=[bt1])
            fw.op("dve", lambda: nc.vector.tensor_scalar(t2[:, :], t1[:, :], 0.044715, 1.0, ALU.mult, ALU.add),
                  reads=[bt1], writes=[bt2])
            fw.op("dve", lambda: nc.vector.tensor_tensor(t1[:, :], t2[:, :], xhid[:, :], ALU.mult), reads=[bt2, bx], writes=[bt1])
            fw.op("act", lambda: nc.scalar.activation(t2[:, :], t1[:, :], AF.Sigmoid, scale=1.5957691216057308),
                  reads=[bt1], writes=[bt2])
            fw.op("dve", lambda: nc.vector.tensor_tensor(hT[:, :], t2[:, :], xhid[:, :], ALU.mult), reads=[bt2, bx], writes=[bh])
            if kv == 0:
                fw.ops("pe", [lambda: nc.tensor.matmul(P5[:, :], w2[0][:, :], hT[:, :], start=True, stop=True)],
                       reads=[bw, bh], writes=[bP5])
                fw.op("dve", lambda: nc.vector.tensor_copy(cout[:, :], P5[:, :]), reads=[bP5], writes=[bco])
                fw.dma("sp", KCC.ap().rearrange("(g p) m -> p g m", p=128),
                       cout[:, :].rearrange("p (g m) -> p g m", g=4), reads=[bco])
            else:
                fw.ops("pe", [(lambda g=g: nc.tensor.matmul(P5[:, g * 128:(g + 1) * 128], hT[:, g * 128:(g + 1) * 128],
                                                            w2[1][:, :], start=True, stop=True)) for g in range(4)],
                       reads=[bw, bh], writes=[bP5])
                fw.op("dve", lambda: nc.vector.tensor_copy(cout[:, :], P5[:, :]), reads=[bP5], writes=[bco])
                fw.dma("sp", VCC[:, :], cout[:, :], reads=[bco])
        fw.barrier()
    if stop_after <= 3:
        fw.finish("sp")
        return nc

    for (a, b) in [(KCC, KCC_all), (VCC, VCC_all), (KST, KST_all), (VS, VS_all)]:
        fw.allgather(a.ap(), b.ap())
    fw.barrier()
    if stop_after <= 4:
        fw.finish("sp")
        return nc

    Sb = [(PA[:, 0:512], bPA0), (PA[:, 512:1024], bPA1), (P6[:, :], bP6)]
    Ob = [(PB[:, 0:512], bPB0), (PB[:, 512:1024], bPB1), (P4[:, :], bP4), (P5[:, :], bP5)]
    with ExitStack() as es:
        sb = lambda n, shp, dt: es.enter_context(nc.sbuf_tensor("a_" + n, shp, dt))
        KSTg = sb("KSTg", [128, 8, 2048], BF16); VSXg = sb("VSXg", [128, 128, 129], BF16)
        KSTl = sb("KSTl", [128, 2048], BF16); VSXl = sb("VSXl", [128, 16, 129], BF16)
        QTg = sb("QTg", [128, 4, 2048], BF16)
        KWTg = sb("KWTg", [128, 2560], BF16); VWXg = sb("VWXg", [128, 20, 129], BF16)
        KCCg = sb("KCCg", [128, 8, 128], BF16); KCCl = sb("KCCl", [128, 128], BF16)
        VCXg = sb("VCXg", [128, 9, 385], BF16)
        LTAB = sb("LTAB", [71, 128, 128], BF16); LCMP = sb("LCMP", [8, 8, 128], BF16)
        BSg = sb("BSg", [128, 512], BF16); BWg = sb("BWg", [128, 5, 512], BF16)
        BCt = [sb("BC%d" % i, [128, 512], BF16) for i in range(2)]
        raug = [sb("raug%d" % i, [71, 4, 512], BF16) for i in range(2)]
        rcmp = [sb("rcmp%d" % i, [8, 512], BF16) for i in range(2)]
        pen = sb("pen", [128, 2], F32)
        sbm = [sb("sbm%d" % i, [128, 256], F32) for i in range(2)]
        valt = [sb("val%d" % i, [128, 256], F32) for i in range(2)]
        gt = sb("gt", [128, 16, 48], BF16); gsig = sb("gsig", [128, 16, 48], F32)
        znt = [sb("znt%d" % i, [128, 512], BF16) for i in range(2)]
        zs = sb("zs", [128, 512], F32)
        pt = [sb("pt%d" % i, [128, 512], BF16) for i in range(4)]
        yacc = sb("yacc", [128, 512], F32); ybf = sb("ybf", [128, 512], BF16); yTs = sb("yTs", [128, 512], BF16)
        dn = sb("dn", [128, 4], F32); rdn = sb("rdn", [128, 4], F32); wts = sb("wts", [128, 4], F32)
        score = sb("score", [128, 256], F32); sc2 = sb("sc2", [128, 256], F32); sel = sb("sel", [128, 256], F32)
        m8a = sb("m8a", [128, 8], F32); m8b = sb("m8b", [128, 8], F32)
        nm = sb("nm", [128, 256], BF16)
        B = {n: Buf(n) for n in ["KSTg", "VSXg", "KSTl", "VSXl", "QTg", "KWTg", "VWXg", "KCCg", "KCCl", "VCXg", "LT",
                                 "BSg", "BWg", "pen", "gt", "gsig", "zs", "yacc", "ybf", "yTs", "dn", "rdn", "wts",
                                 "score", "sc2", "sel", "m8a", "m8b", "nm"]}
        bBC = [Buf(), Buf()]; braug = [Buf(), Buf()]; brcmp = [Buf(), Buf()]; bsbm = [Buf(), Buf()]
        bval = [Buf(), Buf()]; bzn = [Buf(), Buf()]; bpt = [Buf(), Buf(), Buf(), Buf()]

        fw.dma("sp", LTAB[:, :, :], c_ltab.ap().rearrange("r (c k) -> r c k", c=128), writes=[B["LT"]])
        fw.dma("sp", LCMP[:, :, :], c_lcmp.ap().rearrange("r (c k) -> r c k", c=8), writes=[B["LT"]])
        fw.dma("sp", pen[:, :], c_pen[:, :], writes=[B["pen"]])
        fw.op("pool", lambda: nc.gpsimd.memset(VSXg[:, :, 128:129], 1.0), writes=[B["VSXg"]])
        fw.op("pool", lambda: nc.gpsimd.memset(VSXl[:, :, 128:129], 1.0), writes=[B["VSXl"]])
        fw.op("pool", lambda: nc.gpsimd.memset(VWXg[:, :, 128:129], 1.0), writes=[B["VWXg"]])
        fw.op("pool", lambda: nc.gpsimd.memset(VCXg[:, :, 128:129], 1.0), writes=[B["VCXg"]])
        fw.dma("sp", VCXg[:, 0:8, 129:385], c_mc.ap().rearrange("c k j -> k c j"), writes=[B["VCXg"]])
        fw.dma("sp", VCXg[:, 8, 129:385], c_mloc[:, :], writes=[B["VCXg"]])
        fw.dma("sp", gt[:, :, :], GATE.ap().rearrange("(i p) c -> p i c", p=128), writes=[B["gt"]])
        fw.op("act", lambda: nc.scalar.activation(gsig[:, :, :], gt[:, :, :], AF.Sigmoid), reads=[B["gt"]], writes=[B["gsig"]])

        pcount = {"s": 0, "p": 0, "it": 0}

        def attn_stream(chunks, qap, qbufs):
            n = len(chunks)
            LOOK = 2
            pend = []
            for ci in range(n + LOOK):
                if ci < n:
                    ch = chunks[ci]
                    S, bS = Sb[pcount["s"] % 3]
                    pcount["s"] += 1
                    L, R, ab = ch["aug"]
                    fw.ops("pe", [lambda S=S, ch=ch: nc.tensor.matmul(S, ch["k"], qap, start=True, stop=False),
                                  lambda S=S, L=L, R=R: nc.tensor.matmul(S, L, R, start=False, stop=True)],
                           reads=list(ch["kb"]) + list(qbufs) + list(ab), writes=[bS])
                    pi = pcount["p"] % 4
                    pcount["p"] += 1
                    if ch["bias"] is None:
                        fw.op("act", lambda S=S, pi=pi: nc.scalar.activation(pt[pi][:, :], S, AF.Exp, scale=SCALE),
                              reads=[bS], writes=[bpt[pi]])
                    else:
                        fw.op("act", lambda S=S, pi=pi, ch=ch: nc.scalar.activation(pt[pi][:, :], S, AF.Exp,
                                                                                 bias=ch["bias"], scale=SCALE),
                              reads=[bS, B["pen"]], writes=[bpt[pi]])
                    pend.append((ci, pi, ch))
                if pend and (ci >= n or len(pend) > LOOK):
                    pci, ppi, pch = pend.pop(0)
                    nco = pch["ncol"]
                    fw.ops("pe", [(lambda h=h, ppi=ppi, pch=pch, nco=nco, pci=pci: nc.tensor.matmul(
                        Ob[h][0][:, 0:nco], pt[ppi][:, h * 128:(h + 1) * 128], pch["v"],
                        start=(pci == 0), stop=(pci == n - 1))) for h in range(4)],
                        reads=[bpt[ppi]] + list(pch["vb"]), writes=[Ob[h][1] for h in range(4)])
            assert not pend

        def branch_out(g, i, br, first):
            for h in range(4):
                fw.op("dve", lambda h=h: nc.vector.tensor_scalar_max(dn[:, h:h + 1], Ob[h][0][:, 128:129], 1e-30),
                      reads=[Ob[h][1]], writes=[B["dn"]])
            fw.op("dve", lambda: nc.vector.reciprocal(rdn[:, :], dn[:, :]), reads=[B["dn"]], writes=[B["rdn"]])
            gv = gsig[:, i, g * 12:(g + 1) * 12].rearrange("p (h b) -> p h b", b=3)[:, :, br]
            fw.op("dve", lambda: nc.vector.tensor_tensor(wts[:, :], rdn[:, :], gv, ALU.mult),
                  reads=[B["rdn"], B["gsig"]], writes=[B["wts"]])
            for h in range(4):
                if first:
                    fw.op("dve", lambda h=h: nc.vector.tensor_scalar_mul(yacc[:, h * 128:(h + 1) * 128],
                                                                        Ob[h][0][:, 0:128], wts[:, h:h + 1]),
                          reads=[Ob[h][1], B["wts"]], writes=[B["yacc"]])
                else:
                    fw.op("dve", lambda h=h: nc.vector.scalar_tensor_tensor(
                        yacc[:, h * 128:(h + 1) * 128], Ob[h][0][:, 0:128], wts[:, h:h + 1],
                        yacc[:, h * 128:(h + 1) * 128], ALU.mult, ALU.add),
                        reads=[Ob[h][1], B["wts"], B["yacc"]], writes=[B["yacc"]])

        for g in range(int(os.environ.get('K_NG', 4))):
            fw.dma("sp", KSTg[:, :, :], KST_all.ap().rearrange("(r g p) t -> p r g t", r=8, g=4, p=128)[:, :, g, :],
                   writes=[B["KSTg"]])
            vsv = VS_all.ap().rearrange("(c k) (g d) -> k c g d", k=128, g=4)
            for c8 in range(16):
                fw.dma("sp", VSXg[:, c8 * 8:(c8 + 1) * 8, 0:128], vsv[:, c8 * 8:(c8 + 1) * 8, g, :], writes=[B["VSXg"]])
            fw.dma("sp", KSTl[:, :], KST[g * 128:(g + 1) * 128, :], writes=[B["KSTl"]])
            vsl = VS.ap().rearrange("(c k) (g d) -> k c g d", k=128, g=4)
            for c8 in range(2):
                fw.dma("sp", VSXl[:, c8 * 8:(c8 + 1) * 8, 0:128], vsl[:, c8 * 8:(c8 + 1) * 8, g, :], writes=[B["VSXl"]])
            fw.dma("sp", QTg[:, :, :], QT.ap().rearrange("(h p) t -> p h t", p=128)[:, 4 * g:4 * g + 4, :], writes=[B["QTg"]])
            fw.dma("sp", KWTg[:, :], KWT[g * 128:(g + 1) * 128, :], writes=[B["KWTg"]])
            vwv = VW.ap().rearrange("(c k) (g d) -> k c g d", k=128, g=4)
            for c8 in range(0, 20, 10):
                fw.dma("sp", VWXg[:, c8:c8 + 10, 0:128], vwv[:, c8:c8 + 10, g, :], writes=[B["VWXg"]])
            fw.dma("sp", KCCg[:, :, :], KCC_all.ap().rearrange("(r g p) m -> p r g m", r=8, g=4, p=128)[:, :, g, :],
                   writes=[B["KCCg"]])
            fw.dma("sp", KCCl[:, :], KCC[g * 128:(g + 1) * 128, :], writes=[B["KCCl"]])
            fw.dma("sp", VCXg[:, 0:8, 0:128], VCC_all.ap().rearrange("(c k) (g d) -> k c g d", k=128, g=4)[:, :, g, :],
                   writes=[B["VCXg"]])
            fw.dma("sp", VCXg[:, 8, 0:128], VCC[:, g * 128:(g + 1) * 128], writes=[B["VCXg"]])
            fw.dma("sp", BSg[:, :], c_bs[g, :, :], writes=[B["BSg"]])
            fw.dma("sp", BWg[:, :, :], c_bw.ap()[g].rearrange("m k f -> k m f"), writes=[B["BWg"]])
            for i in range(int(os.environ.get('K_NI', 16))):
                it = pcount["it"]
                pcount["it"] += 1
                sl = it % 2
                fw.dma("sp", BCt[sl][:, :], c_bc[i, g, :, :], writes=[bBC[sl]])
                fw.dma("sp", rcmp[sl][:, :], c_rcmp[i, g, :, :], writes=[brcmp[sl]])
                fw.dma("sp", raug[sl][64:71, :, :], c_raug.ap()[i, g].rearrange("r (c f) -> r c f", c=4), writes=[braug[sl]])
                fw.dma("sp", sbm[sl][:, :], c_sbm[i * 128:(i + 1) * 128, :], writes=[bsbm[sl]])
                fw.dma("sp", valt[sl][:, :], c_val[i * 128:(i + 1) * 128, :], writes=[bval[sl]])
                fw.dma("sp", znt[sl][:, :], ZN[i * 128:(i + 1) * 128, g * 512:(g + 1) * 512], writes=[bzn[sl]])
                qap = QTg[:, :, i * 128:(i + 1) * 128]
                qb = [B["QTg"]]
                chunks = []
                for cc in range(8):
                    chunks.append(dict(k=KCCg[:, cc, :], kb=[B["KCCg"]],
                                       aug=(LCMP[0:8, cc, :], rcmp[sl][0:8, :], [B["LT"], brcmp[sl]]),
                                       bias=None, v=VCXg[:, cc, 0:385], vb=[B["VCXg"]], ncol=385))
                chunks.append(dict(k=KCCl[:, :], kb=[B["KCCl"]], aug=(identb[:, :], BCt[sl][:, :], [b_ident, bBC[sl]]),
                                   bias=pen[:, 0:1], v=VCXg[:, 8, 0:385], vb=[B["VCXg"]], ncol=385))
                attn_stream(chunks, qap, qb)
                branch_out(g, i, 0, True)
                fw.op("dve", lambda: nc.vector.scalar_tensor_tensor(score[:, :], Ob[0][0][:, 129:385], rdn[:, 0:1],
                                                                   sbm[sl][:, :], ALU.mult, ALU.add),
                      reads=[Ob[0][1], B["rdn"], bsbm[sl]], writes=[B["score"]])
                for h in range(1, 4):
                    fw.op("dve", lambda h=h: nc.vector.scalar_tensor_tensor(score[:, :], Ob[h][0][:, 129:385], rdn[:, h:h + 1],
                                                                           score[:, :], ALU.mult, ALU.add),
                          reads=[Ob[h][1], B["rdn"], B["score"]], writes=[B["score"]])
                fw.op("dve", lambda: nc.vector.max(m8a[:, :], score[:, :]), reads=[B["score"]], writes=[B["m8a"]])
                fw.op("dve", lambda: nc.vector.match_replace(sc2[:, :], m8a[:, :], score[:, :], -3.0e38),
                      reads=[B["score"], B["m8a"]], writes=[B["sc2"]])
                fw.op("dve", lambda: nc.vector.max(m8b[:, :], sc2[:, :]), reads=[B["sc2"]], writes=[B["m8b"]])
                fw.op("dve", lambda: nc.vector.tensor_scalar(sel[:, :], score[:, :], m8b[:, 7:8], None, ALU.is_ge),
                      reads=[B["score"], B["m8b"]], writes=[B["sel"]])
                fw.op("dve", lambda: nc.vector.tensor_tensor(sc2[:, :], sel[:, :], valt[sl][:, :], ALU.mult),
                      reads=[B["sel"], bval[sl]], writes=[B["sc2"]])
                fw.op("dve", lambda: nc.vector.tensor_scalar(nm[:, :], sc2[:, :], -1.0, BIG * ISC, ALU.add, ALU.mult),
                      reads=[B["sc2"]], writes=[B["nm"]])
                fw.ops("pe", [(lambda qq=qq: nc.tensor.transpose(P7[0:64, qq * 128:(qq + 1) * 128],
                                                                nm[:, qq * 64:(qq + 1) * 64], identb[:, :])) for qq in range(4)],
                       reads=[B["nm"], b_ident], writes=[bP7])
                for qq in range(4):
                    e = "act" if qq % 2 else "pool_skip"
                    src = P7[0:64, qq * 128:(qq + 1) * 128].unsqueeze(1).to_broadcast([64, 4, 128])
                    dst = raug[sl][0:64, qq, :].rearrange("p (h q) -> p h q", h=4)
                    if qq % 2:
                        fw.op("act", lambda src=src, dst=dst: nc.scalar.copy(dst, src), reads=[bP7], writes=[braug[sl]])
                    else:
                        fw.op("dve", lambda src=src, dst=dst: nc.vector.tensor_copy(dst, src), reads=[bP7], writes=[braug[sl]])
                chunks = []
                for c in range(128):
                    if not any(1 <= 16 * r_ + i - c <= DMAX[g] for r_ in range(NCORES)):
                        continue
                    chunks.append(dict(k=KSTg[:, c // 16, (c % 16) * 128:(c % 16 + 1) * 128], kb=[B["KSTg"]],
                                       aug=(LTAB[0:71, c, :], raug[sl][0:71, c // 32, :], [B["LT"], braug[sl]]),
                                       bias=None, v=VSXg[:, c, 0:129], vb=[B["VSXg"]], ncol=129))
                chunks.append(dict(k=KSTl[:, i * 128:(i + 1) * 128], kb=[B["KSTl"]],
                                   aug=(identb[:, :], BSg[:, :], [b_ident, B["BSg"]]),
                                   bias=None, v=VSXl[:, i, 0:129], vb=[B["VSXl"]], ncol=129))
                attn_stream(chunks, qap, qb)
                branch_out(g, i, 1, False)
                chunks = []
                for m in range(4, -1, -1):
                    ci = 4 + i - m
                    chunks.append(dict(k=KWTg[:, ci * 128:(ci + 1) * 128], kb=[B["KWTg"]],
                                       aug=(identb[:, :], BWg[:, m, :], [b_ident, B["BWg"]]),
                                       bias=(pen[:, 1:2] if ci < 4 else None), v=VWXg[:, ci, 0:129], vb=[B["VWXg"]], ncol=129))
                attn_stream(chunks, qap, qb)
                branch_out(g, i, 2, False)
                fw.op("act", lambda: nc.scalar.activation(zs[:, :], znt[sl][:, :], AF.Silu), reads=[bzn[sl]], writes=[B["zs"]])
                fw.op("dve", lambda: nc.vector.tensor_tensor(ybf[:, :], yacc[:, :], zs[:, :], ALU.mult),
                      reads=[B["yacc"], B["zs"]], writes=[B["ybf"]])
                fw.ops("pe", [(lambda h=h: nc.tensor.transpose(P7[:, h * 128:(h + 1) * 128], ybf[:, h * 128:(h + 1) * 128],
                                                              identb[:, :])) for h in range(4)],
                       reads=[B["ybf"], b_ident], writes=[bP7])
                fw.op("dve", lambda: nc.vector.tensor_copy(yTs[:, :], P7[:, 0:512]), reads=[bP7], writes=[B["yTs"]])
                fw.dma("sp", YT[g * 512:(g + 1) * 512, i * 128:(i + 1) * 128].rearrange("(h p) t -> p h t", p=128),
                       yTs[:, :].rearrange("p (h t) -> p h t", h=4), reads=[B["yTs"]])
        fw.barrier()
    if stop_after <= 5:
        fw.finish("sp")
        return nc

    with ExitStack() as es:
        sb = lambda n, shp, dt: es.enter_context(nc.sbuf_tensor("g_" + n, shp, dt))
        gam = sb("gam", [128, 1024], F32); bet = sb("bet", [128, 1024], F32)
        wsn = sb("wsn", [128, 4, 128], F32); trit = sb("tri", [128, 128], F32)
        wsT = sb("wsT", [128, 512], F32); wsTb = sb("wsTb", [128, 512], BF16)
        bsp = sb("bsp", [128, 4], F32)
        vt = [sb("vt%d" % i, [128, 1024], BF16) for i in range(2)]
        ut = [sb("ut%d" % i, [128, 1024], BF16) for i in range(2)]
        zt = [sb("zt%d" % i, [128, 1024], BF16) for i in range(2)]
        cen = sb("cen", [128, 1024], F32); sq = sb("sq", [128, 1024], F32)
        st4 = sb("st4", [128, 4], F32)
        vnb = sb("vnb", [128, 1024], BF16); tm = sb("tm", [128, 1024], F32); zsg = sb("zsg", [128, 1024], F32)
        yg = sb("yg", [128, 1024], BF16); ygT = sb("ygT", [128, 1024], BF16)
        bc0 = Buf(); bws = Buf(); bvt = [Buf(), Buf()]; but = [Buf(), Buf()]; bzt = [Buf(), Buf()]
        bcen = Buf(); bsq = Buf(); bst = Buf(); bvn = Buf(); btm = Buf(); bzs = Buf(); byg = Buf(); bygT = Buf()
        fw.dma("sp", gam[:, :], gm_g.ap().partition_broadcast(128), writes=[bc0])
        fw.dma("sp", bet[:, :], gm_b.ap().partition_broadcast(128), writes=[bc0])
        fw.dma("sp", wsn[:, :, :], w_sp.ap().rearrange("g t s -> t g s"), writes=[bws])
        fw.dma("sp", trit[:, :], c_tri[:, :], writes=[bws])
        fw.dma("sp", bsp[:, :], b_sp.ap().rearrange("g t -> t g"), writes=[bc0], allow_slow_non_contiguous=True)
        fw.ops("pe", [(lambda g=g: nc.tensor.transpose(P6[:, g * 128:(g + 1) * 128], wsn[:, g, :], ident[:, :])) for g in range(4)],
               reads=[bws, b_ident], writes=[bP6])
        fw.op("dve", lambda: nc.vector.tensor_tensor(wsT[:, :].rearrange("p (g t) -> p g t", g=4),
                                                    P6[:, :].rearrange("p (g t) -> p g t", g=4),
                                                    trit[:, :].unsqueeze(1).to_broadcast([128, 4, 128]), ALU.mult),
              reads=[bP6, bws], writes=[bws])
        fw.op("dve", lambda: nc.vector.tensor_copy(wsTb[:, :], wsT[:, :]), reads=[bws], writes=[bws])
        for ch in range(16):
            s = ch % 2
            rows = slice(ch * 128, (ch + 1) * 128)
            fw.dma("sp", vt[s][:, :], VG[rows, :], writes=[bvt[s]])
            fw.dma("sp", ut[s][:, :], U[rows, :], writes=[but[s]])
            fw.dma("sp", zt[s][:, :], ZG[rows, :], writes=[bzt[s]])
            fw.op("dve", lambda s=s: nc.vector.reduce_sum(st4[:, 0:1], vt[s][:, :], AX.X), reads=[bvt[s]], writes=[bst])
            fw.op("dve", lambda: nc.vector.tensor_scalar_mul(st4[:, 1:2], st4[:, 0:1], -1.0 / 1024), reads=[bst], writes=[bst])
            fw.op("act", lambda s=s: nc.scalar.activation(cen[:, :], vt[s][:, :], AF.Identity, bias=st4[:, 1:2], scale=1.0),
                  reads=[bvt[s], bst], writes=[bcen])
            fw.op("dve", lambda: nc.vector.tensor_tensor(sq[:, :], cen[:, :], cen[:, :], ALU.mult), reads=[bcen], writes=[bsq])
            fw.op("dve", lambda: nc.vector.reduce_sum(st4[:, 2:3], sq[:, :], AX.X), reads=[bsq], writes=[bst])
            fw.op("dve", lambda: nc.vector.tensor_scalar(st4[:, 2:3], st4[:, 2:3], 1.0 / 1024, LN_EPS, ALU.mult, ALU.add),
                  reads=[bst], writes=[bst])
            fw.op("act", lambda: nc.scalar.sqrt(st4[:, 3:4], st4[:, 2:3]), reads=[bst], writes=[bst])
            fw.op("dve", lambda: nc.vector.reciprocal(st4[:, 3:4], st4[:, 3:4]), reads=[bst], writes=[bst])
            fw.op("dve", lambda: nc.vector.scalar_tensor_tensor(sq[:, :], cen[:, :], st4[:, 3:4], gam[:, :], ALU.mult, ALU.mult),
                  reads=[bcen, bst, bc0], writes=[bsq])
            fw.op("dve", lambda: nc.vector.tensor_tensor(vnb[:, :], sq[:, :], bet[:, :], ALU.add), reads=[bsq, bc0], writes=[bvn])
            fw.ops("pe", [(lambda g=g: nc.tensor.matmul(PA[:, g * 256:(g + 1) * 256], wsTb[:, g * 128:(g + 1) * 128],
                                                        vnb[:, g * 256:(g + 1) * 256], start=True, stop=True)) for g in range(4)],
                   reads=[bws, bvn], writes=[bPA])
            for g in range(4):
                fw.op("dve", lambda g=g, s=s: nc.vector.scalar_tensor_tensor(
                    tm[:, g * 256:(g + 1) * 256], PA[:, g * 256:(g + 1) * 256], bsp[:, g:g + 1],
                    ut[s][:, g * 256:(g + 1) * 256], ALU.add, ALU.mult), reads=[bPA, bc0, but[s]], writes=[btm])
            fw.op("act", lambda s=s: nc.scalar.activation(zsg[:, :], zt[s][:, :], AF.Silu), reads=[bzt[s]], writes=[bzs])
            fw.op("dve", lambda: nc.vector.tensor_tensor(yg[:, :], tm[:, :], zsg[:, :], ALU.mult), reads=[btm, bzs], writes=[byg])
            fw.ops("pe", [(lambda b=b: nc.tensor.transpose(P7[:, b * 128:(b + 1) * 128], yg[:, b * 128:(b + 1) * 128],
                                                          identb[:, :])) for b in range(8)],
                   reads=[byg, b_ident], writes=[bP7])
            fw.op("act", lambda: nc.scalar.copy(ygT[:, :], P7[:, :]), reads=[bP7], writes=[bygT])
            fw.dma("sp", YT[2048:3072, ch * 128:(ch + 1) * 128].rearrange("(b p) t -> p b t", p=128),
                   ygT[:, :].rearrange("p (b t) -> p b t", b=8), reads=[bygT])
        fw.barrier()
    if stop_after <= 6:
        fw.finish("sp")
        return nc

    with ExitStack() as es:
        sb = lambda n, shp, dt: es.enter_context(nc.sbuf_tensor("m_" + n, shp, dt))
        KM = sb("KM", [128, 8, 256], BF16); VMX = sb("VMX", [128, 2, 4, 257], BF16)
        QM = [sb("QM%d" % i, [128, 8, 512], BF16) for i in range(2)]
        zmt = [sb("zmt%d" % i, [128, 4, 1024], BF16) for i in range(2)]
        zsm = sb("zsm", [128, 4, 1024], F32)
        ym = sb("ym", [128, 4, 1024], BF16); ymT = sb("ymT", [128, 1024], BF16)
        ptm = [sb("ptm%d" % i, [128, 512], BF16) for i in range(4)]
        rd = sb("rd", [128, 4], F32)
        bkm = Buf(); bvm = Buf(); bqm = [Buf(), Buf()]; bzm = [Buf(), Buf()]; bzsm = Buf(); bym = Buf(); bymT = Buf()
        bptm = [Buf() for _ in range(4)]; brd = Buf()
        fw.dma("sp", KM[:, :, :], KMT.ap().rearrange("(c p) m -> p c m", p=128), writes=[bkm])
        fw.op("pool", lambda: nc.gpsimd.memset(VMX[:, :, :, 256:257], 1.0), writes=[bvm])
        for mc in range(2):
            fw.dma("sp", VMX[:, mc, :, 0:256], VM[mc * 128:(mc + 1) * 128, :].rearrange("k (h d) -> k h d", h=4), writes=[bvm])
        np_ = 0
        for tt in range(4):
            s = tt % 2
            fw.dma("sp", QM[s][:, :, :], QMT.ap().rearrange("(c p) t -> p c t", p=128)[:, :, tt * 512:(tt + 1) * 512], writes=[bqm[s]])
            fw.dma("sp", zmt[s][:, :, :], ZM[tt * 512:(tt + 1) * 512, :].rearrange("(s p) c -> p s c", p=128), writes=[bzm[s]])
            fw.op("act", lambda s=s: nc.scalar.activation(zsm[:, :, :], zmt[s][:, :, :], AF.Silu), reads=[bzm[s]], writes=[bzsm])
            for h in range(4):
                pis = []
                for mc in range(2):
                    S, bS = Sb[mc]
                    fw.ops("pe", [lambda S=S, mc=mc, h=h, s=s: nc.tensor.matmul(S, KM[:, h * 2, mc * 128:(mc + 1) * 128],
                                                                                QM[s][:, h * 2, :], start=True, stop=False),
                                  lambda S=S, mc=mc, h=h, s=s: nc.tensor.matmul(S, KM[:, h * 2 + 1, mc * 128:(mc + 1) * 128],
                                                                                QM[s][:, h * 2 + 1, :], start=False, stop=True)],
                           reads=[bkm, bqm[s]], writes=[bS])
                    pi = np_ % 4
                    np_ += 1
                    pis.append(pi)
                    fw.op("act", lambda S=S, pi=pi: nc.scalar.activation(ptm[pi][:, :], S, AF.Exp, scale=1.0 / 16.0),
                          reads=[bS], writes=[bptm[pi]])
                for sc in range(4):
                    O, bO = Ob[sc]
                    fw.ops("pe", [lambda O=O, sc=sc, h=h: nc.tensor.matmul(O[:, 0:257], ptm[pis[0]][:, sc * 128:(sc + 1) * 128],
                                                                          VMX[:, 0, h, :], start=True, stop=False),
                                  lambda O=O, sc=sc, h=h: nc.tensor.matmul(O[:, 0:257], ptm[pis[1]][:, sc * 128:(sc + 1) * 128],
                                                                          VMX[:, 1, h, :], start=False, stop=True)],
                           reads=[bptm[pis[0]], bptm[pis[1]], bvm], writes=[bO])
                for sc in range(4):
                    O, bO = Ob[sc]
                    fw.op("dve", lambda O=O, sc=sc: nc.vector.reciprocal(rd[:, sc:sc + 1], O[:, 256:257]), reads=[bO], writes=[brd])
                    fw.op("dve", lambda O=O, sc=sc, h=h: nc.vector.scalar_tensor_tensor(
                        ym[:, sc, h * 256:(h + 1) * 256], O[:, 0:256], rd[:, sc:sc + 1], zsm[:, sc, h * 256:(h + 1) * 256],
                        ALU.mult, ALU.mult), reads=[bO, brd, bzsm], writes=[bym])
            for sc in range(4):
                fw.ops("pe", [(lambda b=b, sc=sc: nc.tensor.transpose(P7[:, b * 128:(b + 1) * 128], ym[:, sc, b * 128:(b + 1) * 128],
                                                                     identb[:, :])) for b in range(8)],
                       reads=[bym, b_ident], writes=[bP7])
                fw.op("act", lambda: nc.scalar.copy(ymT[:, :], P7[:, :]), reads=[bP7], writes=[bymT])
                t0 = tt * 512 + sc * 128
                fw.dma("sp", YT[3072:4096, t0:t0 + 128].rearrange("(b p) t -> p b t", p=128),
                       ymT[:, :].rearrange("p (b t) -> p b t", b=8), reads=[bymT])
        fw.barrier()
    if stop_after <= 7:
        fw.finish("sp")
        return nc

    with ExitStack() as es:
        sb = lambda n, shp, dt: es.enter_context(nc.sbuf_tensor("o_" + n, shp, dt))
        lg = sb("lg", [128, D], F32); lb = sb("lb", [128, D], F32)
        yT = sb("yT", [128, 32, 512], BF16)
        xr = sb("xr", [128, 4, D], F32)
        sqt = sb("sqt", [128, D], F32)
        wo = [sb("wo%d" % i, [128, 32, 128], BF16) for i in range(3)]
        st = sb("st", [128, 16], F32)
        blg = Buf(); byT = Buf(); bxr = Buf(); bsqt = Buf(); bwo = [Buf(), Buf(), Buf()]; bst = Buf()
        fw.dma("sp", lg[:, :], ln_g.ap().partition_broadcast(128), writes=[blg])
        fw.dma("sp", lb[:, :], ln_b.ap().partition_broadcast(128), writes=[blg])
        wov = w_out.ap().rearrange("(kc p) n -> p kc n", p=128)
        nb = 0
        for tt in range(4):
            fw.dma("sp", yT[:, :, :], YT.ap().rearrange("(kc p) t -> p kc t", p=128)[:, :, tt * 512:(tt + 1) * 512], writes=[byT])
            for sc in range(4):
                r0 = HALO + tt * 512 + sc * 128
                fw.dma("sp", xr[:, sc, :], xh[r0:r0 + 128, :], writes=[bxr])
            for blk in range(32):
                s = nb % 3
                pp, bpp = (PA, bPA0) if nb % 2 == 0 else (PB, bPB0)
                nb += 1
                for q4 in range(4):
                    fw.dma("pool", wo[s][:, q4 * 8:(q4 + 1) * 8, :], wov[:, q4 * 8:(q4 + 1) * 8, blk * 128:(blk + 1) * 128],
                           writes=[bwo[s]])
                fns = []
                for sc in range(4):
                    for kc in range(32):
                        fns.append(lambda sc=sc, kc=kc, s=s, pp=pp: nc.tensor.matmul(
                            pp[:, sc * 128:(sc + 1) * 128], yT[:, kc, sc * 128:(sc + 1) * 128], wo[s][:, kc, :],
                            start=(kc == 0), stop=(kc == 31)))
                fw.ops("pe", fns, reads=[byT, bwo[s]], writes=[bpp])
                xs_ = xr[:, :, blk * 128:(blk + 1) * 128]
                fw.op("dve", lambda xs_=xs_, pp=pp: nc.vector.scalar_tensor_tensor(
                    xs_, xs_, ALPHA, pp[:, 0:512].rearrange("p (a b) -> p a b", a=4), ALU.mult, ALU.add),
                    reads=[bpp, bxr], writes=[bxr])
            fw.op("dve", lambda: nc.vector.reduce_sum(st[:, 0:4], xr[:, :, :], AX.X), reads=[bxr], writes=[bst])
            fw.op("dve", lambda: nc.vector.tensor_scalar_mul(st[:, 4:8], st[:, 0:4], -1.0 / D), reads=[bst], writes=[bst])
            for sc in range(4):
                fw.op("act", lambda sc=sc: nc.scalar.activation(xr[:, sc, :], xr[:, sc, :], AF.Identity,
                                                                bias=st[:, 4 + sc:5 + sc], scale=1.0),
                      reads=[bxr, bst], writes=[bxr])
                fw.op("dve", lambda sc=sc: nc.vector.tensor_tensor(sqt[:, :], xr[:, sc, :], xr[:, sc, :], ALU.mult),
                      reads=[bxr], writes=[bsqt])
                fw.op("dve", lambda sc=sc: nc.vector.reduce_sum(st[:, 8 + sc:9 + sc], sqt[:, :], AX.X), reads=[bsqt], writes=[bst])
            fw.op("dve", lambda: nc.vector.tensor_scalar(st[:, 8:12], st[:, 8:12], 1.0 / D, LN_EPS, ALU.mult, ALU.add),
                  reads=[bst], writes=[bst])
            fw.op("act", lambda: nc.scalar.sqrt(st[:, 12:16], st[:, 8:12]), reads=[bst], writes=[bst])
            fw.op("dve", lambda: nc.vector.reciprocal(st[:, 12:16], st[:, 12:16]), reads=[bst], writes=[bst])
            for sc in range(4):
                fw.op("dve", lambda sc=sc: nc.vector.scalar_tensor_tensor(xr[:, sc, :], xr[:, sc, :], st[:, 12 + sc:13 + sc],
                                                                         lg[:, :], ALU.mult, ALU.mult),
                      reads=[bxr, bst, blg], writes=[bxr])
                fw.op("pool", lambda sc=sc: nc.gpsimd.tensor_tensor(xr[:, sc, :], xr[:, sc, :], lb[:, :], ALU.add),
                      reads=[bxr, blg], writes=[bxr])
                r0 = tt * 512 + sc * 128
                fw.dma("sp", out[r0:r0 + 128, :], xr[:, sc, :], reads=[bxr])
        fw.barrier()

    fw.finish("sp")
    return nc


def _prep_inputs(inputs):
    x = np.asarray(inputs["x"], np.float32)[0]
    sh = _shared_consts()
    in_maps = []
    for r in range(NCORES):
        t0 = r * TOK
        rs = slice(r * (D // NCORES), (r + 1) * (D // NCORES)) if WSHARD else slice(None)
        xhh = np.zeros((NT, D), np.float32)
        lo = max(0, t0 - HALO)
        xhh[HALO - (t0 - lo):] = x[lo:t0 + TOK]
        m = {"xh": xhh,
             "w_in": np.asarray(inputs["w_in"], np.float32)[rs],
             "mem": np.asarray(inputs["mem"], np.float32)[0],
             "w_mem_kv": np.asarray(inputs["w_mem_kv"], np.float32)[rs],
             "w_out": np.asarray(inputs["w_out"], np.float32)[rs],
             "w_cmp_k1": np.asarray(inputs["w_cmp_k1"], np.float32),
             "w_cmp_k2": np.asarray(inputs["w_cmp_k2"], np.float32),
             "w_cmp_v1": np.asarray(inputs["w_cmp_v1"], np.float32),
             "w_cmp_v2": np.asarray(inputs["w_cmp_v2"], np.float32),
             "pe_cmp_k": np.asarray(inputs["pe_cmp_k"], np.float32),
             "pe_cmp_v": np.asarray(inputs["pe_cmp_v"], np.float32),
             "gm_ln_g": np.asarray(inputs["gm_ln_g"], np.float32).reshape(1, 1024),
             "gm_ln_b": np.asarray(inputs["gm_ln_b"], np.float32).reshape(1, 1024),
             "w_spatial": np.asarray(inputs["w_spatial"], np.float32),
             "b_spatial": np.asarray(inputs["b_spatial"], np.float32),
             "ln_g": np.asarray(inputs["ln_g"], np.float32).reshape(1, D),
             "ln_b": np.asarray(inputs["ln_b"], np.float32).reshape(1, D)}
        m.update(sh)
        m.update(_core_consts(r, None))
        in_maps.append(m)
    return in_maps


def kernel(**inputs):
    in_maps = _prep_inputs(inputs)
    nc = build()
    res = run_bass_kernel_spmd(nc, in_maps, core_ids=list(range(NCORES)))
    outs = [np.asarray(res.results[r]["out"], np.float32) for r in range(NCORES)]
    return np.concatenate(outs, axis=0).reshape(1, SEQ, D)
```

```python
import os
import numpy as np
import ml_dtypes
from contextlib import ExitStack
import concourse.bass as bass
import concourse.mybir as mybir
from concourse.bass_utils import run_bass_kernel_spmd

F32 = mybir.dt.float32
BF16 = mybir.dt.bfloat16
AF = mybir.ActivationFunctionType
ALU = mybir.AluOpType
AX = mybir.AxisListType

NCORES = 8
WSHARD = True
D = 4096
SEQ = 16384
TOK = SEQ // NCORES
HALO = 512
NT = TOK + HALO
INW = 12336
DH = 128
SCALE = DH ** -0.5
ISC = DH ** 0.5
BIG = 30000.0
LN_EPS = 1e-5
ALPHA = 2.0 ** 0.25
COLS = dict(q=(0, 2048), kc=(2048, 512), vc=(2560, 512), ks=(3072, 512), vs=(3584, 512),
            kw=(4096, 512), vw=(4608, 512), gate=(5120, 48), zn=(5168, 2048), u=(7216, 1024),
            vg=(8240, 1024), zg=(9264, 1024), qm=(10288, 1024), zm=(11312, 1024))
DMAX = [3, 10, 40, 200]
SLOPES = np.exp2(-8.0 * np.arange(1, 17, dtype=np.float64) / 16).astype(np.float32)


class Buf:
    __slots__ = ("name", "w", "rs")

    def __init__(self, name=""):
        self.name = name
        self.w = None
        self.rs = []


class Tok:
    __slots__ = ("sem", "val", "eng", "dma")

    def __init__(self, sem, val, eng, dma):
        self.sem, self.val, self.eng, self.dma = sem, val, eng, dma


class FW:
    def __init__(self, nc, n_dma_sems=64):
        self.nc = nc
        self.engs = {"pe": nc.tensor, "act": nc.scalar, "dve": nc.vector,
                     "pool": nc.gpsimd, "sp": nc.sync}
        self.esem = {e: nc.alloc_semaphore(name="s_" + e) for e in self.engs}
        self.ecnt = {e: 0 for e in self.engs}
        self.dsems = [nc.alloc_semaphore(name="d%d" % i) for i in range(n_dma_sems)]
        self.dcnt = [0] * n_dma_sems
        self.dlast = [None] * n_dma_sems
        self.dnext = 0
        self.known = {e: {} for e in self.engs}
        self.nwaits = 0
        self.nins = 0

    def _wait(self, e, tok):
        if tok is None:
            return
        k = self.known[e]
        sid = id(tok.sem)
        if k.get(sid, 0) >= tok.val:
            return
        self.engs[e].wait_ge(tok.sem, tok.val)
        self.nwaits += 1
        k[sid] = tok.val

    def _deps(self, e, reads, writes, is_dma):
        for b in reads:
            t = b.w
            if t is not None:
                self._wait(e, t)
        for b in writes:
            t = b.w
            if t is not None and (t.dma or is_dma or t.eng != e):
                self._wait(e, t)
            for r in b.rs:
                if r.dma or is_dma or r.eng != e:
                    self._wait(e, r)

    def _commit(self, tok, reads, writes):
        for b in reads:
            b.rs.append(tok)
        for b in writes:
            b.w = tok
            b.rs = []

    def op(self, e, fn, reads=(), writes=()):
        self._deps(e, reads, writes, False)
        ins = fn()
        self.nins += 1
        self.ecnt[e] += 1
        ins.then_inc(self.esem[e], 1)
        tok = Tok(self.esem[e], self.ecnt[e], e, False)
        self._commit(tok, reads, writes)
        return tok

    def ops(self, e, fns, reads=(), writes=()):
        self._deps(e, reads, writes, False)
        ins = None
        for fn in fns:
            ins = fn()
            self.nins += 1
        self.ecnt[e] += 1
        ins.then_inc(self.esem[e], 1)
        tok = Tok(self.esem[e], self.ecnt[e], e, False)
        self._commit(tok, reads, writes)
        return tok

    def dma(self, e, out, in_, reads=(), writes=(), **kw):
        i = self.dnext
        self.dnext = (self.dnext + 1) % len(self.dsems)
        self._wait(e, self.dlast[i])
        self._deps(e, reads, writes, True)
        ins = self.engs[e].dma_start(out=out, in_=in_, **kw)
        self.nins += 1
        self.dcnt[i] += 16
        ins.then_inc(self.dsems[i], 16)
        tok = Tok(self.dsems[i], self.dcnt[i], e, True)
        self.dlast[i] = tok
        self._commit(tok, reads, writes)
        return tok

    def allgather(self, in_ap, out_ap, reads=(), writes=()):
        e = "pool"
        i = self.dnext
        self.dnext = (self.dnext + 1) % len(self.dsems)
        self._wait(e, self.dlast[i])
        self._deps(e, reads, writes, True)
        ins = self.nc.gpsimd.collective_compute(
            "AllGather", ALU.bypass, replica_groups=[list(range(NCORES))],
            ins=[in_ap], outs=[out_ap])
        self.dcnt[i] += 1
        ins.then_inc(self.dsems[i], 1)
        tok = Tok(self.dsems[i], self.dcnt[i], e, True)
        self.dlast[i] = tok
        self._commit(tok, reads, writes)
        return tok

    def _all_toks(self):
        toks = []
        for e in self.engs:
            if self.ecnt[e] > 0:
                toks.append(Tok(self.esem[e], self.ecnt[e], e, False))
        for t in self.dlast:
            if t is not None:
                toks.append(t)
        return toks

    def barrier(self):
        toks = self._all_toks()
        for e in self.engs:
            for t in toks:
                if t.eng == e and not t.dma:
                    continue
                self._wait(e, t)

    def finish(self, e="sp"):
        for t in self._all_toks():
            self._wait(e, t)


def _bf(a):
    return np.asarray(a, np.float32).astype(ml_dtypes.bfloat16)


def _split3(a):
    a = np.asarray(a, np.float64)
    h = _bf(a).astype(np.float64)
    m = _bf(a - h).astype(np.float64)
    l = _bf(a - h - m)
    return _bf(h), _bf(m), l


def _shared_consts():
    c = {}
    c["ident"] = np.eye(128, dtype=np.float32)
    lt = np.zeros((71, 128, 128), np.float32)
    k = np.arange(128)
    for ch in range(128):
        jq = 2 * (ch % 32) + k // 64
        lt[jq, ch, k] = 1.0
        lt[64, ch, :] = 128.0 * ch
        lt[65, ch, :] = 128.0 * ch
        lt[66, ch, :] = k
        lt[67, ch, :] = k
        lt[68:71, ch, :] = 1.0
    c["ltab"] = _bf(lt.reshape(71, 128 * 128))
    dq = np.arange(128)[None, None, :]
    rk = np.arange(128)[:, None, None]
    bs = np.zeros((4, 128, 4, 128), np.float32)
    bw = np.zeros((4, 5, 128, 4, 128), np.float32)
    for g in range(4):
        sl = SLOPES[4 * g:4 * g + 4].astype(np.float64)[None, :, None]
        dist = (dq - rk) + np.zeros((128, 4, 128))
        b = np.where(dist >= 0, -sl * dist, -BIG) * ISC
        bs[g] = b
        for m in range(5):
            dist = 128.0 * m + (dq - rk) + np.zeros((128, 4, 128))
            b = np.where((dist >= 0) & (dist < 512), -sl * dist, -BIG) * ISC
            bw[g, m] = b
    c["bs"] = _bf(bs.reshape(4, 128, 512))
    c["bw"] = _bf(bw.reshape(4, 5, 128, 512))
    bc = np.zeros((16, 4, 128, 4, 128), np.float32)
    mp = np.arange(128)[:, None, None]
    for i in range(16):
        for g in range(4):
            sl = SLOPES[4 * g:4 * g + 4].astype(np.float64)[None, :, None]
            dist = 128.0 * i + dq - 16.0 * mp - 15.0 + np.zeros((128, 4, 128))
            bc[i, g] = np.where(dist >= 0, -sl * dist, -BIG) * ISC
    c["bc"] = _bf(bc.reshape(16, 4, 128, 512))
    mc = np.zeros((8, 128, 256), np.float32)
    for cc in range(8):
        for r in range(128):
            idx = 128 * cc + r
            for j in (idx // 4, idx // 4 - 1):
                if 0 <= j < 256 and 4 * j <= idx <= 4 * j + 4:
                    mc[cc, r, j] = 1.0
    c["mc"] = _bf(mc)
    tri = (np.arange(128)[:, None] <= np.arange(128)[None, :]).astype(np.float32)
    c["tri"] = tri
    return c


def _core_consts(r, shared_mc):
    c = {}
    t = 2048 * r + np.arange(2048)
    ra = np.zeros((16, 4, 7, 4, 4, 128), np.float32)
    rc = np.zeros((16, 4, 8, 4, 128), np.float32)
    rab = np.zeros((16, 4, 7, 4, 4, 128), ml_dtypes.bfloat16)
    rcb = np.zeros((16, 4, 8, 4, 128), ml_dtypes.bfloat16)
    for g in range(4):
        sl = SLOPES[4 * g:4 * g + 4].astype(np.float64) * ISC
        shi = _bf(sl).astype(np.float64)
        slo = _bf(sl - shi)
        for i in range(16):
            tq = t[128 * i:128 * (i + 1)].astype(np.float64)
            st = -(sl[:, None] * tq[None, :])
            a, b, cpart = _split3(st)
            stc = -(sl[:, None] * (tq[None, :] - 15.0))
            a2, b2, c2 = _split3(stc)
            for cp in range(4):
                rab[i, g, 0, cp] = _bf(shi)[:, None]
                rab[i, g, 1, cp] = slo[:, None]
                rab[i, g, 2, cp] = _bf(shi)[:, None]
                rab[i, g, 3, cp] = slo[:, None]
                rab[i, g, 4, cp] = a
                rab[i, g, 5, cp] = b
                rab[i, g, 6, cp] = cpart
            rcb[i, g, 0] = _bf(shi)[:, None]
            rcb[i, g, 1] = slo[:, None]
            rcb[i, g, 2] = _bf(shi)[:, None]
            rcb[i, g, 3] = slo[:, None]
            rcb[i, g, 4] = a2
            rcb[i, g, 5] = b2
            rcb[i, g, 6] = c2
            rcb[i, g, 7] = _bf(np.full((4, 128), -BIG * ISC))
    c["raug"] = rab.reshape(16, 4, 7, 4 * 512)
    c["rcmp"] = rcb.reshape(16, 4, 8, 512)
    lc = np.zeros((8, 8, 128), np.float32)
    for cc in range(8):
        lc[0, cc] = 2048.0 * cc
        lc[1, cc] = 2048.0 * cc
        lc[2, cc] = 16.0 * np.arange(128)
        lc[3, cc] = 16.0 * np.arange(128)
        lc[4:7, cc] = 1.0
        lc[7, cc] = 0.0 if cc < r else 1.0
    lc[7, 0, 0] = 1.0
    c["lcmp"] = _bf(lc.reshape(8, 8 * 128))
    ml = np.zeros((128, 256), np.float32)
    for m in range(128):
        idx = 128 * r + m
        for j in (idx // 4, idx // 4 - 1):
            if 0 <= j < 256 and 4 * j <= idx <= 4 * j + 4:
                ml[m, j] = 1.0
    c["mloc"] = _bf(ml)
    pen = np.zeros((128, 2), np.float32)
    if r == 0:
        pen[0, 0] = -BIG
        pen[:, 1] = -BIG
    c["pen"] = pen
    sb = np.zeros((2048, 256), np.float32)
    val = np.zeros((2048, 256), np.float32)
    j = np.arange(256)[None, :]
    cur = (t // 64)[:, None]
    sb[:] = 0.0
    forced = (j == 0) | (j == cur) | (j == cur - 1)
    sb = np.where(forced, 1.0e4 + j, sb)
    sb = np.where(j > cur, -1.0e9, sb).astype(np.float32)
    I = (t // 128)[:, None]
    val = (j < 2 * I).astype(np.float32)
    c["sbm"] = sb
    c["val"] = val
    return c


def build(stop_after=99, dbg=()):
    nc = bass.Bass("TRN2", target_bir_lowering=False)
    fw = FW(nc)

    def din(name, shape, dt=F32):
        return nc.dram_tensor(name, list(shape), dt, kind="ExternalInput")

    def dscr(name, shape, dt=BF16):
        if name in dbg:
            return nc.dram_tensor(name, list(shape), dt, kind="ExternalOutput")
        return nc.dram_tensor(name, list(shape), dt)

    xh = din("xh", [NT, D])
    mem = din("mem", [256, D])
    if WSHARD:
        w_in_s = din("w_in", [D // NCORES, INW]); w_mem_kv_s = din("w_mem_kv", [D // NCORES, 2048])
        w_out_s = din("w_out", [D // NCORES, D])
        w_in = nc.dram_tensor("w_in_full", [D, INW], F32)
        w_mem_kv = nc.dram_tensor("w_mem_kv_full", [D, 2048], F32)
        w_out = nc.dram_tensor("w_out_full", [D, D], F32)
        for (src, full, ncol) in [(w_in_s, w_in, INW), (w_mem_kv_s, w_mem_kv, 2048), (w_out_s, w_out, D)]:
            loc = nc.dram_tensor(full.name + "_loc", [D // NCORES, ncol], F32)
            bl = Buf()
            for r4 in range(4):
                fw.dma("sp", loc[r4 * 128:(r4 + 1) * 128, :], src[r4 * 128:(r4 + 1) * 128, :], writes=[Buf()])
            fw.barrier()
            fw.allgather(loc.ap(), full.ap())
        fw.barrier()
    else:
        w_in = din("w_in", [D, INW])
        w_mem_kv = din("w_mem_kv", [D, 2048])
        w_out = din("w_out", [D, D])
    w_k1 = din("w_cmp_k1", [4096, 128]); w_k2 = din("w_cmp_k2", [128, 128])
    w_v1 = din("w_cmp_v1", [4096, 128]); w_v2 = din("w_cmp_v2", [128, 128])
    pe_k = din("pe_cmp_k", [32, 128]); pe_v = din("pe_cmp_v", [32, 128])
    gm_g = din("gm_ln_g", [1, 1024]); gm_b = din("gm_ln_b", [1, 1024])
    w_sp = din("w_spatial", [4, 128, 128]); b_sp = din("b_spatial", [4, 128])
    ln_g = din("ln_g", [1, D]); ln_b = din("ln_b", [1, D])
    c_ident = din("ident", [128, 128])
    c_ltab = din("ltab", [71, 128 * 128], BF16)
    c_bs = din("bs", [4, 128, 512], BF16)
    c_bw = din("bw", [4, 5, 128, 512], BF16)
    c_bc = din("bc", [16, 4, 128, 512], BF16)
    c_mc = din("mc", [8, 128, 256], BF16)
    c_tri = din("tri", [128, 128])
    c_raug = din("raug", [16, 4, 7, 2048], BF16)
    c_rcmp = din("rcmp", [16, 4, 8, 512], BF16)
    c_lcmp = din("lcmp", [8, 1024], BF16)
    c_mloc = din("mloc", [128, 256], BF16)
    c_pen = din("pen", [128, 2])
    c_sbm = din("sbm", [2048, 256])
    c_val = din("val", [2048, 256])
    out = nc.dram_tensor("out", [TOK, D], F32, kind="ExternalOutput")

    QT = dscr("QT", [2048, TOK]); KCT = dscr("KCT", [512, NT]); VCT = dscr("VCT", [512, NT])
    KST = dscr("KST", [512, TOK]); KWT = dscr("KWT", [512, NT]); QMT = dscr("QMT", [1024, TOK])
    VS = dscr("VS", [TOK, 512]); VW = dscr("VW", [NT, 512]); GATE = dscr("GATE", [TOK, 48])
    ZN = dscr("ZN", [TOK, 2048]); U = dscr("U", [TOK, 1024]); VG = dscr("VG", [TOK, 1024])
    ZG = dscr("ZG", [TOK, 1024]); ZM = dscr("ZM", [TOK, 1024])
    KMT = dscr("KMT", [1024, 256]); VM = dscr("VM", [256, 1024])
    KCC = dscr("KCC", [512, 128]); VCC = dscr("VCC", [128, 512])
    KST_all = dscr("KST_all", [8 * 512, TOK]); VS_all = dscr("VS_all", [SEQ, 512])
    KCC_all = dscr("KCC_all", [8 * 512, 128]); VCC_all = dscr("VCC_all", [1024, 512])
    YT = dscr("YT", [D, TOK])

    PA = nc.alloc_psum_tensor("PA", [128, 1024], F32)
    PB = nc.alloc_psum_tensor("PB", [128, 1024], F32)
    P4 = nc.alloc_psum_tensor("P4", [128, 512], F32)
    P5 = nc.alloc_psum_tensor("P5", [128, 512], F32)
    P6 = nc.alloc_psum_tensor("P6", [128, 512], F32)
    P7 = nc.alloc_psum_tensor("P7", [128, 1024], BF16)
    bPA, bPB, bP4, bP5, bP6, bP7 = (Buf(n) for n in ["PA", "PB", "P4", "P5", "P6", "P7"])
    bPA0, bPA1, bPB0, bPB1 = Buf("PA0"), Buf("PA1"), Buf("PB0"), Buf("PB1")

    ident = nc.alloc_sbuf_tensor("identf", [128, 128], F32)
    identb = nc.alloc_sbuf_tensor("identb", [128, 128], BF16)
    b_ident = Buf("ident")
    fw.dma("sp", ident[:, :], c_ident[:, :], writes=[b_ident])
    fw.op("dve", lambda: nc.vector.tensor_copy(identb[:, :], ident[:, :]), reads=[b_ident], writes=[b_ident])

    cnt = {"ev": 0}

    def evac_eng():
        cnt["ev"] += 1
        return "act" if cnt["ev"] % 2 else "dve"

    def copy_on(e, o, i):
        if e == "act":
            return lambda: nc.scalar.copy(o, i)
        if e == "dve":
            return lambda: nc.vector.tensor_copy(o, i)
        return lambda: nc.gpsimd.tensor_copy(o, i)

    class WLoader:
        def __init__(self, es, tag, units, ns=3, pf=2):
            self.units = units
            self.ns, self.pf = ns, pf
            self.stage = [es.enter_context(nc.sbuf_tensor("wst%d%s" % (i, tag), [128, 4096], F32)) for i in range(ns)]
            self.bst = [Buf() for _ in range(ns)]
            self.loaded = 0

        def view(self, u):
            un = self.units[u]
            k, w = un["k"], un["w"]
            return self.stage[u % self.ns][:, 0:k * w].rearrange("p (k c) -> p k c", k=k)

        def need(self, u):
            while self.loaded < min(len(self.units), u + 1 + self.pf):
                v = self.loaded
                un = self.units[v]
                sv = self.view(v)
                nd = un["nd"]
                kk = un["k"] // nd
                for d in range(nd):
                    fw.dma("sp", sv[:, d * kk:(d + 1) * kk, :], un["src"][:, d * kk:(d + 1) * kk, :], writes=[self.bst[v % self.ns]])
                self.loaded += 1
            un = self.units[u]
            sv = self.view(u)
            fw.op("pool", lambda: nc.gpsimd.tensor_copy(un["dst"], sv), reads=[self.bst[u % self.ns]], writes=[un["dbuf"]])

    WT = 256

    def proj_phase(xsrc, wsrc, tiles, tag):
        with ExitStack() as es:
            xT = es.enter_context(nc.sbuf_tensor("xT" + tag, [128, 32, 1024], BF16))
            xs = [es.enter_context(nc.sbuf_tensor("xs%d%s" % (i, tag), [128, D], F32)) for i in range(1)]
            wb = [es.enter_context(nc.sbuf_tensor("wb%d%s" % (i, tag), [128, 32, 128], BF16)) for i in range(3)]
            wbw = [es.enter_context(nc.sbuf_tensor("wbw%d%s" % (i, tag), [128, 32, WT], BF16)) for i in range(2)]
            ost = [es.enter_context(nc.sbuf_tensor("ost%d%s" % (i, tag), [128, 1024], BF16)) for i in range(3)]
            b_xT = Buf("xT"); b_xs = [Buf()]; b_wb = [Buf(), Buf(), Buf()]; b_wbw = [Buf(), Buf()]
            b_ost = [Buf(), Buf(), Buf()]
            wv = wsrc.ap().rearrange("(kc p) n -> p kc n", p=128)
            jobs = []
            units = []
            nfm = ntm = 0
            for ti, (row0, ntok, blocks) in enumerate(tiles):
                for (mode, c0, w, dst) in blocks:
                    jb = dict(ti=ti, mode=mode, c0=c0, w=w, dst=dst, u0=len(units))
                    if mode == "FM":
                        sl = nfm % 3
                        nfm += 1
                        jb["slot"] = sl
                        units.append(dict(k=32, w=w, nd=4, src=wv[:, :, c0:c0 + w], dst=wb[sl][:, :, 0:w], dbuf=b_wb[sl]))
                    else:
                        sl = ntm % 2
                        ntm += 1
                        jb["slot"] = sl
                        for kh in range(2):
                            units.append(dict(k=16, w=w, nd=2, src=wv[:, kh * 16:(kh + 1) * 16, c0:c0 + w],
                                              dst=wbw[sl][:, kh * 16:(kh + 1) * 16, 0:w], dbuf=b_wbw[sl]))
                    jb["u1"] = len(units)
                    jobs.append(jb)
            wl = WLoader(es, tag, units)
            cw = {"o": 0}
            nblk = 0
            cur_tile = -1
            for jb in jobs:
                row0, ntok, _ = tiles[jb["ti"]]
                nch = ntok // 128
                if jb["ti"] != cur_tile:
                    cur_tile = jb["ti"]
                    for ch in range(nch):
                        s = 0
                        fw.dma("sp", xs[s][:, :], xsrc[row0 + ch * 128: row0 + (ch + 1) * 128, :], writes=[b_xs[s]])
                        for kg in range(8):
                            P, bP = (P4, bP4) if kg % 2 == 0 else (P5, bP5)
                            fw.ops("pe", [(lambda j=j, kg=kg, P=P, s=s: nc.tensor.transpose(
                                P[:, j * 128:(j + 1) * 128], xs[s][:, (kg * 4 + j) * 128:(kg * 4 + j + 1) * 128], ident[:, :]))
                                for j in range(4)], reads=[b_xs[s], b_ident], writes=[bP])
                            e = evac_eng()
                            fw.op(e, copy_on(e, xT[:, kg * 4:(kg + 1) * 4, ch * 128:(ch + 1) * 128],
                                             P[:, :].rearrange("p (a b) -> p a b", a=4)),
                                  reads=[bP], writes=[b_xT])
                mode, c0, w, dst, sl = jb["mode"], jb["c0"], jb["w"], jb["dst"], jb["slot"]
                for u in range(jb["u0"], jb["u1"]):
                    wl.need(u)
                if mode == "FM":
                    pp, bpp = (PA, bPA) if nblk % 2 == 0 else (PB, bPB)
                    nblk += 1
                    fns = []
                    nh = (ntok + 511) // 512
                    for kc in range(32):
                        for h in range(nh):
                            n0 = h * 512
                            n1 = min(ntok, n0 + 512)
                            fns.append(lambda kc=kc, n0=n0, n1=n1, sl=sl, pp=pp, w=w: nc.tensor.matmul(
                                pp[0:w, n0:n1], wb[sl][:, kc, 0:w], xT[:, kc, n0:n1], start=(kc == 0), stop=(kc == 31)))
                    fw.ops("pe", fns, reads=[b_wb[sl], b_xT], writes=[bpp])
                    so = cw["o"] % 3
                    cw["o"] += 1
                    e = evac_eng()
                    fw.op(e, copy_on(e, ost[so][0:w, 0:ntok], pp[0:w, 0:ntok]), reads=[bpp], writes=[b_ost[so]])
                    fw.dma("sp", dst, ost[so][0:w, 0:ntok], reads=[b_ost[so]])
                else:
                    G = min(4, nch)
                    for grp in range(nch // G):
                        pp, bpp = (PA, bPA) if nblk % 2 == 0 else (PB, bPB)
                        nblk += 1
                        fns = []
                        for j in range(G):
                            sbi = grp * G + j
                            for kc in range(32):
                                fns.append(lambda kc=kc, sbi=sbi, j=j, sl=sl, pp=pp, w=w: nc.tensor.matmul(
                                    pp[:, j * WT: j * WT + w], xT[:, kc, sbi * 128:(sbi + 1) * 128],
                                    wbw[sl][:, kc, 0:w], start=(kc == 0), stop=(kc == 31)))
                        fw.ops("pe", fns, reads=[b_wbw[sl], b_xT], writes=[bpp])
                        so = cw["o"] % 3
                        cw["o"] += 1
                        e = evac_eng()
                        fw.op(e, copy_on(e, ost[so][:, 0:G * w].rearrange("p (a b) -> p a b", a=G),
                                         pp[:, 0:G * WT].rearrange("p (a b) -> p a b", a=G)[:, :, 0:w]),
                              reads=[bpp], writes=[b_ost[so]])
                        fw.dma("sp", dst(grp, G), ost[so][:, 0:G * w].rearrange("p (a b) -> p a b", a=G), reads=[b_ost[so]])
            fw.barrier()

    FMT = dict(q=(QT, False), kc=(KCT, True), vc=(VCT, True), ks=(KST, False), kw=(KWT, True), qm=(QMT, False))
    TMT = dict(vs=(VS, False), vw=(VW, True), gate=(GATE, False), zn=(ZN, False), u=(U, False), vg=(VG, False),
               zg=(ZG, False), zm=(ZM, False))

    def in_blocks(names, row0, ntok):
        bl = []
        for nm in names:
            c0, wtot = COLS[nm]
            for b0 in range(0, wtot, 128):
                w = min(128, wtot - b0)
                if nm in FMT:
                    T, halo = FMT[nm]
                    tcol = row0 if halo else row0 - HALO
                    bl.append(("FM", c0 + b0, w, T[b0:b0 + w, tcol:tcol + ntok]))
                else:
                    if b0 % 256 != 0:
                        continue
                    w = min(256, wtot - b0)
                    T, halo = TMT[nm]
                    trow = row0 if halo else row0 - HALO
                    bl.append(("TM", c0 + b0, w,
                               (lambda grp, G, T=T, trow=trow, b0=b0, w=w:
                                T[trow + grp * G * 128:trow + (grp + 1) * G * 128, b0:b0 + w]
                                .rearrange("(a p) c -> p a c", p=128))))
        return bl

    order = ["ks", "vs", "kc", "vc", "kw", "vw", "q", "gate", "zn", "u", "vg", "zg", "qm", "zm"]
    tiles = [(0, 512, in_blocks(["kc", "vc", "kw", "vw"], 0, 512)),
             (512, 1024, in_blocks(order, 512, 1024)),
             (1536, 1024, in_blocks(order, 1536, 1024))]
    proj_phase(xh, w_in, tiles, "a")
    if stop_after <= 1:
        fw.finish("sp")
        return nc

    mblocks = []
    for b0 in range(0, 1024, 128):
        mblocks.append(("FM", b0, 128, KMT[b0:b0 + 128, 0:256]))
    for b0 in range(0, 1024, 256):
        mblocks.append(("TM", 1024 + b0, 256,
                        (lambda grp, G, b0=b0: VM[grp * G * 128:(grp + 1) * G * 128, b0:b0 + 256]
                         .rearrange("(a p) c -> p a c", p=128))))
    proj_phase(mem, w_mem_kv, [(0, 256, mblocks)], "m")
    if stop_after <= 2:
        fw.finish("sp")
        return nc

    with ExitStack() as es:
        sb = lambda n, shp, dt: es.enter_context(nc.sbuf_tensor(n, shp, dt))
        w1 = [sb("cw1%d" % i, [128, 32, 128], BF16) for i in range(2)]
        w2 = [sb("cw2%d" % i, [128, 128], BF16) for i in range(2)]
        peT = [sb("cpe%d" % i, [128, 32], BF16) for i in range(2)]
        AT = [sb("cAT%d" % i, [128, 4, 2064], BF16) for i in range(2)]
        cv = sb("ccv", [128, 2], F32)
        xhid = sb("cxh", [128, 512], F32)
        t1 = sb("ct1", [128, 512], F32)
        t2 = sb("ct2", [128, 512], F32)
        hT = sb("chT", [128, 512], BF16)
        cout = sb("cout", [128, 512], BF16)
        bw = Buf(); bA = Buf(); bcv = Buf(); bx = Buf(); bt1 = Buf(); bt2 = Buf(); bh = Buf(); bco = Buf()
        for kv, (wa, wb2, pe, SRC) in enumerate([(w_k1, w_k2, pe_k, KCT), (w_v1, w_v2, pe_v, VCT)]):
            fw.dma("pool", w1[kv][:, :, :], wa.ap().rearrange("(j p) o -> p j o", p=128), writes=[bw])
            fw.dma("pool", w2[kv][:, :], wb2[:, :], writes=[bw])
            fw.dma("pool", peT[kv][:, :], pe.ap().rearrange("j d -> d j"), writes=[bw], allow_slow_non_contiguous=True)
            fw.dma("sp", AT[kv][:, :, :], SRC.ap().rearrange("(g p) t -> p g t", p=128)[:, :, 496:2560], writes=[bA])
        for kv in range(2):
            fns = []
            for g in range(4):
                A4 = AT[kv][:, g, :].rearrange("p (m s) -> p m s", s=16)
                for j in range(32):
                    a, b = j // 16, j % 16
                    fns.append(lambda g=g, j=j, a=a, b=b, A4=A4: nc.tensor.matmul(
                        P4[:, g * 128:(g + 1) * 128], w1[kv][:, j, :], A4[:, a:a + 128, b],
                        start=(j == 0), stop=(j == 31)))
            fw.ops("pe", fns, reads=[bw, bA], writes=[bP4])
            fw.ops("pe", [(lambda j=j: nc.tensor.matmul(P6[:, 0:1], w1[kv][:, j, :], peT[kv][:, j:j + 1],
                                                        start=(j == 0), stop=(j == 31))) for j in range(32)],
                   reads=[bw], writes=[bP6])
            fw.op("dve", lambda: nc.vector.tensor_copy(cv[:, kv:kv + 1], P6[:, 0:1]), reads=[bP6], writes=[bcv])
            fw.op("act", lambda: nc.scalar.activation(xhid[:, :], P4[:, :], AF.Identity, bias=cv[:, kv:kv + 1], scale=1.0),
                  reads=[bP4, bcv], writes=[bx])
            fw.op("dve", lambda: nc.vector.tensor_tensor(t1[:, :], xhid[:, :], xhid[:, :], ALU.mult), reads=[bx], writes=[bt1])
            fw.op("dve", lambda: nc.vector.tensor_scalar(t2[:, :], t1[:, :], 0.044715, 1.0, ALU.mult, ALU.add),
                  reads=[bt1], writes=[bt2])
            fw.op("dve", lambda: nc.vector.tensor_tensor(t1[:, :], t2[:, :], xhid[:, :], ALU.mult), reads=[bt2, bx], writes=[bt1])
            fw.op("act", lambda: nc.scalar.activation(t2[:, :], t1[:, :], AF.Sigmoid, scale=1.5957691216057308),
                  reads=[bt1], writes=[bt2])
            fw.op("dve", lambda: nc.vector.tensor_tensor(hT[:, :], t2[:, :], xhid[:, :], ALU.mult), reads=[bt2, bx], writes=[bh])
            if kv == 0:
                fw.ops("pe", [lambda: nc.tensor.matmul(P5[:, :], w2[0][:, :], hT[:, :], start=True, stop=True)],
                       reads=[bw, bh], writes=[bP5])
                fw.op("dve", lambda: nc.vector.tensor_copy(cout[:, :], P5[:, :]), reads=[bP5], writes=[bco])
                fw.dma("sp", KCC.ap().rearrange("(g p) m -> p g m", p=128),
                       cout[:, :].rearrange("p (g m) -> p g m", g=4), reads=[bco])
            else:
                fw.ops("pe", [(lambda g=g: nc.tensor.matmul(P5[:, g * 128:(g + 1) * 128], hT[:, g * 128:(g + 1) * 128],
                                                            w2[1][:, :], start=True, stop=True)) for g in range(4)],
                       reads=[bw, bh], writes=[bP5])
                fw.op("dve", lambda: nc.vector.tensor_copy(cout[:, :], P5[:, :]), reads=[bP5], writes=[bco])
                fw.dma("sp", VCC[:, :], cout[:, :], reads=[bco])
        fw.barrier()
    if stop_after <= 3:
        fw.finish("sp")
        return nc

    for (a, b) in [(KCC, KCC_all), (VCC, VCC_all), (KST, KST_all), (VS, VS_all)]:
        fw.allgather(a.ap(), b.ap())
    fw.barrier()
    if stop_after <= 4:
        fw.finish("sp")
        return nc

    Sb = [(PA[:, 0:512], bPA0), (PA[:, 512:1024], bPA1), (P6[:, :], bP6)]
    Ob = [(PB[:, 0:512], bPB0), (PB[:, 512:1024], bPB1), (P4[:, :], bP4), (P5[:, :], bP5)]
    with ExitStack() as es:
        sb = lambda n, shp, dt: es.enter_context(nc.sbuf_tensor("a_" + n, shp, dt))
        KSTg = sb("KSTg", [128, 8, 2048], BF16); VSXg = sb("VSXg", [128, 128, 129], BF16)
        KSTl = sb("KSTl", [128, 2048], BF16); VSXl = sb("VSXl", [128, 16, 129], BF16)
        QTg = sb("QTg", [128, 4, 2048], BF16)
        KWTg = sb("KWTg", [128, 2560], BF16); VWXg = sb("VWXg", [128, 20, 129], BF16)
        KCCg = sb("KCCg", [128, 8, 128], BF16); KCCl = sb("KCCl", [128, 128], BF16)
        VCXg = sb("VCXg", [128, 9, 385], BF16)
        LTAB = sb("LTAB", [71, 128, 128], BF16); LCMP = sb("LCMP", [8, 8, 128], BF16)
        BSg = sb("BSg", [128, 512], BF16); BWg = sb("BWg", [128, 5, 512], BF16)
        BCt = [sb("BC%d" % i, [128, 512], BF16) for i in range(2)]
        raug = [sb("raug%d" % i, [71, 4, 512], BF16) for i in range(2)]
        rcmp = [sb("rcmp%d" % i, [8, 512], BF16) for i in range(2)]
        pen = sb("pen", [128, 2], F32)
        sbm = [sb("sbm%d" % i, [128, 256], F32) for i in range(2)]
        valt = [sb("val%d" % i, [128, 256], F32) for i in range(2)]
        gt = sb("gt", [128, 16, 48], BF16); gsig = sb("gsig", [128, 16, 48], F32)
        znt = [sb("znt%d" % i, [128, 512], BF16) for i in range(2)]
        zs = sb("zs", [128, 512], F32)
        pt = [sb("pt%d" % i, [128, 512], BF16) for i in range(4)]
        yacc = sb("yacc", [128, 512], F32); ybf = sb("ybf", [128, 512], BF16); yTs = sb("yTs", [128, 512], BF16)
        dn = sb("dn", [128, 4], F32); rdn = sb("rdn", [128, 4], F32); wts = sb("wts", [128, 4], F32)
        score = sb("score", [128, 256], F32); sc2 = sb("sc2", [128, 256], F32); sel = sb("sel", [128, 256], F32)
        m8a = sb("m8a", [128, 8], F32); m8b = sb("m8b", [128, 8], F32)
        nm = sb("nm", [128, 256], BF16)
        B = {n: Buf(n) for n in ["KSTg", "VSXg", "KSTl", "VSXl", "QTg", "KWTg", "VWXg", "KCCg", "KCCl", "VCXg", "LT",
                                 "BSg", "BWg", "pen", "gt", "gsig", "zs", "yacc", "ybf", "yTs", "dn", "rdn", "wts",
                                 "score", "sc2", "sel", "m8a", "m8b", "nm"]}
        bBC = [Buf(), Buf()]; braug = [Buf(), Buf()]; brcmp = [Buf(), Buf()]; bsbm = [Buf(), Buf()]
        bval = [Buf(), Buf()]; bzn = [Buf(), Buf()]; bpt = [Buf(), Buf(), Buf(), Buf()]

        fw.dma("sp", LTAB[:, :, :], c_ltab.ap().rearrange("r (c k) -> r c k", c=128), writes=[B["LT"]])
        fw.dma("sp", LCMP[:, :, :], c_lcmp.ap().rearrange("r (c k) -> r c k", c=8), writes=[B["LT"]])
        fw.dma("sp", pen[:, :], c_pen[:, :], writes=[B["pen"]])
        fw.op("pool", lambda: nc.gpsimd.memset(VSXg[:, :, 128:129], 1.0), writes=[B["VSXg"]])
        fw.op("pool", lambda: nc.gpsimd.memset(VSXl[:, :, 128:129], 1.0), writes=[B["VSXl"]])
        fw.op("pool", lambda: nc.gpsimd.memset(VWXg[:, :, 128:129], 1.0), writes=[B["VWXg"]])
        fw.op("pool", lambda: nc.gpsimd.memset(VCXg[:, :, 128:129], 1.0), writes=[B["VCXg"]])
        fw.dma("sp", VCXg[:, 0:8, 129:385], c_mc.ap().rearrange("c k j -> k c j"), writes=[B["VCXg"]])
        fw.dma("sp", VCXg[:, 8, 129:385], c_mloc[:, :], writes=[B["VCXg"]])
        fw.dma("sp", gt[:, :, :], GATE.ap().rearrange("(i p) c -> p i c", p=128), writes=[B["gt"]])
        fw.op("act", lambda: nc.scalar.activation(gsig[:, :, :], gt[:, :, :], AF.Sigmoid), reads=[B["gt"]], writes=[B["gsig"]])

        pcount = {"s": 0, "p": 0, "it": 0}

        def attn_stream(chunks, qap, qbufs):
            n = len(chunks)
            LOOK = 2
            pend = []
            for ci in range(n + LOOK):
                if ci < n:
                    ch = chunks[ci]
                    S, bS = Sb[pcount["s"] % 3]
                    pcount["s"] += 1
                    L, R, ab = ch["aug"]
                    fw.ops("pe", [lambda S=S, ch=ch: nc.tensor.matmul(S, ch["k"], qap, start=True, stop=False),
                                  lambda S=S, L=L, R=R: nc.tensor.matmul(S, L, R, start=False, stop=True)],
                           reads=list(ch["kb"]) + list(qbufs) + list(ab), writes=[bS])
                    pi = pcount["p"] % 4
                    pcount["p"] += 1
                    if ch["bias"] is None:
                        fw.op("act", lambda S=S, pi=pi: nc.scalar.activation(pt[pi][:, :], S, AF.Exp, scale=SCALE),
                              reads=[bS], writes=[bpt[pi]])
                    else:
                        fw.op("act", lambda S=S, pi=pi, ch=ch: nc.scalar.activation(pt[pi][:, :], S, AF.Exp,
                                                                                 bias=ch["bias"], scale=SCALE),
                              reads=[bS, B["pen"]], writes=[bpt[pi]])
                    pend.append((ci, pi, ch))
                if pend and (ci >= n or len(pend) > LOOK):
                    pci, ppi, pch = pend.pop(0)
                    nco = pch["ncol"]
                    fw.ops("pe", [(lambda h=h, ppi=ppi, pch=pch, nco=nco, pci=pci: nc.tensor.matmul(
                        Ob[h][0][:, 0:nco], pt[ppi][:, h * 128:(h + 1) * 128], pch["v"],
                        start=(pci == 0), stop=(pci == n - 1))) for h in range(4)],
                        reads=[bpt[ppi]] + list(pch["vb"]), writes=[Ob[h][1] for h in range(4)])
            assert not pend

        def branch_out(g, i, br, first):
            for h in range(4):
                fw.op("dve", lambda h=h: nc.vector.tensor_scalar_max(dn[:, h:h + 1], Ob[h][0][:, 128:129], 1e-30),
                      reads=[Ob[h][1]], writes=[B["dn"]])
            fw.op("dve", lambda: nc.vector.reciprocal(rdn[:, :], dn[:, :]), reads=[B["dn"]], writes=[B["rdn"]])
            gv = gsig[:, i, g * 12:(g + 1) * 12].rearrange("p (h b) -> p h b", b=3)[:, :, br]
            fw.op("dve", lambda: nc.vector.tensor_tensor(wts[:, :], rdn[:, :], gv, ALU.mult),
                  reads=[B["rdn"], B["gsig"]], writes=[B["wts"]])
            for h in range(4):
                if first:
                    fw.op("dve", lambda h=h: nc.vector.tensor_scalar_mul(yacc[:, h * 128:(h + 1) * 128],
                                                                        Ob[h][0][:, 0:128], wts[:, h:h + 1]),
                          reads=[Ob[h][1], B["wts"]], writes=[B["yacc"]])
                else:
                    fw.op("dve", lambda h=h: nc.vector.scalar_tensor_tensor(
                        yacc[:, h * 128:(h + 1) * 128], Ob[h][0][:, 0:128], wts[:, h:h + 1],
                        yacc[:, h * 128:(h + 1) * 128], ALU.mult, ALU.add),
                        reads=[Ob[h][1], B["wts"], B["yacc"]], writes=[B["yacc"]])

        for g in range(int(os.environ.get('K_NG', 4))):
            fw.dma("sp", KSTg[:, :, :], KST_all.ap().rearrange("(r g p) t -> p r g t", r=8, g=4, p=128)[:, :, g, :],
                   writes=[B["KSTg"]])
            vsv = VS_all.ap().rearrange("(c k) (g d) -> k c g d", k=128, g=4)
            for c8 in range(16):
                fw.dma("sp", VSXg[:, c8 * 8:(c8 + 1) * 8, 0:128], vsv[:, c8 * 8:(c8 + 1) * 8, g, :], writes=[B["VSXg"]])
            fw.dma("sp", KSTl[:, :], KST[g * 128:(g + 1) * 128, :], writes=[B["KSTl"]])
            vsl = VS.ap().rearrange("(c k) (g d) -> k c g d", k=128, g=4)
            for c8 in range(2):
                fw.dma("sp", VSXl[:, c8 * 8:(c8 + 1) * 8, 0:128], vsl[:, c8 * 8:(c8 + 1) * 8, g, :], writes=[B["VSXl"]])
            fw.dma("sp", QTg[:, :, :], QT.ap().rearrange("(h p) t -> p h t", p=128)[:, 4 * g:4 * g + 4, :], writes=[B["QTg"]])
            fw.dma("sp", KWTg[:, :], KWT[g * 128:(g + 1) * 128, :], writes=[B["KWTg"]])
            vwv = VW.ap().rearrange("(c k) (g d) -> k c g d", k=128, g=4)
            for c8 in range(0, 20, 10):
                fw.dma("sp", VWXg[:, c8:c8 + 10, 0:128], vwv[:, c8:c8 + 10, g, :], writes=[B["VWXg"]])
            fw.dma("sp", KCCg[:, :, :], KCC_all.ap().rearrange("(r g p) m -> p r g m", r=8, g=4, p=128)[:, :, g, :],
                   writes=[B["KCCg"]])
            fw.dma("sp", KCCl[:, :], KCC[g * 128:(g + 1) * 128, :], writes=[B["KCCl"]])
            fw.dma("sp", VCXg[:, 0:8, 0:128], VCC_all.ap().rearrange("(c k) (g d) -> k c g d", k=128, g=4)[:, :, g, :],
                   writes=[B["VCXg"]])
            fw.dma("sp", VCXg[:, 8, 0:128], VCC[:, g * 128:(g + 1) * 128], writes=[B["VCXg"]])
            fw.dma("sp", BSg[:, :], c_bs[g, :, :], writes=[B["BSg"]])
            fw.dma("sp", BWg[:, :, :], c_bw.ap()[g].rearrange("m k f -> k m f"), writes=[B["BWg"]])
            for i in range(int(os.environ.get('K_NI', 16))):
                it = pcount["it"]
                pcount["it"] += 1
                sl = it % 2
                fw.dma("sp", BCt[sl][:, :], c_bc[i, g, :, :], writes=[bBC[sl]])
                fw.dma("sp", rcmp[sl][:, :], c_rcmp[i, g, :, :], writes=[brcmp[sl]])
                fw.dma("sp", raug[sl][64:71, :, :], c_raug.ap()[i, g].rearrange("r (c f) -> r c f", c=4), writes=[braug[sl]])
                fw.dma("sp", sbm[sl][:, :], c_sbm[i * 128:(i + 1) * 128, :], writes=[bsbm[sl]])
                fw.dma("sp", valt[sl][:, :], c_val[i * 128:(i + 1) * 128, :], writes=[bval[sl]])
                fw.dma("sp", znt[sl][:, :], ZN[i * 128:(i + 1) * 128, g * 512:(g + 1) * 512], writes=[bzn[sl]])
                qap = QTg[:, :, i * 128:(i + 1) * 128]
                qb = [B["QTg"]]
                chunks = []
                for cc in range(8):
                    chunks.append(dict(k=KCCg[:, cc, :], kb=[B["KCCg"]],
                                       aug=(LCMP[0:8, cc, :], rcmp[sl][0:8, :], [B["LT"], brcmp[sl]]),
                                       bias=None, v=VCXg[:, cc, 0:385], vb=[B["VCXg"]], ncol=385))
                chunks.append(dict(k=KCCl[:, :], kb=[B["KCCl"]], aug=(identb[:, :], BCt[sl][:, :], [b_ident, bBC[sl]]),
                                   bias=pen[:, 0:1], v=VCXg[:, 8, 0:385], vb=[B["VCXg"]], ncol=385))
                attn_stream(chunks, qap, qb)
                branch_out(g, i, 0, True)
                fw.op("dve", lambda: nc.vector.scalar_tensor_tensor(score[:, :], Ob[0][0][:, 129:385], rdn[:, 0:1],
                                                                   sbm[sl][:, :], ALU.mult, ALU.add),
                      reads=[Ob[0][1], B["rdn"], bsbm[sl]], writes=[B["score"]])
                for h in range(1, 4):
                    fw.op("dve", lambda h=h: nc.vector.scalar_tensor_tensor(score[:, :], Ob[h][0][:, 129:385], rdn[:, h:h + 1],
                                                                           score[:, :], ALU.mult, ALU.add),
                          reads=[Ob[h][1], B["rdn"], B["score"]], writes=[B["score"]])
                fw.op("dve", lambda: nc.vector.max(m8a[:, :], score[:, :]), reads=[B["score"]], writes=[B["m8a"]])
                fw.op("dve", lambda: nc.vector.match_replace(sc2[:, :], m8a[:, :], score[:, :], -3.0e38),
                      reads=[B["score"], B["m8a"]], writes=[B["sc2"]])
                fw.op("dve", lambda: nc.vector.max(m8b[:, :], sc2[:, :]), reads=[B["sc2"]], writes=[B["m8b"]])
                fw.op("dve", lambda: nc.vector.tensor_scalar(sel[:, :], score[:, :], m8b[:, 7:8], None, ALU.is_ge),
                      reads=[B["score"], B["m8b"]], writes=[B["sel"]])
                fw.op("dve", lambda: nc.vector.tensor_tensor(sc2[:, :], sel[:, :], valt[sl][:, :], ALU.mult),
                      reads=[B["sel"], bval[sl]], writes=[B["sc2"]])
                fw.op("dve", lambda: nc.vector.tensor_scalar(nm[:, :], sc2[:, :], -1.0, BIG * ISC, ALU.add, ALU.mult),
                      reads=[B["sc2"]], writes=[B["nm"]])
                fw.ops("pe", [(lambda qq=qq: nc.tensor.transpose(P7[0:64, qq * 128:(qq + 1) * 128],
                                                                nm[:, qq * 64:(qq + 1) * 64], identb[:, :])) for qq in range(4)],
                       reads=[B["nm"], b_ident], writes=[bP7])
                for qq in range(4):
                    e = "act" if qq % 2 else "pool_skip"
                    src = P7[0:64, qq * 128:(qq + 1) * 128].unsqueeze(1).to_broadcast([64, 4, 128])
                    dst = raug[sl][0:64, qq, :].rearrange("p (h q) -> p h q", h=4)
                    if qq % 2:
                        fw.op("act", lambda src=src, dst=dst: nc.scalar.copy(dst, src), reads=[bP7], writes=[braug[sl]])
                    else:
                        fw.op("dve", lambda src=src, dst=dst: nc.vector.tensor_copy(dst, src), reads=[bP7], writes=[braug[sl]])
                chunks = []
                for c in range(128):
                    if not any(1 <= 16 * r_ + i - c <= DMAX[g] for r_ in range(NCORES)):
                        continue
                    chunks.append(dict(k=KSTg[:, c // 16, (c % 16) * 128:(c % 16 + 1) * 128], kb=[B["KSTg"]],
                                       aug=(LTAB[0:71, c, :], raug[sl][0:71, c // 32, :], [B["LT"], braug[sl]]),
                                       bias=None, v=VSXg[:, c, 0:129], vb=[B["VSXg"]], ncol=129))
                chunks.append(dict(k=KSTl[:, i * 128:(i + 1) * 128], kb=[B["KSTl"]],
                                   aug=(identb[:, :], BSg[:, :], [b_ident, B["BSg"]]),
                                   bias=None, v=VSXl[:, i, 0:129], vb=[B["VSXl"]], ncol=129))
                attn_stream(chunks, qap, qb)
                branch_out(g, i, 1, False)
                chunks = []
                for m in range(4, -1, -1):
                    ci = 4 + i - m
                    chunks.append(dict(k=KWTg[:, ci * 128:(ci + 1) * 128], kb=[B["KWTg"]],
                                       aug=(identb[:, :], BWg[:, m, :], [b_ident, B["BWg"]]),
                                       bias=(pen[:, 1:2] if ci < 4 else None), v=VWXg[:, ci, 0:129], vb=[B["VWXg"]], ncol=129))
                attn_stream(chunks, qap, qb)
                branch_out(g, i, 2, False)
                fw.op("act", lambda: nc.scalar.activation(zs[:, :], znt[sl][:, :], AF.Silu), reads=[bzn[sl]], writes=[B["zs"]])
                fw.op("dve", lambda: nc.vector.tensor_tensor(ybf[:, :], yacc[:, :], zs[:, :], ALU.mult),
                      reads=[B["yacc"], B["zs"]], writes=[B["ybf"]])
                fw.ops("pe", [(lambda h=h: nc.tensor.transpose(P7[:, h * 128:(h + 1) * 128], ybf[:, h * 128:(h + 1) * 128],
                                                              identb[:, :])) for h in range(4)],
                       reads=[B["ybf"], b_ident], writes=[bP7])
                fw.op("dve", lambda: nc.vector.tensor_copy(yTs[:, :], P7[:, 0:512]), reads=[bP7], writes=[B["yTs"]])
                fw.dma("sp", YT[g * 512:(g + 1) * 512, i * 128:(i + 1) * 128].rearrange("(h p) t -> p h t", p=128),
                       yTs[:, :].rearrange("p (h t) -> p h t", h=4), reads=[B["yTs"]])
        fw.barrier()
    if stop_after <= 5:
        fw.finish("sp")
        return nc

    with ExitStack() as es:
        sb = lambda n, shp, dt: es.enter_context(nc.sbuf_tensor("g_" + n, shp, dt))
        gam = sb("gam", [128, 1024], F32); bet = sb("bet", [128, 1024], F32)
        wsn = sb("wsn", [128, 4, 128], F32); trit = sb("tri", [128, 128], F32)
        wsT = sb("wsT", [128, 512], F32); wsTb = sb("wsTb", [128, 512], BF16)
        bsp = sb("bsp", [128, 4], F32)
        vt = [sb("vt%d" % i, [128, 1024], BF16) for i in range(2)]
        ut = [sb("ut%d" % i, [128, 1024], BF16) for i in range(2)]
        zt = [sb("zt%d" % i, [128, 1024], BF16) for i in range(2)]
        cen = sb("cen", [128, 1024], F32); sq = sb("sq", [128, 1024], F32)
        st4 = sb("st4", [128, 4], F32)
        vnb = sb("vnb", [128, 1024], BF16); tm = sb("tm", [128, 1024], F32); zsg = sb("zsg", [128, 1024], F32)
        yg = sb("yg", [128, 1024], BF16); ygT = sb("ygT", [128, 1024], BF16)
        bc0 = Buf(); bws = Buf(); bvt = [Buf(), Buf()]; but = [Buf(), Buf()]; bzt = [Buf(), Buf()]
        bcen = Buf(); bsq = Buf(); bst = Buf(); bvn = Buf(); btm = Buf(); bzs = Buf(); byg = Buf(); bygT = Buf()
        fw.dma("sp", gam[:, :], gm_g.ap().partition_broadcast(128), writes=[bc0])
        fw.dma("sp", bet[:, :], gm_b.ap().partition_broadcast(128), writes=[bc0])
        fw.dma("sp", wsn[:, :, :], w_sp.ap().rearrange("g t s -> t g s"), writes=[bws])
        fw.dma("sp", trit[:, :], c_tri[:, :], writes=[bws])
        fw.dma("sp", bsp[:, :], b_sp.ap().rearrange("g t -> t g"), writes=[bc0], allow_slow_non_contiguous=True)
        fw.ops("pe", [(lambda g=g: nc.tensor.transpose(P6[:, g * 128:(g + 1) * 128], wsn[:, g, :], ident[:, :])) for g in range(4)],
               reads=[bws, b_ident], writes=[bP6])
        fw.op("dve", lambda: nc.vector.tensor_tensor(wsT[:, :].rearrange("p (g t) -> p g t", g=4),
                                                    P6[:, :].rearrange("p (g t) -> p g t", g=4),
                                                    trit[:, :].unsqueeze(1).to_broadcast([128, 4, 128]), ALU.mult),
              reads=[bP6, bws], writes=[bws])
        fw.op("dve", lambda: nc.vector.tensor_copy(wsTb[:, :], wsT[:, :]), reads=[bws], writes=[bws])
        for ch in range(16):
            s = ch % 2
            rows = slice(ch * 128, (ch + 1) * 128)
            fw.dma("sp", vt[s][:, :], VG[rows, :], writes=[bvt[s]])
            fw.dma("sp", ut[s][:, :], U[rows, :], writes=[but[s]])
            fw.dma("sp", zt[s][:, :], ZG[rows, :], writes=[bzt[s]])
            fw.op("dve", lambda s=s: nc.vector.reduce_sum(st4[:, 0:1], vt[s][:, :], AX.X), reads=[bvt[s]], writes=[bst])
            fw.op("dve", lambda: nc.vector.tensor_scalar_mul(st4[:, 1:2], st4[:, 0:1], -1.0 / 1024), reads=[bst], writes=[bst])
            fw.op("act", lambda s=s: nc.scalar.activation(cen[:, :], vt[s][:, :], AF.Identity, bias=st4[:, 1:2], scale=1.0),
                  reads=[bvt[s], bst], writes=[bcen])
            fw.op("dve", lambda: nc.vector.tensor_tensor(sq[:, :], cen[:, :], cen[:, :], ALU.mult), reads=[bcen], writes=[bsq])
            fw.op("dve", lambda: nc.vector.reduce_sum(st4[:, 2:3], sq[:, :], AX.X), reads=[bsq], writes=[bst])
            fw.op("dve", lambda: nc.vector.tensor_scalar(st4[:, 2:3], st4[:, 2:3], 1.0 / 1024, LN_EPS, ALU.mult, ALU.add),
                  reads=[bst], writes=[bst])
            fw.op("act", lambda: nc.scalar.sqrt(st4[:, 3:4], st4[:, 2:3]), reads=[bst], writes=[bst])
            fw.op("dve", lambda: nc.vector.reciprocal(st4[:, 3:4], st4[:, 3:4]), reads=[bst], writes=[bst])
            fw.op("dve", lambda: nc.vector.scalar_tensor_tensor(sq[:, :], cen[:, :], st4[:, 3:4], gam[:, :], ALU.mult, ALU.mult),
                  reads=[bcen, bst, bc0], writes=[bsq])
            fw.op("dve", lambda: nc.vector.tensor_tensor(vnb[:, :], sq[:, :], bet[:, :], ALU.add), reads=[bsq, bc0], writes=[bvn])
            fw.ops("pe", [(lambda g=g: nc.tensor.matmul(PA[:, g * 256:(g + 1) * 256], wsTb[:, g * 128:(g + 1) * 128],
                                                        vnb[:, g * 256:(g + 1) * 256], start=True, stop=True)) for g in range(4)],
                   reads=[bws, bvn], writes=[bPA])
            for g in range(4):
                fw.op("dve", lambda g=g, s=s: nc.vector.scalar_tensor_tensor(
                    tm[:, g * 256:(g + 1) * 256], PA[:, g * 256:(g + 1) * 256], bsp[:, g:g + 1],
                    ut[s][:, g * 256:(g + 1) * 256], ALU.add, ALU.mult), reads=[bPA, bc0, but[s]], writes=[btm])
            fw.op("act", lambda s=s: nc.scalar.activation(zsg[:, :], zt[s][:, :], AF.Silu), reads=[bzt[s]], writes=[bzs])
            fw.op("dve", lambda: nc.vector.tensor_tensor(yg[:, :], tm[:, :], zsg[:, :], ALU.mult), reads=[btm, bzs], writes=[byg])
            fw.ops("pe", [(lambda b=b: nc.tensor.transpose(P7[:, b * 128:(b + 1) * 128], yg[:, b * 128:(b + 1) * 128],
                                                          identb[:, :])) for b in range(8)],
                   reads=[byg, b_ident], writes=[bP7])
            fw.op("act", lambda: nc.scalar.copy(ygT[:, :], P7[:, :]), reads=[bP7], writes=[bygT])
            fw.dma("sp", YT[2048:3072, ch * 128:(ch + 1) * 128].rearrange("(b p) t -> p b t", p=128),
                   ygT[:, :].rearrange("p (b t) -> p b t", b=8), reads=[bygT])
        fw.barrier()
    if stop_after <= 6:
        fw.finish("sp")
        return nc

    with ExitStack() as es:
        sb = lambda n, shp, dt: es.enter_context(nc.sbuf_tensor("m_" + n, shp, dt))
        KM = sb("KM", [128, 8, 256], BF16); VMX = sb("VMX", [128, 2, 4, 257], BF16)
        QM = [sb("QM%d" % i, [128, 8, 512], BF16) for i in range(2)]
        zmt = [sb("zmt%d" % i, [128, 4, 1024], BF16) for i in range(2)]
        zsm = sb("zsm", [128, 4, 1024], F32)
        ym = sb("ym", [128, 4, 1024], BF16); ymT = sb("ymT", [128, 1024], BF16)
        ptm = [sb("ptm%d" % i, [128, 512], BF16) for i in range(4)]
        rd = sb("rd", [128, 4], F32)
        bkm = Buf(); bvm = Buf(); bqm = [Buf(), Buf()]; bzm = [Buf(), Buf()]; bzsm = Buf(); bym = Buf(); bymT = Buf()
        bptm = [Buf() for _ in range(4)]; brd = Buf()
        fw.dma("sp", KM[:, :, :], KMT.ap().rearrange("(c p) m -> p c m", p=128), writes=[bkm])
        fw.op("pool", lambda: nc.gpsimd.memset(VMX[:, :, :, 256:257], 1.0), writes=[bvm])
        for mc in range(2):
            fw.dma("sp", VMX[:, mc, :, 0:256], VM[mc * 128:(mc + 1) * 128, :].rearrange("k (h d) -> k h d", h=4), writes=[bvm])
        np_ = 0
        for tt in range(4):
            s = tt % 2
            fw.dma("sp", QM[s][:, :, :], QMT.ap().rearrange("(c p) t -> p c t", p=128)[:, :, tt * 512:(tt + 1) * 512], writes=[bqm[s]])
            fw.dma("sp", zmt[s][:, :, :], ZM[tt * 512:(tt + 1) * 512, :].rearrange("(s p) c -> p s c", p=128), writes=[bzm[s]])
            fw.op("act", lambda s=s: nc.scalar.activation(zsm[:, :, :], zmt[s][:, :, :], AF.Silu), reads=[bzm[s]], writes=[bzsm])
            for h in range(4):
                pis = []
                for mc in range(2):
                    S, bS = Sb[mc]
                    fw.ops("pe", [lambda S=S, mc=mc, h=h, s=s: nc.tensor.matmul(S, KM[:, h * 2, mc * 128:(mc + 1) * 128],
                                                                                QM[s][:, h * 2, :], start=True, stop=False),
                                  lambda S=S, mc=mc, h=h, s=s: nc.tensor.matmul(S, KM[:, h * 2 + 1, mc * 128:(mc + 1) * 128],
                                                                                QM[s][:, h * 2 + 1, :], start=False, stop=True)],
                           reads=[bkm, bqm[s]], writes=[bS])
                    pi = np_ % 4
                    np_ += 1
                    pis.append(pi)
                    fw.op("act", lambda S=S, pi=pi: nc.scalar.activation(ptm[pi][:, :], S, AF.Exp, scale=1.0 / 16.0),
                          reads=[bS], writes=[bptm[pi]])
                for sc in range(4):
                    O, bO = Ob[sc]
                    fw.ops("pe", [lambda O=O, sc=sc, h=h: nc.tensor.matmul(O[:, 0:257], ptm[pis[0]][:, sc * 128:(sc + 1) * 128],
                                                                          VMX[:, 0, h, :], start=True, stop=False),
                                  lambda O=O, sc=sc, h=h: nc.tensor.matmul(O[:, 0:257], ptm[pis[1]][:, sc * 128:(sc + 1) * 128],
                                                                          VMX[:, 1, h, :], start=False, stop=True)],
                           reads=[bptm[pis[0]], bptm[pis[1]], bvm], writes=[bO])
                for sc in range(4):
                    O, bO = Ob[sc]
                    fw.op("dve", lambda O=O, sc=sc: nc.vector.reciprocal(rd[:, sc:sc + 1], O[:, 256:257]), reads=[bO], writes=[brd])
                    fw.op("dve", lambda O=O, sc=sc, h=h: nc.vector.scalar_tensor_tensor(
                        ym[:, sc, h * 256:(h + 1) * 256], O[:, 0:256], rd[:, sc:sc + 1], zsm[:, sc, h * 256:(h + 1) * 256],
                        ALU.mult, ALU.mult), reads=[bO, brd, bzsm], writes=[bym])
            for sc in range(4):
                fw.ops("pe", [(lambda b=b, sc=sc: nc.tensor.transpose(P7[:, b * 128:(b + 1) * 128], ym[:, sc, b * 128:(b + 1) * 128],
                                                                     identb[:, :])) for b in range(8)],
                       reads=[bym, b_ident], writes=[bP7])
                fw.op("act", lambda: nc.scalar.copy(ymT[:, :], P7[:, :]), reads=[bP7], writes=[bymT])
                t0 = tt * 512 + sc * 128
                fw.dma("sp", YT[3072:4096, t0:t0 + 128].rearrange("(b p) t -> p b t", p=128),
                       ymT[:, :].rearrange("p (b t) -> p b t", b=8), reads=[bymT])
        fw.barrier()
    if stop_after <= 7:
        fw.finish("sp")
        return nc

    with ExitStack() as es:
        sb = lambda n, shp, dt: es.enter_context(nc.sbuf_tensor("o_" + n, shp, dt))
        lg = sb("lg", [128, D], F32); lb = sb("lb", [128, D], F32)
        yT = sb("yT", [128, 32, 512], BF16)
        xr = sb("xr", [128, 4, D], F32)
        sqt = sb("sqt", [128, D], BF16)
        wo = [sb("wo%d" % i, [128, 32, 256], BF16) for i in range(2)]
        st = sb("st", [128, 16], F32)
        blg = Buf(); byT = Buf(); bxr = Buf(); bsqt = Buf(); bwo = [Buf(), Buf()]; bst = Buf()
        fw.dma("sp", lg[:, :], ln_g.ap().partition_broadcast(128), writes=[blg])
        fw.dma("sp", lb[:, :], ln_b.ap().partition_broadcast(128), writes=[blg])
        wov = w_out.ap().rearrange("(kc p) n -> p kc n", p=128)
        ounits = []
        for tt in range(4):
            for blk in range(16):
                sl = (tt * 16 + blk) % 2
                for kh in range(2):
                    ounits.append(dict(k=16, w=256, nd=2, src=wov[:, kh * 16:(kh + 1) * 16, blk * 256:(blk + 1) * 256],
                                       dst=wo[sl][:, kh * 16:(kh + 1) * 16, :], dbuf=bwo[sl]))
        wl = WLoader(es, "o", ounits, ns=2, pf=1)
        npp = 0
        for tt in range(4):
            fw.dma("sp", yT[:, :, :], YT.ap().rearrange("(kc p) t -> p kc t", p=128)[:, :, tt * 512:(tt + 1) * 512], writes=[byT])
            for sc in range(4):
                r0 = HALO + tt * 512 + sc * 128
                fw.dma("sp", xr[:, sc, :], xh[r0:r0 + 128, :], writes=[bxr])
            for blk in range(16):
                s = (tt * 16 + blk) % 2
                u0 = (tt * 16 + blk) * 2
                wl.need(u0)
                wl.need(u0 + 1)
                pp, bpp = (PA, bPA) if npp % 2 == 0 else (PB, bPB)
                npp += 1
                fns = []
                for sc in range(4):
                    for kc in range(32):
                        fns.append(lambda sc=sc, kc=kc, s=s, pp=pp: nc.tensor.matmul(
                            pp[:, sc * 256:(sc + 1) * 256], yT[:, kc, sc * 128:(sc + 1) * 128], wo[s][:, kc, :],
                            start=(kc == 0), stop=(kc == 31)))
                fw.ops("pe", fns, reads=[byT, bwo[s]], writes=[bpp])
                xs_ = xr[:, :, blk * 256:(blk + 1) * 256]
                fw.op("dve", lambda xs_=xs_, pp=pp: nc.vector.scalar_tensor_tensor(
                    xs_, xs_, ALPHA, pp[:, :].rearrange("p (a b) -> p a b", a=4), ALU.mult, ALU.add),
                    reads=[bpp, bxr], writes=[bxr])
            fw.op("dve", lambda: nc.vector.reduce_sum(st[:, 0:4], xr[:, :, :], AX.X), reads=[bxr], writes=[bst])
            fw.op("dve", lambda: nc.vector.tensor_scalar_mul(st[:, 4:8], st[:, 0:4], -1.0 / D), reads=[bst], writes=[bst])
            for sc in range(4):
                fw.op("act", lambda sc=sc: nc.scalar.activation(xr[:, sc, :], xr[:, sc, :], AF.Identity,
                                                                bias=st[:, 4 + sc:5 + sc], scale=1.0),
                      reads=[bxr, bst], writes=[bxr])
                fw.op("dve", lambda sc=sc: nc.vector.tensor_tensor(sqt[:, :], xr[:, sc, :], xr[:, sc, :], ALU.mult),
                      reads=[bxr], writes=[bsqt])
                fw.op("dve", lambda sc=sc: nc.vector.reduce_sum(st[:, 8 + sc:9 + sc], sqt[:, :], AX.X), reads=[bsqt], writes=[bst])
            fw.op("dve", lambda: nc.vector.tensor_scalar(st[:, 8:12], st[:, 8:12], 1.0 / D, LN_EPS, ALU.mult, ALU.add),
                  reads=[bst], writes=[bst])
            fw.op("act", lambda: nc.scalar.sqrt(st[:, 12:16], st[:, 8:12]), reads=[bst], writes=[bst])
            fw.op("dve", lambda: nc.vector.reciprocal(st[:, 12:16], st[:, 12:16]), reads=[bst], writes=[bst])
            for sc in range(4):
                fw.op("dve", lambda sc=sc: nc.vector.scalar_tensor_tensor(xr[:, sc, :], xr[:, sc, :], st[:, 12 + sc:13 + sc],
                                                                         lg[:, :], ALU.mult, ALU.mult),
                      reads=[bxr, bst, blg], writes=[bxr])
                fw.op("pool", lambda sc=sc: nc.gpsimd.tensor_tensor(xr[:, sc, :], xr[:, sc, :], lb[:, :], ALU.add),
                      reads=[bxr, blg], writes=[bxr])
                r0 = tt * 512 + sc * 128
                fw.dma("sp", out[r0:r0 + 128, :], xr[:, sc, :], reads=[bxr])
        fw.barrier()

    fw.finish("sp")
    return nc


def _prep_inputs(inputs):
    x = np.asarray(inputs["x"], np.float32)[0]
    sh = _shared_consts()
    in_maps = []
    for r in range(NCORES):
        t0 = r * TOK
        rs = slice(r * (D // NCORES), (r + 1) * (D // NCORES)) if WSHARD else slice(None)
        xhh = np.zeros((NT, D), np.float32)
        lo = max(0, t0 - HALO)
        xhh[HALO - (t0 - lo):] = x[lo:t0 + TOK]
        m = {"xh": xhh,
             "w_in": np.asarray(inputs["w_in"], np.float32)[rs],
             "mem": np.asarray(inputs["mem"], np.float32)[0],
             "w_mem_kv": np.asarray(inputs["w_mem_kv"], np.float32)[rs],
             "w_out": np.asarray(inputs["w_out"], np.float32)[rs],
             "w_cmp_k1": np.asarray(inputs["w_cmp_k1"], np.float32),
             "w_cmp_k2": np.asarray(inputs["w_cmp_k2"], np.float32),
             "w_cmp_v1": np.asarray(inputs["w_cmp_v1"], np.float32),
             "w_cmp_v2": np.asarray(inputs["w_cmp_v2"], np.float32),
             "pe_cmp_k": np.asarray(inputs["pe_cmp_k"], np.float32),
             "pe_cmp_v": np.asarray(inputs["pe_cmp_v"], np.float32),
             "gm_ln_g": np.asarray(inputs["gm_ln_g"], np.float32).reshape(1, 1024),
             "gm_ln_b": np.asarray(inputs["gm_ln_b"], np.float32).reshape(1, 1024),
             "w_spatial": np.asarray(inputs["w_spatial"], np.float32),
             "b_spatial": np.asarray(inputs["b_spatial"], np.float32),
             "ln_g": np.asarray(inputs["ln_g"], np.float32).reshape(1, D),
             "ln_b": np.asarray(inputs["ln_b"], np.float32).reshape(1, D)}
        m.update(sh)
        m.update(_core_consts(r, None))
        in_maps.append(m)
    return in_maps


def kernel(**inputs):
    in_maps = _prep_inputs(inputs)
    nc = build()
    res = run_bass_kernel_spmd(nc, in_maps, core_ids=list(range(NCORES)))
    outs = [np.asarray(res.results[r]["out"], np.float32) for r in range(NCORES)]
    return np.concatenate(outs, axis=0).reshape(1, SEQ, D)
```

```python
import os
import numpy as np
import ml_dtypes
from contextlib import ExitStack
import concourse.bass as bass
import concourse.mybir as mybir
from concourse.bass_utils import run_bass_kernel_spmd

F32 = mybir.dt.float32
BF16 = mybir.dt.bfloat16
AF = mybir.ActivationFunctionType
ALU = mybir.AluOpType
AX = mybir.AxisListType

NCORES = 8
WSHARD = False
D = 4096
SEQ = 16384
TOK = SEQ // NCORES
HALO = 512
NT = TOK + HALO
INW = 12336
DH = 128
SCALE = DH ** -0.5
ISC = DH ** 0.5
BIG = 30000.0
LN_EPS = 1e-5
ALPHA = 2.0 ** 0.25
COLS = dict(q=(0, 2048), kc=(2048, 512), vc=(2560, 512), ks=(3072, 512), vs=(3584, 512),
            kw=(4096, 512), vw=(4608, 512), gate=(5120, 48), zn=(5168, 2048), u=(7216, 1024),
            vg=(8240, 1024), zg=(9264, 1024), qm=(10288, 1024), zm=(11312, 1024))
DMAX = [3, 10, 40, 200]
SLOPES = np.exp2(-8.0 * np.arange(1, 17, dtype=np.float64) / 16).astype(np.float32)


class Buf:
    __slots__ = ("name", "w", "rs")

    def __init__(self, name=""):
        self.name = name
        self.w = None
        self.rs = []


class Tok:
    __slots__ = ("sem", "val", "eng", "dma")

    def __init__(self, sem, val, eng, dma):
        self.sem, self.val, self.eng, self.dma = sem, val, eng, dma


class FW:
    def __init__(self, nc, n_dma_sems=64):
        self.nc = nc
        self.engs = {"pe": nc.tensor, "act": nc.scalar, "dve": nc.vector,
                     "pool": nc.gpsimd, "sp": nc.sync}
        self.esem = {e: nc.alloc_semaphore(name="s_" + e) for e in self.engs}
        self.ecnt = {e: 0 for e in self.engs}
        self.dsems = [nc.alloc_semaphore(name="d%d" % i) for i in range(n_dma_sems)]
        self.dcnt = [0] * n_dma_sems
        self.dlast = [None] * n_dma_sems
        self.dnext = 0
        self.known = {e: {} for e in self.engs}
        self.nwaits = 0
        self.nins = 0

    def _wait(self, e, tok):
        if tok is None:
            return
        k = self.known[e]
        sid = id(tok.sem)
        if k.get(sid, 0) >= tok.val:
            return
        self.engs[e].wait_ge(tok.sem, tok.val)
        self.nwaits += 1
        k[sid] = tok.val

    def _deps(self, e, reads, writes, is_dma):
        for b in reads:
            t = b.w
            if t is not None:
                self._wait(e, t)
        for b in writes:
            t = b.w
            if t is not None and (t.dma or is_dma or t.eng != e):
                self._wait(e, t)
            for r in b.rs:
                if r.dma or is_dma or r.eng != e:
                    self._wait(e, r)

    def _commit(self, tok, reads, writes):
        for b in reads:
            b.rs.append(tok)
        for b in writes:
            b.w = tok
            b.rs = []

    def op(self, e, fn, reads=(), writes=()):
        self._deps(e, reads, writes, False)
        ins = fn()
        self.nins += 1
        self.ecnt[e] += 1
        ins.then_inc(self.esem[e], 1)
        tok = Tok(self.esem[e], self.ecnt[e], e, False)
        self._commit(tok, reads, writes)
        return tok

    def ops(self, e, fns, reads=(), writes=()):
        self._deps(e, reads, writes, False)
        ins = None
        for fn in fns:
            ins = fn()
            self.nins += 1
        self.ecnt[e] += 1
        ins.then_inc(self.esem[e], 1)
        tok = Tok(self.esem[e], self.ecnt[e], e, False)
        self._commit(tok, reads, writes)
        return tok

    def dma(self, e, out, in_, reads=(), writes=(), **kw):
        i = self.dnext
        self.dnext = (self.dnext + 1) % len(self.dsems)
        self._wait(e, self.dlast[i])
        self._deps(e, reads, writes, True)
        ins = self.engs[e].dma_start(out=out, in_=in_, **kw)
        self.nins += 1
        self.dcnt[i] += 16
        ins.then_inc(self.dsems[i], 16)
        tok = Tok(self.dsems[i], self.dcnt[i], e, True)
        self.dlast[i] = tok
        self._commit(tok, reads, writes)
        return tok

    def allgather(self, in_ap, out_ap, reads=(), writes=()):
        e = "pool"
        i = self.dnext
        self.dnext = (self.dnext + 1) % len(self.dsems)
        self._wait(e, self.dlast[i])
        self._deps(e, reads, writes, True)
        ins = self.nc.gpsimd.collective_compute(
            "AllGather", ALU.bypass, replica_groups=[list(range(NCORES))],
            ins=[in_ap], outs=[out_ap])
        self.dcnt[i] += 1
        ins.then_inc(self.dsems[i], 1)
        tok = Tok(self.dsems[i], self.dcnt[i], e, True)
        self.dlast[i] = tok
        self._commit(tok, reads, writes)
        return tok

    def _all_toks(self):
        toks = []
        for e in self.engs:
            if self.ecnt[e] > 0:
                toks.append(Tok(self.esem[e], self.ecnt[e], e, False))
        for t in self.dlast:
            if t is not None:
                toks.append(t)
        return toks

    def barrier(self):
        toks = self._all_toks()
        for e in self.engs:
            for t in toks:
                if t.eng == e and not t.dma:
                    continue
                self._wait(e, t)

    def finish(self, e="sp"):
        for t in self._all_toks():
            self._wait(e, t)


def _bf(a):
    return np.asarray(a, np.float32).astype(ml_dtypes.bfloat16)


def _split3(a):
    a = np.asarray(a, np.float64)
    h = _bf(a).astype(np.float64)
    m = _bf(a - h).astype(np.float64)
    l = _bf(a - h - m)
    return _bf(h), _bf(m), l


def _shared_consts():
    c = {}
    c["ident"] = np.eye(128, dtype=np.float32)
    lt = np.zeros((71, 128, 128), np.float32)
    k = np.arange(128)
    for ch in range(128):
        jq = 2 * (ch % 32) + k // 64
        lt[jq, ch, k] = 1.0
        lt[64, ch, :] = 128.0 * ch
        lt[65, ch, :] = 128.0 * ch
        lt[66, ch, :] = k
        lt[67, ch, :] = k
        lt[68:71, ch, :] = 1.0
    c["ltab"] = _bf(lt.reshape(71, 128 * 128))
    dq = np.arange(128)[None, None, :]
    rk = np.arange(128)[:, None, None]
    bs = np.zeros((4, 128, 4, 128), np.float32)
    bw = np.zeros((4, 5, 128, 4, 128), np.float32)
    for g in range(4):
        sl = SLOPES[4 * g:4 * g + 4].astype(np.float64)[None, :, None]
        dist = (dq - rk) + np.zeros((128, 4, 128))
        b = np.where(dist >= 0, -sl * dist, -BIG) * ISC
        bs[g] = b
        for m in range(5):
            dist = 128.0 * m + (dq - rk) + np.zeros((128, 4, 128))
            b = np.where((dist >= 0) & (dist < 512), -sl * dist, -BIG) * ISC
            bw[g, m] = b
    c["bs"] = _bf(bs.reshape(4, 128, 512))
    c["bw"] = _bf(bw.reshape(4, 5, 128, 512))
    bc = np.zeros((16, 4, 128, 4, 128), np.float32)
    mp = np.arange(128)[:, None, None]
    for i in range(16):
        for g in range(4):
            sl = SLOPES[4 * g:4 * g + 4].astype(np.float64)[None, :, None]
            dist = 128.0 * i + dq - 16.0 * mp - 15.0 + np.zeros((128, 4, 128))
            bc[i, g] = np.where(dist >= 0, -sl * dist, -BIG) * ISC
    c["bc"] = _bf(bc.reshape(16, 4, 128, 512))
    mc = np.zeros((8, 128, 256), np.float32)
    for cc in range(8):
        for r in range(128):
            idx = 128 * cc + r
            for j in (idx // 4, idx // 4 - 1):
                if 0 <= j < 256 and 4 * j <= idx <= 4 * j + 4:
                    mc[cc, r, j] = 1.0
    c["mc"] = _bf(mc)
    tri = (np.arange(128)[:, None] <= np.arange(128)[None, :]).astype(np.float32)
    c["tri"] = tri
    return c


def _core_consts(r, shared_mc):
    c = {}
    t = 2048 * r + np.arange(2048)
    ra = np.zeros((16, 4, 7, 4, 4, 128), np.float32)
    rc = np.zeros((16, 4, 8, 4, 128), np.float32)
    rab = np.zeros((16, 4, 7, 4, 4, 128), ml_dtypes.bfloat16)
    rcb = np.zeros((16, 4, 8, 4, 128), ml_dtypes.bfloat16)
    for g in range(4):
        sl = SLOPES[4 * g:4 * g + 4].astype(np.float64) * ISC
        shi = _bf(sl).astype(np.float64)
        slo = _bf(sl - shi)
        for i in range(16):
            tq = t[128 * i:128 * (i + 1)].astype(np.float64)
            st = -(sl[:, None] * tq[None, :])
            a, b, cpart = _split3(st)
            stc = -(sl[:, None] * (tq[None, :] - 15.0))
            a2, b2, c2 = _split3(stc)
            for cp in range(4):
                rab[i, g, 0, cp] = _bf(shi)[:, None]
                rab[i, g, 1, cp] = slo[:, None]
                rab[i, g, 2, cp] = _bf(shi)[:, None]
                rab[i, g, 3, cp] = slo[:, None]
                rab[i, g, 4, cp] = a
                rab[i, g, 5, cp] = b
                rab[i, g, 6, cp] = cpart
            rcb[i, g, 0] = _bf(shi)[:, None]
            rcb[i, g, 1] = slo[:, None]
            rcb[i, g, 2] = _bf(shi)[:, None]
            rcb[i, g, 3] = slo[:, None]
            rcb[i, g, 4] = a2
            rcb[i, g, 5] = b2
            rcb[i, g, 6] = c2
            rcb[i, g, 7] = _bf(np.full((4, 128), -BIG * ISC))
    c["raug"] = rab.reshape(16, 4, 7, 4 * 512)
    c["rcmp"] = rcb.reshape(16, 4, 8, 512)
    lc = np.zeros((8, 8, 128), np.float32)
    for cc in range(8):
        lc[0, cc] = 2048.0 * cc
        lc[1, cc] = 2048.0 * cc
        lc[2, cc] = 16.0 * np.arange(128)
        lc[3, cc] = 16.0 * np.arange(128)
        lc[4:7, cc] = 1.0
        lc[7, cc] = 0.0 if cc < r else 1.0
    lc[7, 0, 0] = 1.0
    c["lcmp"] = _bf(lc.reshape(8, 8 * 128))
    ml = np.zeros((128, 256), np.float32)
    for m in range(128):
        idx = 128 * r + m
        for j in (idx // 4, idx // 4 - 1):
            if 0 <= j < 256 and 4 * j <= idx <= 4 * j + 4:
                ml[m, j] = 1.0
    c["mloc"] = _bf(ml)
    pen = np.zeros((128, 2), np.float32)
    if r == 0:
        pen[0, 0] = -BIG
        pen[:, 1] = -BIG
    c["pen"] = pen
    sb = np.zeros((2048, 256), np.float32)
    val = np.zeros((2048, 256), np.float32)
    j = np.arange(256)[None, :]
    cur = (t // 64)[:, None]
    sb[:] = 0.0
    forced = (j == 0) | (j == cur) | (j == cur - 1)
    sb = np.where(forced, 1.0e4 + j, sb)
    sb = np.where(j > cur, -1.0e9, sb).astype(np.float32)
    I = (t // 128)[:, None]
    val = (j < 2 * I).astype(np.float32)
    c["sbm"] = sb
    c["val"] = val
    return c


def build(stop_after=99, dbg=()):
    nc = bass.Bass("TRN2", target_bir_lowering=False)
    fw = FW(nc)

    def din(name, shape, dt=F32):
        return nc.dram_tensor(name, list(shape), dt, kind="ExternalInput")

    def dscr(name, shape, dt=BF16):
        if name in dbg:
            return nc.dram_tensor(name, list(shape), dt, kind="ExternalOutput")
        return nc.dram_tensor(name, list(shape), dt)

    xh = din("xh", [NT, D])
    mem = din("mem", [256, D])
    if WSHARD:
        w_in_s = din("w_in", [D // NCORES, INW]); w_mem_kv_s = din("w_mem_kv", [D // NCORES, 2048])
        w_out_s = din("w_out", [D // NCORES, D])
        w_in = nc.dram_tensor("w_in_full", [D, INW], F32)
        w_mem_kv = nc.dram_tensor("w_mem_kv_full", [D, 2048], F32)
        w_out = nc.dram_tensor("w_out_full", [D, D], F32)
        for (src, full, ncol) in [(w_in_s, w_in, INW), (w_mem_kv_s, w_mem_kv, 2048), (w_out_s, w_out, D)]:
            loc = nc.dram_tensor(full.name + "_loc", [D // NCORES, ncol], F32)
            bl = Buf()
            for r4 in range(4):
                fw.dma("sp", loc[r4 * 128:(r4 + 1) * 128, :], src[r4 * 128:(r4 + 1) * 128, :], writes=[Buf()])
            fw.barrier()
            fw.allgather(loc.ap(), full.ap())
        fw.barrier()
    else:
        w_in = din("w_in", [D, INW])
        w_mem_kv = din("w_mem_kv", [D, 2048])
        w_out = din("w_out", [D, D])
    w_k1 = din("w_cmp_k1", [4096, 128]); w_k2 = din("w_cmp_k2", [128, 128])
    w_v1 = din("w_cmp_v1", [4096, 128]); w_v2 = din("w_cmp_v2", [128, 128])
    pe_k = din("pe_cmp_k", [32, 128]); pe_v = din("pe_cmp_v", [32, 128])
    gm_g = din("gm_ln_g", [1, 1024]); gm_b = din("gm_ln_b", [1, 1024])
    w_sp = din("w_spatial", [4, 128, 128]); b_sp = din("b_spatial", [4, 128])
    ln_g = din("ln_g", [1, D]); ln_b = din("ln_b", [1, D])
    c_ident = din("ident", [128, 128])
    c_ltab = din("ltab", [71, 128 * 128], BF16)
    c_bs = din("bs", [4, 128, 512], BF16)
    c_bw = din("bw", [4, 5, 128, 512], BF16)
    c_bc = din("bc", [16, 4, 128, 512], BF16)
    c_mc = din("mc", [8, 128, 256], BF16)
    c_tri = din("tri", [128, 128])
    c_raug = din("raug", [16, 4, 7, 2048], BF16)
    c_rcmp = din("rcmp", [16, 4, 8, 512], BF16)
    c_lcmp = din("lcmp", [8, 1024], BF16)
    c_mloc = din("mloc", [128, 256], BF16)
    c_pen = din("pen", [128, 2])
    c_sbm = din("sbm", [2048, 256])
    c_val = din("val", [2048, 256])
    out = nc.dram_tensor("out", [TOK, D], F32, kind="ExternalOutput")

    QT = dscr("QT", [2048, TOK]); KCT = dscr("KCT", [512, NT]); VCT = dscr("VCT", [512, NT])
    KST = dscr("KST", [512, TOK]); KWT = dscr("KWT", [512, NT]); QMT = dscr("QMT", [1024, TOK])
    VS = dscr("VS", [TOK, 512]); VW = dscr("VW", [NT, 512]); GATE = dscr("GATE", [TOK, 48])
    ZN = dscr("ZN", [TOK, 2048]); U = dscr("U", [TOK, 1024]); VG = dscr("VG", [TOK, 1024])
    ZG = dscr("ZG", [TOK, 1024]); ZM = dscr("ZM", [TOK, 1024])
    KMT = dscr("KMT", [1024, 256]); VM = dscr("VM", [256, 1024])
    KCC = dscr("KCC", [512, 128]); VCC = dscr("VCC", [128, 512])
    KST_all = dscr("KST_all", [8 * 512, TOK]); VS_all = dscr("VS_all", [SEQ, 512])
    KCC_all = dscr("KCC_all", [8 * 512, 128]); VCC_all = dscr("VCC_all", [1024, 512])
    YT = dscr("YT", [D, TOK])

    PA = nc.alloc_psum_tensor("PA", [128, 1024], F32)
    PB = nc.alloc_psum_tensor("PB", [128, 1024], F32)
    P4 = nc.alloc_psum_tensor("P4", [128, 512], F32)
    P5 = nc.alloc_psum_tensor("P5", [128, 512], F32)
    P6 = nc.alloc_psum_tensor("P6", [128, 512], F32)
    P7 = nc.alloc_psum_tensor("P7", [128, 1024], BF16)
    bPA, bPB, bP4, bP5, bP6, bP7 = (Buf(n) for n in ["PA", "PB", "P4", "P5", "P6", "P7"])
    bPA0, bPA1, bPB0, bPB1 = Buf("PA0"), Buf("PA1"), Buf("PB0"), Buf("PB1")

    Sb = [(PA[:, 0:512], bPA0), (PA[:, 512:1024], bPA1), (P6[:, :], bP6)]
    Ob = [(PB[:, 0:512], bPB0), (PB[:, 512:1024], bPB1), (P4[:, :], bP4), (P5[:, :], bP5)]
    ident = nc.alloc_sbuf_tensor("identf", [128, 128], F32)
    identb = nc.alloc_sbuf_tensor("identb", [128, 128], BF16)
    b_ident = Buf("ident")
    fw.dma("sp", ident[:, :], c_ident[:, :], writes=[b_ident])
    fw.op("dve", lambda: nc.vector.tensor_copy(identb[:, :], ident[:, :]), reads=[b_ident], writes=[b_ident])

    cnt = {"ev": 0}

    def evac_eng():
        cnt["ev"] += 1
        return "act" if cnt["ev"] % 2 else "dve"

    def copy_on(e, o, i):
        if e == "act":
            return lambda: nc.scalar.copy(o, i)
        if e == "dve":
            return lambda: nc.vector.tensor_copy(o, i)
        return lambda: nc.gpsimd.tensor_copy(o, i)

    class WLoader:
        def __init__(self, es, tag, units, ns=3, pf=2):
            self.units = units
            self.ns, self.pf = ns, pf
            self.stage = [es.enter_context(nc.sbuf_tensor("wst%d%s" % (i, tag), [128, 4096], F32)) for i in range(ns)]
            self.bst = [Buf() for _ in range(ns)]
            self.loaded = 0
            self.done = set()

        def view(self, u):
            un = self.units[u]
            k, w = un["k"], un["w"]
            return self.stage[u % self.ns][:, 0:k * w].rearrange("p (k c) -> p k c", k=k)

        def need(self, u, eng="pool"):
            if u in self.done or u >= len(self.units):
                return
            self.done.add(u)
            while self.loaded < min(len(self.units), u + 1 + self.pf):
                v = self.loaded
                un = self.units[v]
                sv = self.view(v)
                nd = un["nd"]
                kk = un["k"] // nd
                for d in range(nd):
                    fw.dma("sp", sv[:, d * kk:(d + 1) * kk, :], un["src"][:, d * kk:(d + 1) * kk, :], writes=[self.bst[v % self.ns]])
                self.loaded += 1
            un = self.units[u]
            sv = self.view(u)
            fw.op(eng, copy_on(eng, un["dst"], sv), reads=[self.bst[u % self.ns]], writes=[un["dbuf"]])

    WT = 256

    def proj_phase(xsrc, wsrc, tiles, tag):
        with ExitStack() as es:
            xT = es.enter_context(nc.sbuf_tensor("xT" + tag, [128, 32, 1024], BF16))
            xs = [es.enter_context(nc.sbuf_tensor("xs%d%s" % (i, tag), [128, D], F32)) for i in range(1)]
            wb = [es.enter_context(nc.sbuf_tensor("wb%d%s" % (i, tag), [128, 32, 128], BF16)) for i in range(3)]
            wbw = [es.enter_context(nc.sbuf_tensor("wbw%d%s" % (i, tag), [128, 32, WT], BF16)) for i in range(2)]
            ost = [es.enter_context(nc.sbuf_tensor("ost%d%s" % (i, tag), [128, 1024], BF16)) for i in range(3)]
            b_xT = Buf("xT"); b_xs = [Buf()]; b_wb = [Buf(), Buf(), Buf()]; b_wbw = [Buf(), Buf()]
            b_ost = [Buf(), Buf(), Buf()]
            wv = wsrc.ap().rearrange("(kc p) n -> p kc n", p=128)
            jobs = []
            units = []
            nfm = ntm = 0
            for ti, (row0, ntok, blocks) in enumerate(tiles):
                for (mode, c0, w, dst) in blocks:
                    jb = dict(ti=ti, mode=mode, c0=c0, w=w, dst=dst, u0=len(units))
                    if mode == "FM":
                        sl = nfm % 3
                        nfm += 1
                        jb["slot"] = sl
                        units.append(dict(k=32, w=w, nd=4, src=wv[:, :, c0:c0 + w], dst=wb[sl][:, :, 0:w], dbuf=b_wb[sl]))
                    else:
                        sl = ntm % 2
                        ntm += 1
                        jb["slot"] = sl
                        for kh in range(2):
                            units.append(dict(k=16, w=w, nd=2, src=wv[:, kh * 16:(kh + 1) * 16, c0:c0 + w],
                                              dst=wbw[sl][:, kh * 16:(kh + 1) * 16, 0:w], dbuf=b_wbw[sl]))
                    jb["u1"] = len(units)
                    jb["idx"] = len(jobs)
                    jobs.append(jb)
            wl = WLoader(es, tag, units)
            cw = {"o": 0}
            nblk = 0
            cur_tile = -1
            for jb in jobs:
                row0, ntok, _ = tiles[jb["ti"]]
                nch = ntok // 128
                if jb["ti"] != cur_tile:
                    cur_tile = jb["ti"]
                    for ch in range(nch):
                        s = 0
                        fw.dma("sp", xs[s][:, :], xsrc[row0 + ch * 128: row0 + (ch + 1) * 128, :], writes=[b_xs[s]])
                        for kg in range(8):
                            P, bP = (P4, bP4) if kg % 2 == 0 else (P5, bP5)
                            fw.ops("pe", [(lambda j=j, kg=kg, P=P, s=s: nc.tensor.transpose(
                                P[:, j * 128:(j + 1) * 128], xs[s][:, (kg * 4 + j) * 128:(kg * 4 + j + 1) * 128], ident[:, :]))
                                for j in range(4)], reads=[b_xs[s], b_ident], writes=[bP])
                            e = evac_eng()
                            fw.op(e, copy_on(e, xT[:, kg * 4:(kg + 1) * 4, ch * 128:(ch + 1) * 128],
                                             P[:, :].rearrange("p (a b) -> p a b", a=4)),
                                  reads=[bP], writes=[b_xT])
                mode, c0, w, dst, sl = jb["mode"], jb["c0"], jb["w"], jb["dst"], jb["slot"]
                for u in range(jb["u0"], jb["u1"]):
                    wl.need(u, evac_eng())
                ji = jb["idx"]
                if ji + 1 < len(jobs):
                    for u in range(jobs[ji + 1]["u0"], jobs[ji + 1]["u1"]):
                        wl.need(u, evac_eng())
                if mode == "FM":
                    pp, bpp = (PA, bPA) if nblk % 2 == 0 else (PB, bPB)
                    nblk += 1
                    fns = []
                    nh = (ntok + 511) // 512
                    for kc in range(32):
                        for h in range(nh):
                            n0 = h * 512
                            n1 = min(ntok, n0 + 512)
                            fns.append(lambda kc=kc, n0=n0, n1=n1, sl=sl, pp=pp, w=w: nc.tensor.matmul(
                                pp[0:w, n0:n1], wb[sl][:, kc, 0:w], xT[:, kc, n0:n1], start=(kc == 0), stop=(kc == 31)))
                    fw.ops("pe", fns, reads=[b_wb[sl], b_xT], writes=[bpp])
                    so = cw["o"] % 3
                    cw["o"] += 1
                    e = evac_eng()
                    fw.op(e, copy_on(e, ost[so][0:w, 0:ntok], pp[0:w, 0:ntok]), reads=[bpp], writes=[b_ost[so]])
                    fw.dma("sp", dst, ost[so][0:w, 0:ntok], reads=[b_ost[so]])
                else:
                    G = min(4, nch)
                    for grp in range(nch // G):
                        pp, bpp = (PA, bPA) if nblk % 2 == 0 else (PB, bPB)
                        nblk += 1
                        fns = []
                        for j in range(G):
                            sbi = grp * G + j
                            for kc in range(32):
                                fns.append(lambda kc=kc, sbi=sbi, j=j, sl=sl, pp=pp, w=w: nc.tensor.matmul(
                                    pp[:, j * WT: j * WT + w], xT[:, kc, sbi * 128:(sbi + 1) * 128],
                                    wbw[sl][:, kc, 0:w], start=(kc == 0), stop=(kc == 31)))
                        fw.ops("pe", fns, reads=[b_wbw[sl], b_xT], writes=[bpp])
                        so = cw["o"] % 3
                        cw["o"] += 1
                        e = evac_eng()
                        fw.op(e, copy_on(e, ost[so][:, 0:G * w].rearrange("p (a b) -> p a b", a=G),
                                         pp[:, 0:G * WT].rearrange("p (a b) -> p a b", a=G)[:, :, 0:w]),
                              reads=[bpp], writes=[b_ost[so]])
                        fw.dma("sp", dst(grp, G), ost[so][:, 0:G * w].rearrange("p (a b) -> p a b", a=G), reads=[b_ost[so]])
            fw.barrier()

    FMT = dict(q=(QT, False), kc=(KCT, True), vc=(VCT, True), ks=(KST, False), kw=(KWT, True), qm=(QMT, False))
    TMT = dict(vs=(VS, False), vw=(VW, True), gate=(GATE, False), zn=(ZN, False), u=(U, False), vg=(VG, False),
               zg=(ZG, False), zm=(ZM, False))

    def in_blocks(names, row0, ntok):
        bl = []
        for nm in names:
            c0, wtot = COLS[nm]
            for b0 in range(0, wtot, 128):
                w = min(128, wtot - b0)
                if nm in FMT:
                    T, halo = FMT[nm]
                    tcol = row0 if halo else row0 - HALO
                    bl.append(("FM", c0 + b0, w, T[b0:b0 + w, tcol:tcol + ntok]))
                else:
                    if b0 % 256 != 0:
                        continue
                    w = min(256, wtot - b0)
                    T, halo = TMT[nm]
                    trow = row0 if halo else row0 - HALO
                    bl.append(("TM", c0 + b0, w,
                               (lambda grp, G, T=T, trow=trow, b0=b0, w=w:
                                T[trow + grp * G * 128:trow + (grp + 1) * G * 128, b0:b0 + w]
                                .rearrange("(a p) c -> p a c", p=128))))
        return bl

    order = ["ks", "vs", "kc", "vc", "kw", "vw", "q", "gate", "zn", "u", "vg", "zg", "qm", "zm"]
    tiles = [(0, 512, in_blocks(["kc", "vc", "kw", "vw"], 0, 512)),
             (512, 1024, in_blocks(order, 512, 1024)),
             (1536, 1024, in_blocks(order, 1536, 1024))]
    proj_phase(xh, w_in, tiles, "a")
    if stop_after <= 1:
        fw.finish("sp")
        return nc

    mblocks = []
    for b0 in range(0, 1024, 128):
        mblocks.append(("FM", b0, 128, KMT[b0:b0 + 128, 0:256]))
    for b0 in range(0, 1024, 256):
        mblocks.append(("TM", 1024 + b0, 256,
                        (lambda grp, G, b0=b0: VM[grp * G * 128:(grp + 1) * G * 128, b0:b0 + 256]
                         .rearrange("(a p) c -> p a c", p=128))))
    proj_phase(mem, w_mem_kv, [(0, 256, mblocks)], "m")
    if stop_after <= 2:
        fw.finish("sp")
        return nc

    with ExitStack() as es:
        sb = lambda n, shp, dt: es.enter_context(nc.sbuf_tensor(n, shp, dt))
        w1 = [sb("cw1%d" % i, [128, 32, 128], BF16) for i in range(2)]
        w2 = [sb("cw2%d" % i, [128, 128], BF16) for i in range(2)]
        peT = [sb("cpe%d" % i, [128, 32], BF16) for i in range(2)]
        AT = [sb("cAT%d" % i, [128, 4, 2064], BF16) for i in range(2)]
        cv = sb("ccv", [128, 2], F32)
        xhid = sb("cxh", [128, 512], F32)
        t1 = sb("ct1", [128, 512], F32)
        t2 = sb("ct2", [128, 512], F32)
        hT = sb("chT", [128, 512], BF16)
        cout = sb("cout", [128, 512], BF16)
        bw = Buf(); bA = Buf(); bcv = Buf(); bx = Buf(); bt1 = Buf(); bt2 = Buf(); bh = Buf(); bco = Buf()
        for kv, (wa, wb2, pe, SRC) in enumerate([(w_k1, w_k2, pe_k, KCT), (w_v1, w_v2, pe_v, VCT)]):
            fw.dma("pool", w1[kv][:, :, :], wa.ap().rearrange("(j p) o -> p j o", p=128), writes=[bw])
            fw.dma("pool", w2[kv][:, :], wb2[:, :], writes=[bw])
            fw.dma("pool", peT[kv][:, :], pe.ap().rearrange("j d -> d j"), writes=[bw], allow_slow_non_contiguous=True)
            fw.dma("sp", AT[kv][:, :, :], SRC.ap().rearrange("(g p) t -> p g t", p=128)[:, :, 496:2560], writes=[bA])
        for kv in range(2):
            fns = []
            for g in range(4):
                A4 = AT[kv][:, g, :].rearrange("p (m s) -> p m s", s=16)
                for j in range(32):
                    a, b = j // 16, j % 16
                    fns.append(lambda g=g, j=j, a=a, b=b, A4=A4: nc.tensor.matmul(
                        P4[:, g * 128:(g + 1) * 128], w1[kv][:, j, :], A4[:, a:a + 128, b],
                        start=(j == 0), stop=(j == 31)))
            fw.ops("pe", fns, reads=[bw, bA], writes=[bP4])
            fw.ops("pe", [(lambda j=j: nc.tensor.matmul(P6[:, 0:1], w1[kv][:, j, :], peT[kv][:, j:j + 1],
                                                        start=(j == 0), stop=(j == 31))) for j in range(32)],
                   reads=[bw], writes=[bP6])
            fw.op("dve", lambda: nc.vector.tensor_copy(cv[:, kv:kv + 1], P6[:, 0:1]), reads=[bP6], writes=[bcv])
            fw.op("act", lambda: nc.scalar.activation(xhid[:, :], P4[:, :], AF.Identity, bias=cv[:, kv:kv + 1], scale=1.0),
                  reads=[bP4, bcv], writes=[bx])
            fw.op("dve", lambda: nc.vector.tensor_tensor(t1[:, :], xhid[:, :], xhid[:, :], ALU.mult), reads=[bx], writes=[bt1])
            fw.op("dve", lambda: nc.vector.tensor_scalar(t2[:, :], t1[:, :], 0.044715, 1.0, ALU.mult, ALU.add),
                  reads=[bt1], writes=[bt2])
            fw.op("dve", lambda: nc.vector.tensor_tensor(t1[:, :], t2[:, :], xhid[:, :], ALU.mult), reads=[bt2, bx], writes=[bt1])
            fw.op("act", lambda: nc.scalar.activation(t2[:, :], t1[:, :], AF.Sigmoid, scale=1.5957691216057308),
                  reads=[bt1], writes=[bt2])
            fw.op("dve", lambda: nc.vector.tensor_tensor(hT[:, :], t2[:, :], xhid[:, :], ALU.mult), reads=[bt2, bx], writes=[bh])
            if kv == 0:
                fw.ops("pe", [lambda: nc.tensor.matmul(P5[:, :], w2[0][:, :], hT[:, :], start=True, stop=True)],
                       reads=[bw, bh], writes=[bP5])
                fw.op("dve", lambda: nc.vector.tensor_copy(cout[:, :], P5[:, :]), reads=[bP5], writes=[bco])
                fw.dma("sp", KCC.ap().rearrange("(g p) m -> p g m", p=128),
                       cout[:, :].rearrange("p (g m) -> p g m", g=4), reads=[bco])
            else:
                fw.ops("pe", [(lambda g=g: nc.tensor.matmul(P5[:, g * 128:(g + 1) * 128], hT[:, g * 128:(g + 1) * 128],
                                                            w2[1][:, :], start=True, stop=True)) for g in range(4)],
                       reads=[bw, bh], writes=[bP5])
                fw.op("dve", lambda: nc.vector.tensor_copy(cout[:, :], P5[:, :]), reads=[bP5], writes=[bco])
                fw.dma("sp", VCC[:, :], cout[:, :], reads=[bco])
        fw.barrier()
    if stop_after <= 3:
        fw.finish("sp")
        return nc

    for (a, b) in [(KCC, KCC_all), (VCC, VCC_all), (KST, KST_all), (VS, VS_all)]:
        tk = fw.allgather(a.ap(), b.ap())
        fw._wait("pool", tk)

    with ExitStack() as es:
        sb = lambda n, shp, dt: es.enter_context(nc.sbuf_tensor("g_" + n, shp, dt))
        gam = sb("gam", [128, 1024], F32); bet = sb("bet", [128, 1024], F32)
        wsn = sb("wsn", [128, 4, 128], F32); trit = sb("tri", [128, 128], F32)
        wsT = sb("wsT", [128, 512], F32); wsTb = sb("wsTb", [128, 512], BF16)
        bsp = sb("bsp", [128, 4], F32)
        vt = [sb("vt%d" % i, [128, 1024], BF16) for i in range(2)]
        ut = [sb("ut%d" % i, [128, 1024], BF16) for i in range(2)]
        zt = [sb("zt%d" % i, [128, 1024], BF16) for i in range(2)]
        cen = sb("cen", [128, 1024], F32); sq = sb("sq", [128, 1024], F32)
        st4 = sb("st4", [128, 4], F32)
        vnb = sb("vnb", [128, 1024], BF16); tm = sb("tm", [128, 1024], F32); zsg = sb("zsg", [128, 1024], F32)
        yg = sb("yg", [128, 1024], BF16); ygT = sb("ygT", [128, 1024], BF16)
        bc0 = Buf(); bws = Buf(); bvt = [Buf(), Buf()]; but = [Buf(), Buf()]; bzt = [Buf(), Buf()]
        bcen = Buf(); bsq = Buf(); bst = Buf(); bvn = Buf(); btm = Buf(); bzs = Buf(); byg = Buf(); bygT = Buf()
        fw.dma("sp", gam[:, :], gm_g.ap().partition_broadcast(128), writes=[bc0])
        fw.dma("sp", bet[:, :], gm_b.ap().partition_broadcast(128), writes=[bc0])
        fw.dma("sp", wsn[:, :, :], w_sp.ap().rearrange("g t s -> t g s"), writes=[bws])
        fw.dma("sp", trit[:, :], c_tri[:, :], writes=[bws])
        fw.dma("sp", bsp[:, :], b_sp.ap().rearrange("g t -> t g"), writes=[bc0], allow_slow_non_contiguous=True)
        fw.ops("pe", [(lambda g=g: nc.tensor.transpose(P6[:, g * 128:(g + 1) * 128], wsn[:, g, :], ident[:, :])) for g in range(4)],
               reads=[bws, b_ident], writes=[bP6])
        fw.op("dve", lambda: nc.vector.tensor_tensor(wsT[:, :].rearrange("p (g t) -> p g t", g=4),
                                                    P6[:, :].rearrange("p (g t) -> p g t", g=4),
                                                    trit[:, :].unsqueeze(1).to_broadcast([128, 4, 128]), ALU.mult),
              reads=[bP6, bws], writes=[bws])
        fw.op("dve", lambda: nc.vector.tensor_copy(wsTb[:, :], wsT[:, :]), reads=[bws], writes=[bws])
        for ch in range(16):
            s = ch % 2
            rows = slice(ch * 128, (ch + 1) * 128)
            fw.dma("sp", vt[s][:, :], VG[rows, :], writes=[bvt[s]])
            fw.dma("sp", ut[s][:, :], U[rows, :], writes=[but[s]])
            fw.dma("sp", zt[s][:, :], ZG[rows, :], writes=[bzt[s]])
            fw.op("dve", lambda s=s: nc.vector.reduce_sum(st4[:, 0:1], vt[s][:, :], AX.X), reads=[bvt[s]], writes=[bst])
            fw.op("dve", lambda: nc.vector.tensor_scalar_mul(st4[:, 1:2], st4[:, 0:1], -1.0 / 1024), reads=[bst], writes=[bst])
            fw.op("act", lambda s=s: nc.scalar.activation(cen[:, :], vt[s][:, :], AF.Identity, bias=st4[:, 1:2], scale=1.0),
                  reads=[bvt[s], bst], writes=[bcen])
            fw.op("dve", lambda: nc.vector.tensor_tensor(sq[:, :], cen[:, :], cen[:, :], ALU.mult), reads=[bcen], writes=[bsq])
            fw.op("dve", lambda: nc.vector.reduce_sum(st4[:, 2:3], sq[:, :], AX.X), reads=[bsq], writes=[bst])
            fw.op("dve", lambda: nc.vector.tensor_scalar(st4[:, 2:3], st4[:, 2:3], 1.0 / 1024, LN_EPS, ALU.mult, ALU.add),
                  reads=[bst], writes=[bst])
            fw.op("act", lambda: nc.scalar.sqrt(st4[:, 3:4], st4[:, 2:3]), reads=[bst], writes=[bst])
            fw.op("dve", lambda: nc.vector.reciprocal(st4[:, 3:4], st4[:, 3:4]), reads=[bst], writes=[bst])
            fw.op("dve", lambda: nc.vector.scalar_tensor_tensor(sq[:, :], cen[:, :], st4[:, 3:4], gam[:, :], ALU.mult, ALU.mult),
                  reads=[bcen, bst, bc0], writes=[bsq])
            fw.op("dve", lambda: nc.vector.tensor_tensor(vnb[:, :], sq[:, :], bet[:, :], ALU.add), reads=[bsq, bc0], writes=[bvn])
            fw.ops("pe", [(lambda g=g: nc.tensor.matmul(PA[:, g * 256:(g + 1) * 256], wsTb[:, g * 128:(g + 1) * 128],
                                                        vnb[:, g * 256:(g + 1) * 256], start=True, stop=True)) for g in range(4)],
                   reads=[bws, bvn], writes=[bPA])
            for g in range(4):
                fw.op("dve", lambda g=g, s=s: nc.vector.scalar_tensor_tensor(
                    tm[:, g * 256:(g + 1) * 256], PA[:, g * 256:(g + 1) * 256], bsp[:, g:g + 1],
                    ut[s][:, g * 256:(g + 1) * 256], ALU.add, ALU.mult), reads=[bPA, bc0, but[s]], writes=[btm])
            fw.op("act", lambda s=s: nc.scalar.activation(zsg[:, :], zt[s][:, :], AF.Silu), reads=[bzt[s]], writes=[bzs])
            fw.op("dve", lambda: nc.vector.tensor_tensor(yg[:, :], tm[:, :], zsg[:, :], ALU.mult), reads=[btm, bzs], writes=[byg])
            fw.ops("pe", [(lambda b=b: nc.tensor.transpose(P7[:, b * 128:(b + 1) * 128], yg[:, b * 128:(b + 1) * 128],
                                                          identb[:, :])) for b in range(8)],
                   reads=[byg, b_ident], writes=[bP7])
            fw.op("act", lambda: nc.scalar.copy(ygT[:, :], P7[:, :]), reads=[bP7], writes=[bygT])
            fw.dma("sp", YT[2048:3072, ch * 128:(ch + 1) * 128].rearrange("(b p) t -> p b t", p=128),
                   ygT[:, :].rearrange("p (b t) -> p b t", b=8), reads=[bygT])
        fw.barrier()
    if stop_after <= 6:
        fw.finish("sp")
        return nc

    with ExitStack() as es:
        sb = lambda n, shp, dt: es.enter_context(nc.sbuf_tensor("m_" + n, shp, dt))
        KM = sb("KM", [128, 8, 256], BF16); VMX = sb("VMX", [128, 2, 4, 257], BF16)
        QM = [sb("QM%d" % i, [128, 8, 512], BF16) for i in range(2)]
        zmt = [sb("zmt%d" % i, [128, 4, 1024], BF16) for i in range(2)]
        zsm = sb("zsm", [128, 4, 1024], F32)
        ym = sb("ym", [128, 4, 1024], BF16); ymT = sb("ymT", [128, 1024], BF16)
        ptm = [sb("ptm%d" % i, [128, 512], BF16) for i in range(4)]
        rd = sb("rd", [128, 4], F32)
        bkm = Buf(); bvm = Buf(); bqm = [Buf(), Buf()]; bzm = [Buf(), Buf()]; bzsm = Buf(); bym = Buf(); bymT = Buf()
        bptm = [Buf() for _ in range(4)]; brd = Buf()
        fw.dma("sp", KM[:, :, :], KMT.ap().rearrange("(c p) m -> p c m", p=128), writes=[bkm])
        fw.op("pool", lambda: nc.gpsimd.memset(VMX[:, :, :, 256:257], 1.0), writes=[bvm])
        for mc in range(2):
            fw.dma("sp", VMX[:, mc, :, 0:256], VM[mc * 128:(mc + 1) * 128, :].rearrange("k (h d) -> k h d", h=4), writes=[bvm])
        np_ = 0
        for tt in range(4):
            s = tt % 2
            fw.dma("sp", QM[s][:, :, :], QMT.ap().rearrange("(c p) t -> p c t", p=128)[:, :, tt * 512:(tt + 1) * 512], writes=[bqm[s]])
            fw.dma("sp", zmt[s][:, :, :], ZM[tt * 512:(tt + 1) * 512, :].rearrange("(s p) c -> p s c", p=128), writes=[bzm[s]])
            fw.op("act", lambda s=s: nc.scalar.activation(zsm[:, :, :], zmt[s][:, :, :], AF.Silu), reads=[bzm[s]], writes=[bzsm])
            for h in range(4):
                pis = []
                for mc in range(2):
                    S, bS = Sb[mc]
                    fw.ops("pe", [lambda S=S, mc=mc, h=h, s=s: nc.tensor.matmul(S, KM[:, h * 2, mc * 128:(mc + 1) * 128],
                                                                                QM[s][:, h * 2, :], start=True, stop=False),
                                  lambda S=S, mc=mc, h=h, s=s: nc.tensor.matmul(S, KM[:, h * 2 + 1, mc * 128:(mc + 1) * 128],
                                                                                QM[s][:, h * 2 + 1, :], start=False, stop=True)],
                           reads=[bkm, bqm[s]], writes=[bS])
                    pi = np_ % 4
                    np_ += 1
                    pis.append(pi)
                    fw.op("act", lambda S=S, pi=pi: nc.scalar.activation(ptm[pi][:, :], S, AF.Exp, scale=1.0 / 16.0),
                          reads=[bS], writes=[bptm[pi]])
                for sc in range(4):
                    O, bO = Ob[sc]
                    fw.ops("pe", [lambda O=O, sc=sc, h=h: nc.tensor.matmul(O[:, 0:257], ptm[pis[0]][:, sc * 128:(sc + 1) * 128],
                                                                          VMX[:, 0, h, :], start=True, stop=False),
                                  lambda O=O, sc=sc, h=h: nc.tensor.matmul(O[:, 0:257], ptm[pis[1]][:, sc * 128:(sc + 1) * 128],
                                                                          VMX[:, 1, h, :], start=False, stop=True)],
                           reads=[bptm[pis[0]], bptm[pis[1]], bvm], writes=[bO])
                for sc in range(4):
                    O, bO = Ob[sc]
                    fw.op("dve", lambda O=O, sc=sc: nc.vector.reciprocal(rd[:, sc:sc + 1], O[:, 256:257]), reads=[bO], writes=[brd])
                    fw.op("dve", lambda O=O, sc=sc, h=h: nc.vector.scalar_tensor_tensor(
                        ym[:, sc, h * 256:(h + 1) * 256], O[:, 0:256], rd[:, sc:sc + 1], zsm[:, sc, h * 256:(h + 1) * 256],
                        ALU.mult, ALU.mult), reads=[bO, brd, bzsm], writes=[bym])
            for sc in range(4):
                fw.ops("pe", [(lambda b=b, sc=sc: nc.tensor.transpose(P7[:, b * 128:(b + 1) * 128], ym[:, sc, b * 128:(b + 1) * 128],
                                                                     identb[:, :])) for b in range(8)],
                       reads=[bym, b_ident], writes=[bP7])
                fw.op("act", lambda: nc.scalar.copy(ymT[:, :], P7[:, :]), reads=[bP7], writes=[bymT])
                t0 = tt * 512 + sc * 128
                fw.dma("sp", YT[3072:4096, t0:t0 + 128].rearrange("(b p) t -> p b t", p=128),
                       ymT[:, :].rearrange("p (b t) -> p b t", b=8), reads=[bymT])
        fw.barrier()
    if stop_after <= 7:
        fw.finish("sp")
        return nc

    with ExitStack() as es:
        sb = lambda n, shp, dt: es.enter_context(nc.sbuf_tensor("a_" + n, shp, dt))
        KSTg = sb("KSTg", [128, 8, 2048], BF16); VSXg = sb("VSXg", [128, 128, 129], BF16)
        KSTl = sb("KSTl", [128, 2048], BF16); VSXl = sb("VSXl", [128, 16, 129], BF16)
        QTg = sb("QTg", [128, 4, 2048], BF16)
        KWTg = sb("KWTg", [128, 2560], BF16); VWXg = sb("VWXg", [128, 20, 129], BF16)
        KCCg = sb("KCCg", [128, 8, 128], BF16); KCCl = sb("KCCl", [128, 128], BF16)
        VCXg = sb("VCXg", [128, 9, 385], BF16)
        LTAB = sb("LTAB", [71, 128, 128], BF16); LCMP = sb("LCMP", [8, 8, 128], BF16)
        BSg = sb("BSg", [128, 512], BF16); BWg = sb("BWg", [128, 5, 512], BF16)
        BCt = [sb("BC%d" % i, [128, 512], BF16) for i in range(2)]
        raug = [sb("raug%d" % i, [71, 4, 512], BF16) for i in range(2)]
        rcmp = [sb("rcmp%d" % i, [8, 512], BF16) for i in range(2)]
        pen = sb("pen", [128, 2], F32)
        sbm = [sb("sbm%d" % i, [128, 256], F32) for i in range(2)]
        valt = [sb("val%d" % i, [128, 256], F32) for i in range(2)]
        gt = sb("gt", [128, 16, 48], BF16); gsig = sb("gsig", [128, 16, 48], F32)
        znt = [sb("znt%d" % i, [128, 512], BF16) for i in range(2)]
        zs = sb("zs", [128, 512], F32)
        pt = [sb("pt%d" % i, [128, 512], BF16) for i in range(4)]
        yacc = sb("yacc", [128, 512], F32); ybf = sb("ybf", [128, 512], BF16); yTs = sb("yTs", [128, 512], BF16)
        dn = sb("dn", [128, 4], F32); rdn = sb("rdn", [128, 4], F32); wts = sb("wts", [128, 4], F32)
        score = sb("score", [128, 256], F32); sc2 = sb("sc2", [128, 256], F32); sel = sb("sel", [128, 256], F32)
        m8a = sb("m8a", [128, 8], F32); m8b = sb("m8b", [128, 8], F32)
        nm = sb("nm", [128, 256], BF16)
        B = {n: Buf(n) for n in ["KSTg", "VSXg", "KSTl", "VSXl", "QTg", "KWTg", "VWXg", "KCCg", "KCCl", "VCXg", "LT",
                                 "BSg", "BWg", "pen", "gt", "gsig", "zs", "yacc", "ybf", "yTs", "dn", "rdn", "wts",
                                 "score", "sc2", "sel", "m8a", "m8b", "nm"]}
        bBC = [Buf(), Buf()]; braug = [Buf(), Buf()]; brcmp = [Buf(), Buf()]; bsbm = [Buf(), Buf()]
        bval = [Buf(), Buf()]; bzn = [Buf(), Buf()]; bpt = [Buf(), Buf(), Buf(), Buf()]

        fw.dma("sp", LTAB[:, :, :], c_ltab.ap().rearrange("r (c k) -> r c k", c=128), writes=[B["LT"]])
        fw.dma("sp", LCMP[:, :, :], c_lcmp.ap().rearrange("r (c k) -> r c k", c=8), writes=[B["LT"]])
        fw.dma("sp", pen[:, :], c_pen[:, :], writes=[B["pen"]])
        fw.op("pool", lambda: nc.gpsimd.memset(VSXg[:, :, 128:129], 1.0), writes=[B["VSXg"]])
        fw.op("pool", lambda: nc.gpsimd.memset(VSXl[:, :, 128:129], 1.0), writes=[B["VSXl"]])
        fw.op("pool", lambda: nc.gpsimd.memset(VWXg[:, :, 128:129], 1.0), writes=[B["VWXg"]])
        fw.op("pool", lambda: nc.gpsimd.memset(VCXg[:, :, 128:129], 1.0), writes=[B["VCXg"]])
        fw.dma("sp", VCXg[:, 0:8, 129:385], c_mc.ap().rearrange("c k j -> k c j"), writes=[B["VCXg"]])
        fw.dma("sp", VCXg[:, 8, 129:385], c_mloc[:, :], writes=[B["VCXg"]])
        fw.dma("sp", gt[:, :, :], GATE.ap().rearrange("(i p) c -> p i c", p=128), writes=[B["gt"]])
        fw.op("act", lambda: nc.scalar.activation(gsig[:, :, :], gt[:, :, :], AF.Sigmoid), reads=[B["gt"]], writes=[B["gsig"]])

        pcount = {"s": 0, "p": 0, "it": 0}

        def attn_stream(chunks, qap, qbufs):
            n = len(chunks)
            LOOK = 2
            pend = []
            for ci in range(n + LOOK):
                if ci < n:
                    ch = chunks[ci]
                    S, bS = Sb[pcount["s"] % 3]
                    pcount["s"] += 1
                    L, R, ab = ch["aug"]
                    fw.ops("pe", [lambda S=S, ch=ch: nc.tensor.matmul(S, ch["k"], qap, start=True, stop=False),
                                  lambda S=S, L=L, R=R: nc.tensor.matmul(S, L, R, start=False, stop=True)],
                           reads=list(ch["kb"]) + list(qbufs) + list(ab), writes=[bS])
                    pi = pcount["p"] % 4
                    pcount["p"] += 1
                    if ch["bias"] is None:
                        fw.op("act", lambda S=S, pi=pi: nc.scalar.activation(pt[pi][:, :], S, AF.Exp, scale=SCALE),
                              reads=[bS], writes=[bpt[pi]])
                    else:
                        fw.op("act", lambda S=S, pi=pi, ch=ch: nc.scalar.activation(pt[pi][:, :], S, AF.Exp,
                                                                                 bias=ch["bias"], scale=SCALE),
                              reads=[bS, B["pen"]], writes=[bpt[pi]])
                    pend.append((ci, pi, ch))
                if pend and (ci >= n or len(pend) > LOOK):
                    pci, ppi, pch = pend.pop(0)
                    nco = pch["ncol"]
                    fw.ops("pe", [(lambda h=h, ppi=ppi, pch=pch, nco=nco, pci=pci: nc.tensor.matmul(
                        Ob[h][0][:, 0:nco], pt[ppi][:, h * 128:(h + 1) * 128], pch["v"],
                        start=(pci == 0), stop=(pci == n - 1))) for h in range(4)],
                        reads=[bpt[ppi]] + list(pch["vb"]), writes=[Ob[h][1] for h in range(4)])
            assert not pend

        def branch_out(g, i, br, first):
            for h in range(4):
                fw.op("dve", lambda h=h: nc.vector.tensor_scalar_max(dn[:, h:h + 1], Ob[h][0][:, 128:129], 1e-30),
                      reads=[Ob[h][1]], writes=[B["dn"]])
            fw.op("dve", lambda: nc.vector.reciprocal(rdn[:, :], dn[:, :]), reads=[B["dn"]], writes=[B["rdn"]])
            gv = gsig[:, i, g * 12:(g + 1) * 12].rearrange("p (h b) -> p h b", b=3)[:, :, br]
            fw.op("dve", lambda: nc.vector.tensor_tensor(wts[:, :], rdn[:, :], gv, ALU.mult),
                  reads=[B["rdn"], B["gsig"]], writes=[B["wts"]])
            for h in range(4):
                if first:
                    fw.op("dve", lambda h=h: nc.vector.tensor_scalar_mul(yacc[:, h * 128:(h + 1) * 128],
                                                                        Ob[h][0][:, 0:128], wts[:, h:h + 1]),
                          reads=[Ob[h][1], B["wts"]], writes=[B["yacc"]])
                else:
                    fw.op("dve", lambda h=h: nc.vector.scalar_tensor_tensor(
                        yacc[:, h * 128:(h + 1) * 128], Ob[h][0][:, 0:128], wts[:, h:h + 1],
                        yacc[:, h * 128:(h + 1) * 128], ALU.mult, ALU.add),
                        reads=[Ob[h][1], B["wts"], B["yacc"]], writes=[B["yacc"]])

        for g in range(int(os.environ.get('K_NG', 4))):
            fw.dma("sp", KSTg[:, :, :], KST_all.ap().rearrange("(r g p) t -> p r g t", r=8, g=4, p=128)[:, :, g, :],
                   writes=[B["KSTg"]])
            vsv = VS_all.ap().rearrange("(c k) (g d) -> k c g d", k=128, g=4)
            for c8 in range(16):
                fw.dma("sp", VSXg[:, c8 * 8:(c8 + 1) * 8, 0:128], vsv[:, c8 * 8:(c8 + 1) * 8, g, :], writes=[B["VSXg"]])
            fw.dma("sp", KSTl[:, :], KST[g * 128:(g + 1) * 128, :], writes=[B["KSTl"]])
            vsl = VS.ap().rearrange("(c k) (g d) -> k c g d", k=128, g=4)
            for c8 in range(2):
                fw.dma("sp", VSXl[:, c8 * 8:(c8 + 1) * 8, 0:128], vsl[:, c8 * 8:(c8 + 1) * 8, g, :], writes=[B["VSXl"]])
            fw.dma("sp", QTg[:, :, :], QT.ap().rearrange("(h p) t -> p h t", p=128)[:, 4 * g:4 * g + 4, :], writes=[B["QTg"]])
            fw.dma("sp", KWTg[:, :], KWT[g * 128:(g + 1) * 128, :], writes=[B["KWTg"]])
            vwv = VW.ap().rearrange("(c k) (g d) -> k c g d", k=128, g=4)
            for c8 in range(0, 20, 10):
                fw.dma("sp", VWXg[:, c8:c8 + 10, 0:128], vwv[:, c8:c8 + 10, g, :], writes=[B["VWXg"]])
            fw.dma("sp", KCCg[:, :, :], KCC_all.ap().rearrange("(r g p) m -> p r g m", r=8, g=4, p=128)[:, :, g, :],
                   writes=[B["KCCg"]])
            fw.dma("sp", KCCl[:, :], KCC[g * 128:(g + 1) * 128, :], writes=[B["KCCl"]])
            fw.dma("sp", VCXg[:, 0:8, 0:128], VCC_all.ap().rearrange("(c k) (g d) -> k c g d", k=128, g=4)[:, :, g, :],
                   writes=[B["VCXg"]])
            fw.dma("sp", VCXg[:, 8, 0:128], VCC[:, g * 128:(g + 1) * 128], writes=[B["VCXg"]])
            fw.dma("sp", BSg[:, :], c_bs[g, :, :], writes=[B["BSg"]])
            fw.dma("sp", BWg[:, :, :], c_bw.ap()[g].rearrange("m k f -> k m f"), writes=[B["BWg"]])
            for i in range(int(os.environ.get('K_NI', 16))):
                it = pcount["it"]
                pcount["it"] += 1
                sl = it % 2
                fw.dma("sp", BCt[sl][:, :], c_bc[i, g, :, :], writes=[bBC[sl]])
                fw.dma("sp", rcmp[sl][:, :], c_rcmp[i, g, :, :], writes=[brcmp[sl]])
                fw.dma("sp", raug[sl][64:71, :, :], c_raug.ap()[i, g].rearrange("r (c f) -> r c f", c=4), writes=[braug[sl]])
                fw.dma("sp", sbm[sl][:, :], c_sbm[i * 128:(i + 1) * 128, :], writes=[bsbm[sl]])
                fw.dma("sp", valt[sl][:, :], c_val[i * 128:(i + 1) * 128, :], writes=[bval[sl]])
                fw.dma("sp", znt[sl][:, :], ZN[i * 128:(i + 1) * 128, g * 512:(g + 1) * 512], writes=[bzn[sl]])
                qap = QTg[:, :, i * 128:(i + 1) * 128]
                qb = [B["QTg"]]
                chunks = []
                for cc in range(8):
                    chunks.append(dict(k=KCCg[:, cc, :], kb=[B["KCCg"]],
                                       aug=(LCMP[0:8, cc, :], rcmp[sl][0:8, :], [B["LT"], brcmp[sl]]),
                                       bias=None, v=VCXg[:, cc, 0:385], vb=[B["VCXg"]], ncol=385))
                chunks.append(dict(k=KCCl[:, :], kb=[B["KCCl"]], aug=(identb[:, :], BCt[sl][:, :], [b_ident, bBC[sl]]),
                                   bias=pen[:, 0:1], v=VCXg[:, 8, 0:385], vb=[B["VCXg"]], ncol=385))
                attn_stream(chunks, qap, qb)
                branch_out(g, i, 0, True)
                fw.op("dve", lambda: nc.vector.scalar_tensor_tensor(score[:, :], Ob[0][0][:, 129:385], rdn[:, 0:1],
                                                                   sbm[sl][:, :], ALU.mult, ALU.add),
                      reads=[Ob[0][1], B["rdn"], bsbm[sl]], writes=[B["score"]])
                for h in range(1, 4):
                    fw.op("dve", lambda h=h: nc.vector.scalar_tensor_tensor(score[:, :], Ob[h][0][:, 129:385], rdn[:, h:h + 1],
                                                                           score[:, :], ALU.mult, ALU.add),
                          reads=[Ob[h][1], B["rdn"], B["score"]], writes=[B["score"]])
                fw.op("dve", lambda: nc.vector.max(m8a[:, :], score[:, :]), reads=[B["score"]], writes=[B["m8a"]])
                fw.op("dve", lambda: nc.vector.match_replace(sc2[:, :], m8a[:, :], score[:, :], -3.0e38),
                      reads=[B["score"], B["m8a"]], writes=[B["sc2"]])
                fw.op("dve", lambda: nc.vector.max(m8b[:, :], sc2[:, :]), reads=[B["sc2"]], writes=[B["m8b"]])
                fw.op("dve", lambda: nc.vector.tensor_scalar(sel[:, :], score[:, :], m8b[:, 7:8], None, ALU.is_ge),
                      reads=[B["score"], B["m8b"]], writes=[B["sel"]])
                fw.op("dve", lambda: nc.vector.tensor_tensor(sc2[:, :], sel[:, :], valt[sl][:, :], ALU.mult),
                      reads=[B["sel"], bval[sl]], writes=[B["sc2"]])
                fw.op("dve", lambda: nc.vector.tensor_scalar(nm[:, :], sc2[:, :], -1.0, BIG * ISC, ALU.add, ALU.mult),
                      reads=[B["sc2"]], writes=[B["nm"]])
                fw.ops("pe", [(lambda qq=qq: nc.tensor.transpose(P7[0:64, qq * 128:(qq + 1) * 128],
                                                                nm[:, qq * 64:(qq + 1) * 64], identb[:, :])) for qq in range(4)],
                       reads=[B["nm"], b_ident], writes=[bP7])
                for qq in range(4):
                    e = "act" if qq % 2 else "pool_skip"
                    src = P7[0:64, qq * 128:(qq + 1) * 128].unsqueeze(1).to_broadcast([64, 4, 128])
                    dst = raug[sl][0:64, qq, :].rearrange("p (h q) -> p h q", h=4)
                    fw.op("dve", lambda src=src, dst=dst: nc.vector.tensor_copy(dst, src), reads=[bP7], writes=[braug[sl]])
                chunks = []
                for c in range(128):
                    if not any(1 <= 16 * r_ + i - c <= DMAX[g] for r_ in range(NCORES)):
                        continue
                    chunks.append(dict(k=KSTg[:, c // 16, (c % 16) * 128:(c % 16 + 1) * 128], kb=[B["KSTg"]],
                                       aug=(LTAB[0:71, c, :], raug[sl][0:71, c // 32, :], [B["LT"], braug[sl]]),
                                       bias=None, v=VSXg[:, c, 0:129], vb=[B["VSXg"]], ncol=129))
                chunks.append(dict(k=KSTl[:, i * 128:(i + 1) * 128], kb=[B["KSTl"]],
                                   aug=(identb[:, :], BSg[:, :], [b_ident, B["BSg"]]),
                                   bias=None, v=VSXl[:, i, 0:129], vb=[B["VSXl"]], ncol=129))
                attn_stream(chunks, qap, qb)
                branch_out(g, i, 1, False)
                chunks = []
                for m in range(4, -1, -1):
                    ci = 4 + i - m
                    chunks.append(dict(k=KWTg[:, ci * 128:(ci + 1) * 128], kb=[B["KWTg"]],
                                       aug=(identb[:, :], BWg[:, m, :], [b_ident, B["BWg"]]),
                                       bias=(pen[:, 1:2] if ci < 4 else None), v=VWXg[:, ci, 0:129], vb=[B["VWXg"]], ncol=129))
                attn_stream(chunks, qap, qb)
                branch_out(g, i, 2, False)
                fw.op("act", lambda: nc.scalar.activation(zs[:, :], znt[sl][:, :], AF.Silu), reads=[bzn[sl]], writes=[B["zs"]])
                fw.op("dve", lambda: nc.vector.tensor_tensor(ybf[:, :], yacc[:, :], zs[:, :], ALU.mult),
                      reads=[B["yacc"], B["zs"]], writes=[B["ybf"]])
                fw.ops("pe", [(lambda h=h: nc.tensor.transpose(P7[:, h * 128:(h + 1) * 128], ybf[:, h * 128:(h + 1) * 128],
                                                              identb[:, :])) for h in range(4)],
                       reads=[B["ybf"], b_ident], writes=[bP7])
                fw.op("dve", lambda: nc.vector.tensor_copy(yTs[:, :], P7[:, 0:512]), reads=[bP7], writes=[B["yTs"]])
                fw.dma("sp", YT[g * 512:(g + 1) * 512, i * 128:(i + 1) * 128].rearrange("(h p) t -> p h t", p=128),
                       yTs[:, :].rearrange("p (h t) -> p h t", h=4), reads=[B["yTs"]])
        fw.barrier()
    if stop_after <= 5:
        fw.finish("sp")
        return nc

    with ExitStack() as es:
        sb = lambda n, shp, dt: es.enter_context(nc.sbuf_tensor("o_" + n, shp, dt))
        lg = sb("lg", [128, D], F32); lb = sb("lb", [128, D], F32)
        yT = sb("yT", [128, 32, 512], BF16)
        xr = sb("xr", [128, 4, D], F32)
        sqt = sb("sqt", [128, D], BF16)
        wo = [sb("wo%d" % i, [128, 32, 256], BF16) for i in range(2)]
        st = sb("st", [128, 16], F32)
        blg = Buf(); byT = Buf(); bxr = Buf(); bsqt = Buf(); bwo = [Buf(), Buf()]; bst = Buf()
        fw.dma("sp", lg[:, :], ln_g.ap().partition_broadcast(128), writes=[blg])
        fw.dma("sp", lb[:, :], ln_b.ap().partition_broadcast(128), writes=[blg])
        wov = w_out.ap().rearrange("(kc p) n -> p kc n", p=128)
        ounits = []
        for tt in range(4):
            for blk in range(16):
                sl = (tt * 16 + blk) % 2
                for kh in range(2):
                    ounits.append(dict(k=16, w=256, nd=2, src=wov[:, kh * 16:(kh + 1) * 16, blk * 256:(blk + 1) * 256],
                                       dst=wo[sl][:, kh * 16:(kh + 1) * 16, :], dbuf=bwo[sl]))
        wl = WLoader(es, "o", ounits, ns=2, pf=1)
        npp = 0
        for tt in range(4):
            fw.dma("sp", yT[:, :, :], YT.ap().rearrange("(kc p) t -> p kc t", p=128)[:, :, tt * 512:(tt + 1) * 512], writes=[byT])
            for sc in range(4):
                r0 = HALO + tt * 512 + sc * 128
                fw.dma("sp", xr[:, sc, :], xh[r0:r0 + 128, :], writes=[bxr])
            for blk in range(16):
                s = (tt * 16 + blk) % 2
                u0 = (tt * 16 + blk) * 2
                wl.need(u0, "act")
                wl.need(u0 + 1, "pool")
                wl.need(u0 + 2, "act")
                wl.need(u0 + 3, "pool")
                pp, bpp = (PA, bPA) if npp % 2 == 0 else (PB, bPB)
                npp += 1
                fns = []
                for sc in range(4):
                    for kc in range(32):
                        fns.append(lambda sc=sc, kc=kc, s=s, pp=pp: nc.tensor.matmul(
                            pp[:, sc * 256:(sc + 1) * 256], yT[:, kc, sc * 128:(sc + 1) * 128], wo[s][:, kc, :],
                            start=(kc == 0), stop=(kc == 31)))
                fw.ops("pe", fns, reads=[byT, bwo[s]], writes=[bpp])
                xs_ = xr[:, :, blk * 256:(blk + 1) * 256]
                fw.op("dve", lambda xs_=xs_, pp=pp: nc.vector.scalar_tensor_tensor(
                    xs_, xs_, ALPHA, pp[:, :].rearrange("p (a b) -> p a b", a=4), ALU.mult, ALU.add),
                    reads=[bpp, bxr], writes=[bxr])
            fw.op("dve", lambda: nc.vector.reduce_sum(st[:, 0:4], xr[:, :, :], AX.X), reads=[bxr], writes=[bst])
            fw.op("dve", lambda: nc.vector.tensor_scalar_mul(st[:, 4:8], st[:, 0:4], -1.0 / D), reads=[bst], writes=[bst])
            for sc in range(4):
                fw.op("act", lambda sc=sc: nc.scalar.activation(xr[:, sc, :], xr[:, sc, :], AF.Identity,
                                                                bias=st[:, 4 + sc:5 + sc], scale=1.0),
                      reads=[bxr, bst], writes=[bxr])
                fw.op("dve", lambda sc=sc: nc.vector.tensor_tensor(sqt[:, :], xr[:, sc, :], xr[:, sc, :], ALU.mult),
                      reads=[bxr], writes=[bsqt])
                fw.op("dve", lambda sc=sc: nc.vector.reduce_sum(st[:, 8 + sc:9 + sc], sqt[:, :], AX.X), reads=[bsqt], writes=[bst])
            fw.op("dve", lambda: nc.vector.tensor_scalar(st[:, 8:12], st[:, 8:12], 1.0 / D, LN_EPS, ALU.mult, ALU.add),
                  reads=[bst], writes=[bst])
            fw.op("act", lambda: nc.scalar.sqrt(st[:, 12:16], st[:, 8:12]), reads=[bst], writes=[bst])
            fw.op("dve", lambda: nc.vector.reciprocal(st[:, 12:16], st[:, 12:16]), reads=[bst], writes=[bst])
            for sc in range(4):
                fw.op("dve", lambda sc=sc: nc.vector.scalar_tensor_tensor(xr[:, sc, :], xr[:, sc, :], st[:, 12 + sc:13 + sc],
                                                                         lg[:, :], ALU.mult, ALU.mult),
                      reads=[bxr, bst, blg], writes=[bxr])
                fw.op("pool", lambda sc=sc: nc.gpsimd.tensor_tensor(xr[:, sc, :], xr[:, sc, :], lb[:, :], ALU.add),
                      reads=[bxr, blg], writes=[bxr])
                r0 = tt * 512 + sc * 128
                fw.dma("sp", out[r0:r0 + 128, :], xr[:, sc, :], reads=[bxr])
        fw.barrier()

    fw.finish("sp")
    return nc


def _prep_inputs(inputs):
    x = np.asarray(inputs["x"], np.float32)[0]
    sh = _shared_consts()
    in_maps = []
    for r in range(NCORES):
        t0 = r * TOK
        rs = slice(r * (D // NCORES), (r + 1) * (D // NCORES)) if WSHARD else slice(None)
        xhh = np.zeros((NT, D), np.float32)
        lo = max(0, t0 - HALO)
        xhh[HALO - (t0 - lo):] = x[lo:t0 + TOK]
        m = {"xh": xhh,
             "w_in": np.asarray(inputs["w_in"], np.float32)[rs],
             "mem": np.asarray(inputs["mem"], np.float32)[0],
             "w_mem_kv": np.asarray(inputs["w_mem_kv"], np.float32)[rs],
             "w_out": np.asarray(inputs["w_out"], np.float32)[rs],
             "w_cmp_k1": np.asarray(inputs["w_cmp_k1"], np.float32),
             "w_cmp_k2": np.asarray(inputs["w_cmp_k2"], np.float32),
             "w_cmp_v1": np.asarray(inputs["w_cmp_v1"], np.float32),
             "w_cmp_v2": np.asarray(inputs["w_cmp_v2"], np.float32),
             "pe_cmp_k": np.asarray(inputs["pe_cmp_k"], np.float32),
             "pe_cmp_v": np.asarray(inputs["pe_cmp_v"], np.float32),
             "gm_ln_g": np.asarray(inputs["gm_ln_g"], np.float32).reshape(1, 1024),
             "gm_ln_b": np.asarray(inputs["gm_ln_b"], np.float32).reshape(1, 1024),
             "w_spatial": np.asarray(inputs["w_spatial"], np.float32),
             "b_spatial": np.asarray(inputs["b_spatial"], np.float32),
             "ln_g": np.asarray(inputs["ln_g"], np.float32).reshape(1, D),
             "ln_b": np.asarray(inputs["ln_b"], np.float32).reshape(1, D)}
        m.update(sh)
        m.update(_core_consts(r, None))
        in_maps.append(m)
    return in_maps


def kernel(**inputs):
    in_maps = _prep_inputs(inputs)
    nc = build()
    res = run_bass_kernel_spmd(nc, in_maps, core_ids=list(range(NCORES)))
    outs = [np.asarray(res.results[r]["out"], np.float32) for r in range(NCORES)]
    return np.concatenate(outs, axis=0).reshape(1, SEQ, D)
```
